# Optimizing a Trainium2 kernel written in Bass

```python
import math
import jax, jax.numpy as jnp
from jax import lax
import numpy as np

D_MODEL = 2048
BATCH = 4
SEQ = 2048
DEPTH = 1

HY_WIDTH = D_MODEL // 2
HY_ORDER = 2
HY_SHORT_CONV = 3
HY_EMB_DIM = 33
HY_FILTER_HIDDEN = 64
HY_FAST_DECAY_PCT = 0.3
HY_SLOW_DECAY_PCT = 1.5
HY_DECAY_TARGET = 1e-2
HY_MIN_DECAY = math.log(HY_DECAY_TARGET) / HY_SLOW_DECAY_PCT
HY_MAX_DECAY = math.log(HY_DECAY_TARGET) / HY_FAST_DECAY_PCT
CF_WIDTH = D_MODEL // 2
CF_KERNEL = 31
FFN_HIDDEN = -(-8 * D_MODEL // (3 * 256)) * 256
N_IN = 3 * HY_WIDTH + 2 * CF_WIDTH + 2 * D_MODEL
EPS = 1e-6

kernel_name = "hyena_conformer_gated_hybrid_block"


def rms_norm(x, g):
    xf = x.astype(jnp.float32)
    y = xf * lax.rsqrt(jnp.mean(xf * xf, axis=-1, keepdims=True) + EPS)
    return (y * g.astype(jnp.float32)).astype(x.dtype)


def layer_norm(x, g, b):
    xf = x.astype(jnp.float32)
    mu = jnp.mean(xf, axis=-1, keepdims=True)
    var = jnp.mean(jnp.square(xf - mu), axis=-1, keepdims=True)
    y = (xf - mu) * lax.rsqrt(var + EPS)
    return (y * g.astype(jnp.float32) + b.astype(jnp.float32)).astype(x.dtype)


def depthwise_conv(u, w, b):
    pad = w.shape[0] // 2
    y = lax.conv_general_dilated(
        u, w[:, None, :].astype(u.dtype), window_strides=(1,), padding=[(pad, pad)],
        dimension_numbers=("NWC", "WIO", "NWC"), feature_group_count=u.shape[-1])
    return y + b.astype(u.dtype)


def hyena_filters_freq(L, w1, b1, fr1, w2, b2, fr2, w3):
    f32 = jnp.float32
    t = jnp.linspace(0.0, 1.0, L, dtype=f32)[:, None]
    bands = (HY_EMB_DIM - 1) // 2
    w = 2.0 * math.pi * jnp.arange(L, dtype=f32)[:, None] / L
    f = jnp.linspace(1e-4, bands - 1, bands, dtype=f32)[None, :]
    z = jnp.concatenate([t, jnp.cos(f * w), -jnp.sin(f * w)], axis=-1)
    h = jnp.sin(fr1.astype(f32) * (z @ w1.astype(f32) + b1.astype(f32)))
    h = jnp.sin(fr2.astype(f32) * (h @ w2.astype(f32) + b2.astype(f32)))
    k = (h @ w3.astype(f32)).reshape(L, HY_ORDER, 2, HY_WIDTH)
    deltas = jnp.linspace(HY_MIN_DECAY, HY_MAX_DECAY, HY_WIDTH, dtype=f32)
    k = k * jnp.exp(-t * jnp.abs(deltas))[:, None, None, :]
    k_fwd = k[:, :, 0]
    k_bwd = k[1:, :, 1]
    l1 = jnp.sum(jnp.abs(k_fwd), axis=0) + jnp.sum(jnp.abs(k_bwd), axis=0)
    two_sided = jnp.concatenate(
        [k_fwd, jnp.zeros((1, HY_ORDER, HY_WIDTH), f32), k_bwd[::-1]], axis=0) / l1
    return jnp.fft.rfft(two_sided, axis=0)


def fft_long_conv(u, kf, bias):
    L = u.shape[1]
    uf32 = u.astype(jnp.float32)
    uf = jnp.fft.rfft(uf32, n=2 * L, axis=1)
    y = jnp.fft.irfft(uf * kf[None], n=2 * L, axis=1)[:, :L]
    return (y + uf32 * bias.astype(jnp.float32)).astype(u.dtype)


def setup_inputs(seed: int = 0) -> dict:
    key = jax.random.key(seed)
    ks = jax.random.split(key, 26)
    f32 = jnp.float32

    def nrm(k, shape, scale):
        return jax.random.normal(k, (DEPTH,) + shape, f32) * scale

    def gain(k, shape):
        return 1.0 + nrm(k, shape, 0.01)

    return {
        "x": jax.random.normal(ks[0], (BATCH, SEQ, D_MODEL), f32),
        "mix_pre_g": gain(ks[1], (D_MODEL,)),
        "w_in": nrm(ks[2], (D_MODEL, N_IN), D_MODEL ** -0.5),
        "hy_conv_w": nrm(ks[3], (HY_SHORT_CONV, 3 * HY_WIDTH), HY_SHORT_CONV ** -0.5),
        "hy_conv_b": nrm(ks[4], (3 * HY_WIDTH,), 0.01),
        "hy_filt_w1": nrm(ks[5], (HY_EMB_DIM, HY_FILTER_HIDDEN), HY_EMB_DIM ** -0.5),
        "hy_filt_b1": nrm(ks[6], (HY_FILTER_HIDDEN,), 0.01),
        "hy_filt_fr1": gain(ks[7], (HY_FILTER_HIDDEN,)),
        "hy_filt_w2": nrm(ks[8], (HY_FILTER_HIDDEN, HY_FILTER_HIDDEN), HY_FILTER_HIDDEN ** -0.5),
        "hy_filt_b2": nrm(ks[9], (HY_FILTER_HIDDEN,), 0.01),
        "hy_filt_fr2": gain(ks[10], (HY_FILTER_HIDDEN,)),
        "hy_filt_w3": nrm(ks[11], (HY_FILTER_HIDDEN, HY_ORDER * 2 * HY_WIDTH), HY_FILTER_HIDDEN ** -0.5),
        "hy_bias": nrm(ks[12], (HY_ORDER, HY_WIDTH), 1.0),
        "hy_proj": nrm(ks[13], (HY_WIDTH, D_MODEL), HY_WIDTH ** -0.5),
        "cf_dw_w": nrm(ks[14], (CF_KERNEL, CF_WIDTH), CF_KERNEL ** -0.5),
        "cf_dw_b": nrm(ks[15], (CF_WIDTH,), 0.01),
        "cf_ln_g": gain(ks[16], (CF_WIDTH,)),
        "cf_ln_b": nrm(ks[17], (CF_WIDTH,), 0.01),
        "cf_proj": nrm(ks[18], (CF_WIDTH, D_MODEL), CF_WIDTH ** -0.5),
        "w_out": nrm(ks[19], (D_MODEL, D_MODEL), D_MODEL ** -0.5),
        "mix_post_g": gain(ks[20], (D_MODEL,)),
        "ffn_pre_g": gain(ks[21], (D_MODEL,)),
        "ffn_w_gu": nrm(ks[22], (D_MODEL, 2 * FFN_HIDDEN), D_MODEL ** -0.5),
        "ffn_w_down": nrm(ks[23], (FFN_HIDDEN, D_MODEL), FFN_HIDDEN ** -0.5),
        "ffn_post_g": gain(ks[24], (D_MODEL,)),
    }


def reference(x, mix_pre_g, w_in, hy_conv_w, hy_conv_b, hy_filt_w1, hy_filt_b1, hy_filt_fr1,
              hy_filt_w2, hy_filt_b2, hy_filt_fr2, hy_filt_w3, hy_bias, hy_proj,
              cf_dw_w, cf_dw_b, cf_ln_g, cf_ln_b, cf_proj, w_out, mix_post_g,
              ffn_pre_g, ffn_w_gu, ffn_w_down, ffn_post_g):
    L = x.shape[1]
    s1 = 3 * HY_WIDTH
    s2 = s1 + CF_WIDTH
    s3 = s2 + CF_WIDTH
    s4 = s3 + D_MODEL
    for l in range(DEPTH):
        h = rms_norm(x, mix_pre_g[l])
        proj = h @ w_in[l]
        hy_in, cf_a, cf_b = proj[..., :s1], proj[..., s1:s2], proj[..., s2:s3]
        gate_a, gate_b = proj[..., s3:s4], proj[..., s4:]

        hy_in = depthwise_conv(hy_in, hy_conv_w[l], hy_conv_b[l])
        v, x1, x2 = jnp.split(hy_in, 3, axis=-1)
        kf = hyena_filters_freq(L, hy_filt_w1[l], hy_filt_b1[l], hy_filt_fr1[l],
                                hy_filt_w2[l], hy_filt_b2[l], hy_filt_fr2[l], hy_filt_w3[l])
        z = x1 * fft_long_conv(v, kf[:, 0], hy_bias[l, 0])
        y_a = x2 * fft_long_conv(z, kf[:, 1], hy_bias[l, 1])

        u = cf_a * jax.nn.sigmoid(cf_b)
        u = depthwise_conv(u, cf_dw_w[l], cf_dw_b[l])
        y_b = jax.nn.silu(layer_norm(u, cf_ln_g[l], cf_ln_b[l]))

        merged = (jax.nn.sigmoid(gate_a) * (y_a @ hy_proj[l])
                  + jax.nn.sigmoid(gate_b) * (y_b @ cf_proj[l]))
        x = x + rms_norm(merged @ w_out[l], mix_post_g[l])

        h = rms_norm(x, ffn_pre_g[l])
        gu = h @ ffn_w_gu[l]
        gate, up = gu[..., :FFN_HIDDEN], gu[..., FFN_HIDDEN:]
        x = x + rms_norm((jax.nn.silu(gate) * up) @ ffn_w_down[l], ffn_post_g[l])
    return x
```

```python
import math
import numpy as np
import ml_dtypes
import concourse.bass as bass
import concourse.mybir as mybir
from concourse.bass_utils import run_bass_kernel_spmd

F32 = mybir.dt.float32
BF16 = mybir.dt.bfloat16
AF = mybir.ActivationFunctionType
ALU = mybir.AluOpType
AX = mybir.AxisListType

D = 2048
L = 2048
T1 = 1024
HWID = 1024
NIN = 9216
FF = 5632
EPS = 1e-6
NPV = 480
HY_MIN_DECAY = math.log(1e-2) / 1.5
HY_MAX_DECAY = math.log(1e-2) / 0.3


class Buf:
    __slots__ = ("name", "last_w", "readers")

    def __init__(self, name=""):
        self.name = name
        self.last_w = None
        self.readers = []


class Op:
    __slots__ = ("eng", "fn", "deps", "signal", "sig", "is_dma", "sem", "semval", "idx", "key")


class Prog:
    ENG_BLOCK = {"pe": "tensor", "act": "scalar", "dve": "vector", "pool": "gpsimd", "sp": "sync"}

    def __init__(self, nc):
        self.nc = nc
        self.ops = {k: [] for k in self.ENG_BLOCK}
        self.prog_sem = {k: nc.alloc_semaphore(name="prog_" + k) for k in self.ENG_BLOCK}
        self.dma_sems = {}
        self.nops = 0
        self.last_compute = {}
        self.phase_dmas = {}

    def _deps(self, o, reads, writes):
        deps = []

        def add(d):
            if d is None or d is o:
                return
            if (not d.is_dma) and (not o.is_dma) and d.eng == "pe" and o.eng == "pe":
                return
            for x in deps:
                if x is d:
                    return
            deps.append(d)

        for b in reads:
            add(b.last_w)
        for b in writes:
            add(b.last_w)
            for r in b.readers:
                add(r)
        return deps

    def _commit(self, o, reads, writes):
        for b in reads:
            if not o.is_dma:
                b.readers = [r for r in b.readers if r.is_dma or r.eng != o.eng]
            b.readers.append(o)
        for b in writes:
            b.last_w = o
            b.readers = []
        for d in o.deps:
            d.signal = True
        self.ops[o.eng].append(o)

    def op(self, eng, fn, reads=(), writes=()):
        o = Op()
        o.eng, o.fn, o.signal, o.sig, o.is_dma, o.sem, o.semval = eng, fn, False, 0, False, None, 0
        o.idx = self.nops
        self.nops += 1
        o.deps = self._deps(o, reads, writes)
        self._commit(o, reads, writes)
        if fn is not None:
            self.last_compute[eng] = o
        return o

    def dma(self, queue, fn, reads=(), writes=(), key=None, n=1):
        o = Op()
        o.eng, o.fn, o.signal, o.sig, o.is_dma = queue, fn, False, 0, True
        o.idx = self.nops
        self.nops += 1
        o.deps = self._deps(o, reads, writes)
        if key is None:
            key = writes[0].name
        if key not in self.dma_sems:
            self.dma_sems[key] = [self.nc.alloc_semaphore(name="d_" + key), 0]
        ent = self.dma_sems[key]
        ent[1] += 16 * n
        o.sem, o.semval, o.key = ent[0], ent[1], key
        self._commit(o, reads, writes)
        self.phase_dmas[key] = o
        return o

    def barrier(self):
        lasts = dict(self.last_compute)
        dmas = list(self.phase_dmas.values())
        for e in self.ENG_BLOCK:
            o = Op()
            o.eng, o.fn, o.signal, o.sig, o.is_dma, o.sem, o.semval = e, None, False, 0, False, None, 0
            o.idx = self.nops
            self.nops += 1
            o.deps = [v for k, v in lasts.items() if k != e] + dmas
            for d in o.deps:
                d.signal = True
            self.ops[e].append(o)
        self.phase_dmas = {}

    def emit(self):
        nc = self.nc
        for e, lst in self.ops.items():
            c = 0
            for o in lst:
                if o.is_dma:
                    continue
                if o.signal:
                    c += 1
                    o.sig = c
        with nc.Block() as block:
            for ename, bname in self.ENG_BLOCK.items():
                if not self.ops[ename]:
                    continue
                deco = getattr(block, bname)

                def body(eng, ename=ename):
                    waited = {}
                    mysem = self.prog_sem[ename]
                    for o in self.ops[ename]:
                        for d in o.deps:
                            if d.is_dma:
                                sem, val = d.sem, d.semval
                            else:
                                sem, val = self.prog_sem[d.eng], d.sig
                            k = id(sem)
                            if waited.get(k, 0) < val:
                                eng.wait_ge(sem, val)
                                waited[k] = val
                        if o.fn is None:
                            continue
                        r = o.fn(eng)
                        if o.is_dma:
                            if not isinstance(r, (list, tuple)):
                                r = [r]
                            for ins in r:
                                ins.then_inc(o.sem, 16)
                        elif o.signal:
                            r.then_inc(mysem, 1)

                deco(body)


_CONST = {}


def _lhsT_layout(M):
    A = M.reshape(16, 128, 16, 128)
    return np.ascontiguousarray(A.transpose(2, 1, 0, 3))


def consts():
    if _CONST:
        return _CONST
    bf = ml_dtypes.bfloat16
    n = np.arange(2048, dtype=np.float64)
    phi = math.pi / 4096.0
    th = phi * np.outer(n, 2 * n + 1)
    _CONST["tabF_c"] = _lhsT_layout(np.cos(th)).astype(bf)
    _CONST["tabF_s"] = _lhsT_layout(-np.sin(th)).astype(bf)
    ps = (phi / 2) * np.outer(2 * n + 1, 2 * n + 1)
    Ct = np.cos(ps)
    St = -np.sin(ps)
    _CONST["tabD_c"] = _lhsT_layout(Ct).astype(bf)
    _CONST["tabD_s"] = _lhsT_layout(St).astype(bf)
    _CONST["tabR"] = np.ascontiguousarray(np.stack([Ct, St], 0)).astype(bf)
    _CONST["ident_bf"] = np.eye(128).astype(bf)
    _CONST["ident_f"] = np.eye(128).astype(np.float32)
    _CONST["ones_bf"] = np.ones((128, 128)).astype(bf)
    _CONST["tau_row"] = n.astype(np.float32)[None, :]
    deltas = np.linspace(HY_MIN_DECAY, HY_MAX_DECAY, HWID, dtype=np.float32)
    _CONST["adel_row"] = np.abs(deltas)[None, :].astype(np.float32)
    f32 = np.float32
    t = np.linspace(0.0, 1.0, L, dtype=f32)[:, None]
    w = (2.0 * math.pi * np.arange(L, dtype=f32)[:, None] / L).astype(f32)
    f = np.linspace(1e-4, 15, 16, dtype=f32)[None, :]
    z = np.concatenate([t, np.cos(f * w), -np.sin(f * w)], axis=-1).astype(f32)
    _CONST["zT"] = np.ascontiguousarray(z.T)
    negt = -(np.arange(2048, dtype=np.float64) / (L - 1))
    _CONST["negt"] = negt.reshape(16, 128).T.astype(np.float32)
    nd = -(np.abs(deltas).astype(np.float64) / (L - 1))
    _CONST["negdel"] = nd.reshape(8, 128).T.astype(np.float32)
    return _CONST


def pm(v, nch):
    return np.ascontiguousarray(np.asarray(v).reshape(nch, 128).T)


def make_pvec(inp, rev):
    C = consts()
    pv = np.zeros((128, NPV), np.float32)
    pv[:, 0:16] = pm(inp["mix_pre_g"][0], 16)
    pv[:, 16:32] = pm(inp["mix_post_g"][0], 16)
    pv[:, 32:48] = pm(inp["ffn_pre_g"][0], 16)
    pv[:, 48:64] = pm(inp["ffn_post_g"][0], 16)
    hcw = inp["hy_conv_w"][0]
    if rev:
        hcw = hcw[::-1]
    pv[:, 64:136] = hcw.T.reshape(24, 128, 3).transpose(1, 0, 2).reshape(128, 72)
    pv[:, 136:160] = pm(inp["hy_conv_b"][0], 24)
    cdw = inp["cf_dw_w"][0]
    if rev:
        cdw = cdw[::-1]
    pv[:, 160:408] = cdw.T.reshape(8, 128, 31).transpose(1, 0, 2).reshape(128, 248)
    pv[:, 408:416] = pm(inp["cf_dw_b"][0], 8)
    pv[:, 416:424] = pm(inp["cf_ln_g"][0], 8)
    pv[:, 424:432] = pm(inp["cf_ln_b"][0], 8)
    hb = inp["hy_bias"][0]
    pv[:, 432:448] = hb.reshape(2, 8, 128).transpose(2, 0, 1).reshape(128, 16)
    pv[:, 448] = 0.0 if rev else 1.0
    pv[:, 449] = 1.0 if rev else 0.0
    pv[:, 450:466] = C["negt"]
    pv[:, 466:474] = C["negdel"]
    return pv


PV = dict(g1=0, g2=16, g3=32, g4=48, hcw=64, hcb=136, cdw=160, cdb=408, lng=416, lnb=424, hyb=432,
          flags=448, negt=450, negdel=466)


def build(stages=("F", "N1", "P", "C", "M", "G", "Dn"), dbg=None):
    nc = bass.Bass("TRN2", target_bir_lowering=False)
    P = Prog(nc)

    def din(n, s, dt=F32):
        return nc.dram_tensor(n, list(s), dt, kind="ExternalInput").ap()

    def dscr(n, s, dt=F32):
        return nc.dram_tensor(n, list(s), dt).ap()

    x_d = din("x", [2048, 2048])
    pvec_d = din("pvec", [128, NPV])
    w_in_d = din("w_in", [D, NIN])
    hy_proj_d = din("hy_proj", [HWID, D])
    cf_proj_d = din("cf_proj", [HWID, D])
    w_out_d = din("w_out", [D, D])
    w_gu_d = din("w_gu", [D, 2 * FF])
    w_down_d = din("w_down", [FF, D])
    w1_d = din("fw1", [33, 64])
    fm_d = din("fmlp", [64, 68])
    w3_d = din("fw3", [64, 4096])
    zT_d = din("zT", [33, 2048])
    tabF_d = [din("tabF_c", [16, 128, 16, 128], BF16), din("tabF_s", [16, 128, 16, 128], BF16)]
    tabD_d = [din("tabD_c", [16, 128, 16, 128], BF16), din("tabD_s", [16, 128, 16, 128], BF16)]
    tabR_d = din("tabR", [2, 2048, 2048], BF16)
    identb_d = din("ident_bf", [128, 128], BF16)
    identf_d = din("ident_f", [128, 128])
    onesb_d = din("ones_bf", [128, 128], BF16)
    tau_d = din("tau_row", [1, 2048])
    adel_d = din("adel_row", [1, 1024])
    y_d = nc.dram_tensor("y", [T1, D], F32, kind="ExternalOutput").ap()
    dbg_d = None
    dbg_name = None
    if dbg is not None:
        dbg_name, dshape = dbg
        dbg_d = nc.dram_tensor("dbg", list(dshape), F32, kind="ExternalOutput").ap()

    KT_s = dscr("KT_s", [2, 2, 2048, 1024])
    xT_s = dscr("xT_s", [16, 128, T1])
    v_s = dscr("v_s", [8, 128, 2048])
    x1_s = dscr("x1_s", [8, 128, 2048])
    x2_s = dscr("x2_s", [8, 128, T1])
    yaT_s = dscr("yaT_s", [8, 128, T1], BF16)
    ybT_s = dscr("ybT_s", [8, 128, T1], BF16)
    sg_s = dscr("sg_s", [2, 16, 128, T1])
    r1T_s = dscr("r1T_s", [16, 128, T1])
    act_s = dscr("act_s", [44, 128, T1], BF16)

    pv = nc.alloc_sbuf_tensor("pv", [128, NPV], F32)
    identb = nc.alloc_sbuf_tensor("identb", [128, 128], BF16)
    identf = nc.alloc_sbuf_tensor("identf", [128, 128], F32)
    onesb = nc.alloc_sbuf_tensor("onesb", [128, 128], BF16)
    rl1s = nc.alloc_sbuf_tensor("rl1s", [128, 16], F32)
    epsc = nc.alloc_sbuf_tensor("epsc", [128, 1], F32)
    ARENA_BYTES = 200 * 1024
    arena = nc.alloc_sbuf_tensor("arena", [128, ARENA_BYTES // 2], BF16)
    psum = nc.alloc_psum_tensor("psum", [128, 8, 512], F32)
    Bc = Buf("const")
    Brl1 = Buf("rl1s")

    class Arena:
        def __init__(self):
            self.off = 0

        def take(self, nelem, dt):
            sz = 2 if dt == BF16 else 4
            nb = (nelem * sz + 63) // 64 * 64
            assert self.off + nb <= ARENA_BYTES, (self.off, nb)
            v = arena[:, self.off // 2:(self.off + nb) // 2]
            self.off += nb
            if dt == F32:
                v = v.bitcast(F32)
            return v[:, 0:nelem]

    def pvs(name, i=0, n=1):
        o = PV[name] + i
        return pv[:, o:o + n]

    def PB():
        return [Buf("pb%d" % i) for i in range(8)]

    P.dma("sp", lambda e: e.dma_start(out=pv[:], in_=pvec_d), writes=[Bc], key="c0")
    P.dma("sp", lambda e: e.dma_start(out=identb[:], in_=identb_d), writes=[Bc], key="c1")
    P.dma("sp", lambda e: e.dma_start(out=identf[:], in_=identf_d), writes=[Bc], key="c2")
    P.dma("sp", lambda e: e.dma_start(out=onesb[:], in_=onesb_d), writes=[Bc], key="c3")
    P.op("dve", lambda e: e.memset(epsc[:], EPS), writes=[Bc])
    P.barrier()
    Bc = Buf("const")

    def phase_F():
        A = Arena()
        zT = A.take(2048, F32)
        h1T = A.take(2048, F32)
        h2T = A.take(2048, F32)
        w3 = A.take(4096, F32)
        w1 = A.take(64, F32)
        fm = A.take(68, F32)
        cc12 = A.take(2, F32)
        tau_bc = A.take(2048, F32)
        adel_bc = A.take(1024, F32)
        arg = [A.take(512, F32) for _ in range(2)]
        l1p = A.take(128, F32)
        l1q = A.take(32, F32)
        l1t = A.take(16, F32)
        junk = A.take(512, F32)
        wincm = A.take(2048, F32)
        wintm = [A.take(1024, F32) for _ in range(2)]
        tmpf = [A.take(512, F32) for _ in range(2)]
        tmpb = [A.take(512, F32) for _ in range(2)]
        tmpa = [A.take(512, F32) for _ in range(2)]
        ke = A.take(16 * 1024, BF16).rearrange("p (a b) -> p a b", b=1024)
        kd = A.take(16 * 1024, BF16).rearrange("p (a b) -> p a b", b=1024)
        tsl = [A.take(16 * 128, BF16).rearrange("p (a b) -> p a b", b=128) for _ in range(3)]
        ksb = [A.take(1024, F32) for _ in range(2)]
        pb = PB()
        Bz, Bw = Buf("Fz"), Buf("Fw")
        Bh1, Bh2 = Buf("h1"), Buf("h2")
        Barg = [Buf("arg0"), Buf("arg1")]
        Bjk = Buf("jk")
        P.dma("sp", lambda e: e.dma_start(out=zT[0:33, :], in_=zT_d), writes=[Bz], key="f0")
        P.dma("sp", lambda e: e.dma_start(out=w1[0:33, :], in_=w1_d), writes=[Bw], key="f1")
        P.dma("sp", lambda e: e.dma_start(out=fm[0:64, :], in_=fm_d), writes=[Bw], key="f2")
        P.dma("sp", lambda e: e.dma_start(out=w3[0:64, :], in_=w3_d), writes=[Bw], key="f3")
        P.dma("sp", lambda e: e.dma_start(out=tau_bc, in_=tau_d.partition_broadcast(128)), writes=[Bw], key="f4")
        P.dma("sp", lambda e: e.dma_start(out=adel_bc, in_=adel_d.partition_broadcast(128)), writes=[Bw], key="f5")
        Bcc = Buf("cc")
        P.op("dve", lambda e: e.tensor_tensor(out=cc12[0:64, 0:1], in0=fm[0:64, 64:65], in1=fm[0:64, 65:66], op=ALU.mult),
             reads=[Bw], writes=[Bcc])
        P.op("dve", lambda e: e.tensor_tensor(out=cc12[0:64, 1:2], in0=fm[0:64, 66:67], in1=fm[0:64, 67:68], op=ALU.mult),
             reads=[Bw, Bcc], writes=[Bcc])
        cnt = 0
        for layer in range(2):
            src = zT if layer == 0 else h1T
            dst = h1T if layer == 0 else h2T
            Bs = Bz if layer == 0 else Bh1
            Bd = Bh1 if layer == 0 else Bh2
            kk = 33 if layer == 0 else 64
            wl = w1[0:33, 0:64] if layer == 0 else fm[0:64, 0:64]
            frc = fm[0:64, 65:66] if layer == 0 else fm[0:64, 67:68]
            cb_ = cc12[0:64, layer:layer + 1]
            for tb in range(4):
                bk = cnt % 2
                cnt += 1
                ts = slice(tb * 512, (tb + 1) * 512)
                P.op("pe", lambda e, bk=bk, ts=ts, src=src, kk=kk, wl=wl: e.matmul(psum[0:64, bk, :], lhsT=wl, rhs=src[0:kk, ts], start=True, stop=True),
                     reads=[Bs, Bw], writes=[pb[bk]])
                P.op("act", lambda e, bk=bk, frc=frc, cb_=cb_: e.activation(out=arg[bk][0:64, :], in_=psum[0:64, bk, :], func=AF.Identity, scale=frc, bias=cb_),
                     reads=[pb[bk], Bw, Bcc], writes=[Barg[bk]])
                for _ in range(2):
                    for (thr, cmp_, per) in ((math.pi, ALU.is_gt, -2 * math.pi), (-math.pi, ALU.is_lt, 2 * math.pi)):
                        P.op("dve", lambda e, bk=bk, thr=thr, cmp_=cmp_, per=per: e.tensor_scalar(out=junk[0:64, :], in0=arg[bk][0:64, :], scalar1=thr, scalar2=per, op0=cmp_, op1=ALU.mult),
                             reads=[Barg[bk]], writes=[Bjk])
                        P.op("dve", lambda e, bk=bk: e.tensor_tensor(out=arg[bk][0:64, :], in0=arg[bk][0:64, :], in1=junk[0:64, :], op=ALU.add),
                             reads=[Barg[bk], Bjk], writes=[Barg[bk]])
                P.op("act", lambda e, bk=bk, dst=dst, ts=ts: e.activation(out=dst[0:64, ts], in_=arg[bk][0:64, :], func=AF.Sin),
                     reads=[Barg[bk]], writes=[Bd])
        Bwin, Bl1p = Buf("wincm"), Buf("l1p")
        Btmp = [Buf("tmpa0"), Buf("tmpa1")]
        Bjunk = Buf("junk")
        cnt = 0
        for cc in range(8):
            P.op("act", lambda e, cc=cc: e.activation(out=wincm, in_=tau_bc, func=AF.Exp, scale=pvs("negdel", cc)),
                 reads=[Bw, Bc], writes=[Bwin])
            for o in range(2):
                for dr in range(2):
                    q = o * 16 + dr * 8 + cc
                    for tb in range(4):
                        bk = 2 + cnt % 2
                        tm = cnt % 2
                        cnt += 1
                        ts = slice(tb * 512, (tb + 1) * 512)
                        P.op("pe", lambda e, bk=bk, q=q, ts=ts: e.matmul(psum[:, bk, :], lhsT=w3[0:64, q * 128:(q + 1) * 128], rhs=h2T[0:64, ts], start=True, stop=True),
                             reads=[Bw, Bh2], writes=[pb[bk]])
                        P.op("dve", lambda e, bk=bk, tm=tm, ts=ts: e.tensor_tensor(out=tmpa[tm], in0=psum[:, bk, :], in1=wincm[:, ts], op=ALU.mult),
                             reads=[pb[bk], Bwin], writes=[Btmp[tm]])
                        if tb == 0:
                            P.op("dve", lambda e, tm=tm, dr=dr: e.tensor_scalar(out=tmpa[tm][:, 0:1], in0=tmpa[tm][:, 0:1], scalar1=pvs("flags", dr), scalar2=None, op0=ALU.mult),
                                 reads=[Btmp[tm], Bc], writes=[Btmp[tm]])
                        P.op("act", lambda e, tm=tm, q=q, tb=tb: e.activation(out=junk, in_=tmpa[tm], func=AF.Abs, accum_out=l1p[:, q * 4 + tb:q * 4 + tb + 1]),
                             reads=[Btmp[tm]], writes=[Bjunk, Bl1p])
        P.op("dve", lambda e: e.tensor_reduce(out=l1q, in_=l1p.rearrange("p (a b) -> p a b", b=4), axis=AX.X, op=ALU.add),
             reads=[Bl1p], writes=[Bl1p])
        for o in range(2):
            P.op("dve", lambda e, o=o: e.tensor_tensor(out=l1t[:, o * 8:(o + 1) * 8], in0=l1q[:, o * 16:o * 16 + 8], in1=l1q[:, o * 16 + 8:o * 16 + 16], op=ALU.add),
                 reads=[Bl1p], writes=[Bl1p])
        P.op("dve", lambda e: e.reciprocal(out=l1t, in_=l1t), reads=[Bl1p], writes=[Bl1p])
        P.op("dve", lambda e: e.tensor_scalar(out=rl1s[:], in0=l1t, scalar1=1.0 / 2048.0, scalar2=None, op0=ALU.mult),
             reads=[Bl1p], writes=[Brl1])
        Bwt = [Buf("wintm0"), Buf("wintm1")]
        Btf = [Buf("tmpf0"), Buf("tmpf1")]
        Btb = [Buf("tmpb0"), Buf("tmpb1")]
        Bke, Bkd = Buf("ke"), Buf("kd")
        Bts = [Buf("tsl%d" % i) for i in range(3)]
        Bks = [Buf("ksb0"), Buf("ksb1")]
        BKT = Buf("KT")
        cnt = 0
        tcnt = 0
        kcnt = 0
        for o in range(2):
            for tc in range(16):
                wi = tc % 2
                P.op("act", lambda e, wi=wi, tc=tc: e.activation(out=wintm[wi], in_=adel_bc, func=AF.Exp, scale=pvs("negt", tc)),
                     reads=[Bw, Bc], writes=[Bwt[wi]])
                for half in range(2):
                    colf = (o * 2 + 0) * 1024 + half * 512
                    colb = (o * 2 + 1) * 1024 + half * 512
                    i2 = cnt % 2
                    cnt += 1
                    bF, bB = 4 + 2 * i2, 5 + 2 * i2
                    tsl_ = slice(tc * 128, (tc + 1) * 128)
                    hs = slice(half * 512, (half + 1) * 512)
                    P.op("pe", lambda e, bF=bF, tsl_=tsl_, colf=colf: e.matmul(psum[:, bF, :], lhsT=h2T[0:64, tsl_], rhs=w3[0:64, colf:colf + 512], start=True, stop=True),
                         reads=[Bw, Bh2], writes=[pb[bF]])
                    P.op("pe", lambda e, bB=bB, tsl_=tsl_, colb=colb: e.matmul(psum[:, bB, :], lhsT=h2T[0:64, tsl_], rhs=w3[0:64, colb:colb + 512], start=True, stop=True),
                         reads=[Bw, Bh2], writes=[pb[bB]])
                    P.op("dve", lambda e, bF=bF, i2=i2, wi=wi, hs=hs: e.tensor_tensor(out=tmpf[i2], in0=psum[:, bF, :], in1=wintm[wi][:, hs], op=ALU.mult),
                         reads=[pb[bF], Bwt[wi]], writes=[Btf[i2]])
                    P.op("dve", lambda e, bB=bB, i2=i2, wi=wi, hs=hs: e.tensor_tensor(out=tmpb[i2], in0=psum[:, bB, :], in1=wintm[wi][:, hs], op=ALU.mult),
                         reads=[pb[bB], Bwt[wi]], writes=[Btb[i2]])
                    if tc == 0:
                        P.op("dve", lambda e, i2=i2: e.tensor_scalar(out=tmpf[i2][0:1, :], in0=tmpf[i2][0:1, :], scalar1=pv[0:1, PV["flags"]:PV["flags"] + 1], scalar2=None, op0=ALU.mult),
                             reads=[Btf[i2], Bc], writes=[Btf[i2]])
                        P.op("dve", lambda e, i2=i2: e.tensor_scalar(out=tmpb[i2][0:1, :], in0=tmpb[i2][0:1, :], scalar1=pv[0:1, PV["flags"] + 1:PV["flags"] + 2], scalar2=None, op0=ALU.mult),
                             reads=[Btb[i2], Bc], writes=[Btb[i2]])
                    P.op("pool", lambda e, i2=i2, tc=tc, hs=hs: e.tensor_tensor(out=ke[:, tc, hs], in0=tmpf[i2], in1=tmpb[i2], op=ALU.add),
                         reads=[Btf[i2], Btb[i2]], writes=[Bke])
                    P.op("pool", lambda e, i2=i2, tc=tc, hs=hs: e.tensor_tensor(out=kd[:, tc, hs], in0=tmpf[i2], in1=tmpb[i2], op=ALU.subtract),
                         reads=[Btf[i2], Btb[i2]], writes=[Bkd])
            for part in range(2):
                src, Bsrc = (ke, Bke) if part == 0 else (kd, Bkd)
                for fc in range(16):
                    s3 = tcnt % 3
                    tcnt += 1
                    P.dma("sp", lambda e, s3=s3, part=part, fc=fc: e.dma_start(out=tsl[s3], in_=tabF_d[part][fc]), writes=[Bts[s3]])
                    ks = kcnt % 2
                    kcnt += 1
                    for cb in range(2):
                        bk = (kcnt * 2 + cb) % 4
                        for tc in range(16):
                            P.op("pe", lambda e, bk=bk, s3=s3, tc=tc, cb=cb, src=src: e.matmul(psum[:, bk, :], lhsT=tsl[s3][:, tc, :], rhs=src[:, tc, cb * 512:(cb + 1) * 512], start=(tc == 0), stop=(tc == 15)),
                                 reads=[Bts[s3], Bsrc], writes=[pb[bk]])
                        eng = "act" if cb == 0 else "dve"
                        if eng == "act":
                            P.op("act", lambda e, bk=bk, ks=ks, cb=cb: e.copy(out=ksb[ks][:, cb * 512:(cb + 1) * 512], in_=psum[:, bk, :]),
                                 reads=[pb[bk]], writes=[Bks[ks]])
                        else:
                            P.op("dve", lambda e, bk=bk, ks=ks, cb=cb: e.tensor_copy(out=ksb[ks][:, cb * 512:(cb + 1) * 512], in_=psum[:, bk, :]),
                                 reads=[pb[bk]], writes=[Bks[ks]])
                    P.dma("sp", lambda e, ks=ks, o=o, part=part, fc=fc: e.dma_start(out=KT_s[o, part, fc * 128:(fc + 1) * 128, :], in_=ksb[ks]),
                          reads=[Bks[ks]], writes=[BKT], key="ksb_st%d" % ks)
        P.barrier()

    if "F" in stages:
        phase_F()

    TAIL_OFF = 168 * 1024
    uT = arena[:, TAIL_OFF // 2:(TAIL_OFF + 32768) // 2].rearrange("p (a b) -> p a b", b=1024)
    BuT = [Buf("uT0"), Buf("uT1")]
    z_s = dscr("z_s", [8, 128, T1])
    cpy_cnt = [0]

    def evac(out, in_, Bin, Bout):
        cpy_cnt[0] += 1
        if cpy_cnt[0] % 2:
            P.op("act", lambda e: e.copy(out=out, in_=in_), reads=Bin, writes=Bout)
        else:
            P.op("dve", lambda e: e.tensor_copy(out=out, in_=in_), reads=Bin, writes=Bout)

    def rms_stats(sq3, nch, ncol, bank, pbk, Bsq, rstd, Brstd):
        for c in range(nch):
            P.op("pe", lambda e, c=c: e.matmul(psum[:, bank, 0:ncol], lhsT=onesb[:], rhs=sq3[:, c, :], start=(c == 0), stop=(c == nch - 1)),
                 reads=[Bsq, Bc], writes=[pbk])
        P.op("act", lambda e: e.activation(out=rstd, in_=psum[:, bank, 0:ncol], func=AF.Sqrt, scale=1.0 / (nch * 128), bias=epsc[:, 0:1]),
             reads=[pbk, Bc], writes=[Brstd])
        P.op("dve", lambda e: e.reciprocal(out=rstd, in_=rstd), reads=[Brstd], writes=[Brstd])

    def phase_N1():
        A = Arena()
        hT = A.take(16 * 2048, BF16).rearrange("p (a b) -> p a b", b=2048)
        xt = [A.take(2048, F32) for _ in range(2)]
        xTb = A.take(16 * 512, F32).rearrange("p (a b) -> p a b", b=512)
        sq = A.take(16 * 512, BF16).rearrange("p (a b) -> p a b", b=512)
        rstd = A.take(512, F32)
        pb = PB()
        Bxt = [Buf("xt0"), Buf("xt1")]
        BxTb, Bsq, Brs, BhT, BxTs = Buf("xTb"), Buf("sq"), Buf("rstd"), Buf("hT"), Buf("xTs")
        bcnt = 0
        for tb in range(4):
            for i in range(4):
                tile = tb * 4 + i
                s = tile % 2
                P.dma("sp", lambda e, s=s, tile=tile: e.dma_start(out=xt[s], in_=x_d[tile * 128:(tile + 1) * 128, :]), writes=[Bxt[s]])
                for g4 in range(4):
                    bank = bcnt % 4
                    bcnt += 1
                    for q in range(4):
                        dc = g4 * 4 + q
                        P.op("pe", lambda e, s=s, bank=bank, q=q, dc=dc: e.transpose(out=psum[:, bank, q * 128:(q + 1) * 128], in_=xt[s][:, dc * 128:(dc + 1) * 128], identity=identf[:]),
                             reads=[Bxt[s], Bc], writes=[pb[bank]])
                    evac(xTb[:, g4 * 4:(g4 + 1) * 4, i * 128:(i + 1) * 128], psum[:, bank, :].rearrange("p (a b) -> p a b", b=128), [pb[bank]], [BxTb])
            for g4 in range(4):
                P.op("act", lambda e, g4=g4: e.activation(out=sq[:, g4 * 4:(g4 + 1) * 4, :], in_=xTb[:, g4 * 4:(g4 + 1) * 4, :], func=AF.Square),
                     reads=[BxTb], writes=[Bsq])
            rms_stats(sq, 16, 512, 4, pb[4], Bsq, rstd, Brs)
            for dc in range(16):
                P.op("dve", lambda e, dc=dc, tb=tb: e.scalar_tensor_tensor(out=hT[:, dc, tb * 512:(tb + 1) * 512], in0=xTb[:, dc, :], scalar=pvs("g1", dc), in1=rstd, op0=ALU.mult, op1=ALU.mult),
                     reads=[BxTb, Brs, Bc], writes=[BhT])
            if tb < 2:
                P.dma("sp", lambda e, tb=tb: e.dma_start(out=xT_s[:, :, tb * 512:(tb + 1) * 512].rearrange("c p t -> p c t"), in_=xTb),
                      reads=[BxTb], writes=[BxTs], key="xTb_st")
        P.barrier()

    def phase_P():
        A = Arena()
        hT = A.take(16 * 2048, BF16).rearrange("p (a b) -> p a b", b=2048)
        wsl = [A.take(16 * 512, BF16).rearrange("p (a b) -> p a b", b=512) for _ in range(2)]
        sgt = [A.take(512, F32) for _ in range(2)]
        mark = A.off
        p_sb = [A.take(2050, F32) for _ in range(2)]
        acc = [A.take(2048, F32) for _ in range(2)]
        accb = A.take(2048, BF16)
        assert A.off <= TAIL_OFF, A.off
        A.off = mark
        upad = A.take(15 + 1056 + 1, BF16)
        dg = A.take(31 * 128, BF16).rearrange("p (a b) -> p a b", b=128)
        ucv = A.take(8 * 1024, F32).rearrange("p (a b) -> p a b", b=1024)
        ub = A.take(8 * 512, BF16).rearrange("p (a b) -> p a b", b=512)
        usq = A.take(8 * 512, BF16).rearrange("p (a b) -> p a b", b=512)
        mu = A.take(512, F32)
        m2 = A.take(512, F32)
        rstd = A.take(512, F32)
        tq = sgt
        ybt = [A.take(512, BF16) for _ in range(2)]
        assert A.off <= TAIL_OFF, A.off
        pb = PB()
        BhT = Buf("hT")
        Bws = [Buf("wsl0"), Buf("wsl1")]
        Bp = [Buf("p_sb0"), Buf("p_sb1")]
        Bacc = [Buf("acc0"), Buf("acc1")]
        Baccb, Bup, Bdg = Buf("accb"), Buf("upad"), Buf("dg")
        Bsg = [Buf("sgt0"), Buf("sgt1")]
        Bucv, Bub, Busq, Bmu, Bm2, Brs = Buf("ucv"), Buf("ub"), Buf("usq"), Buf("mu"), Buf("m2"), Buf("rstdP")
        Btq = Bsg
        Bybt = [Buf("ybt0"), Buf("ybt1")]
        Bvs, Bx1s, Bx2s, Bybs, Bsgs = Buf("v_s"), Buf("x1_s"), Buf("x2_s"), Buf("ybT_s"), Buf("sg_s")
        for i in range(2):
            P.op("dve", lambda e, i=i: e.memset(p_sb[i], 0.0), writes=[Bp[i]])
        wcnt = [0]
        bcnt = [0]

        def loadw(cols):
            s = wcnt[0] % 2
            wcnt[0] += 1

            def fn(e, s=s, cols=cols):
                r = []
                o = 0
                for (c0, n) in cols:
                    r.append(e.dma_start(out=wsl[s][:, :, o:o + n], in_=w_in_d[:, c0:c0 + n].rearrange("(k p) n -> p k n", p=128)))
                    o += n
                return r
            P.dma("pool", fn, writes=[Bws[s]], n=len(cols))
            return s

        def mm16(bank, n, s, col, t0):
            for k in range(16):
                P.op("pe", lambda e, k=k: e.matmul(psum[:, bank, 0:n], lhsT=wsl[s][:, k, col:col + 128], rhs=hT[:, k, t0:t0 + n], start=(k == 0), stop=(k == 15)),
                     reads=[Bws[s], BhT], writes=[pb[bank]])

        def nextbank(lo=0, n=4):
            b = lo + bcnt[0] % n
            bcnt[0] += 1
            return b

        for blk in range(6):
            s = loadw([(blk * 512, 512)])
            for q in range(4):
                ch = blk * 4 + q
                kind = ch // 8
                i2 = ch % 2
                groups = [(0, 512), (512, 512), (1024, 512), (1536, 512)] if kind < 2 else [(0, 512), (512, 512), (1024, 32)]
                for (t0, n) in groups:
                    bank = nextbank()
                    mm16(bank, n, s, q * 128, t0)
                    evac(p_sb[i2][:, 1 + t0:1 + t0 + n], psum[:, bank, 0:n], [pb[bank]], [Bp[i2]])
                nt = 2048 if kind < 2 else 1024
                a = acc[i2][:, 0:nt]
                P.op("act", lambda e, a=a, i2=i2, nt=nt, ch=ch: e.activation(out=a, in_=p_sb[i2][:, 1:1 + nt], func=AF.Identity, scale=pvs("hcw", ch * 3 + 1), bias=pvs("hcb", ch)),
                     reads=[Bp[i2], Bc], writes=[Bacc[i2]])
                P.op("dve", lambda e, a=a, i2=i2, nt=nt, ch=ch: e.scalar_tensor_tensor(out=a, in0=p_sb[i2][:, 0:nt], scalar=pvs("hcw", ch * 3 + 0), in1=a, op0=ALU.mult, op1=ALU.add),
                     reads=[Bp[i2], Bc, Bacc[i2]], writes=[Bacc[i2]])
                P.op("dve", lambda e, a=a, i2=i2, nt=nt, ch=ch: e.scalar_tensor_tensor(out=a, in0=p_sb[i2][:, 2:2 + nt], scalar=pvs("hcw", ch * 3 + 2), in1=a, op0=ALU.mult, op1=ALU.add),
                     reads=[Bp[i2], Bc, Bacc[i2]], writes=[Bacc[i2]])
                dst = (v_s, x1_s, x2_s)[kind][ch % 8]
                Bd = (Bvs, Bx1s, Bx2s)[kind]
                P.dma("sp", lambda e, a=a, dst=dst: e.dma_start(out=dst, in_=a), reads=[Bacc[i2]], writes=[Bd], key="acc_st%d" % i2)
                if kind == 0:
                    P.op("pool", lambda e, i2=i2: e.tensor_copy(out=accb, in_=acc[i2]), reads=[Bacc[i2]], writes=[Baccb])
                    psT = psum[:, 4:6, :].rearrange("p a b -> p (a b)").bitcast(BF16).rearrange("p (a b) -> p a b", b=128)
                    for tc in range(16):
                        P.op("pe", lambda e, tc=tc: e.transpose(out=psT[:, tc, :], in_=accb[:, tc * 128:(tc + 1) * 128], identity=identb[:]),
                             reads=[Baccb, Bc], writes=[pb[4], pb[5]])
                    P.op("act", lambda e, ch=ch: e.copy(out=uT[:, :, ch * 128:(ch + 1) * 128], in_=psT), reads=[pb[4], pb[5]], writes=[BuT[ch // 4]])
        P.barrier()
        P.op("dve", lambda e: e.memset(upad, 0.0), writes=[Bup])
        for pg in range(4):
            s = loadw([(3072 + pg * 256, 256), (4096 + pg * 256, 256)])
            for i in range(2):
                cc = pg * 2 + i
                for (t0, n) in [(0, 512), (512, 512), (1024, 32)]:
                    bA = nextbank()
                    mm16(bA, n, s, i * 128, t0)
                    bB = nextbank()
                    mm16(bB, n, s, 256 + i * 128, t0)
                    g2 = bB % 2
                    P.op("act", lambda e, bB=bB, n=n, g2=g2: e.activation(out=sgt[g2][:, 0:n], in_=psum[:, bB, 0:n], func=AF.Sigmoid),
                         reads=[pb[bB]], writes=[Bsg[g2]])
                    P.op("dve", lambda e, bA=bA, n=n, g2=g2, t0=t0: e.tensor_tensor(out=upad[:, 15 + t0:15 + t0 + n], in0=psum[:, bA, 0:n], in1=sgt[g2][:, 0:n], op=ALU.mult),
                         reads=[pb[bA], Bsg[g2]], writes=[Bup])
                for k in range(31):
                    P.op("pool", lambda e, k=k, cc=cc: e.tensor_scalar(out=dg[:, k, :], in0=identb[:], scalar1=pvs("cdw", cc * 31 + k), scalar2=None, op0=ALU.mult),
                         reads=[Bc], writes=[Bdg])
                for tb in range(2):
                    bank = nextbank()
                    for k in range(31):
                        P.op("pe", lambda e, k=k, tb=tb, bank=bank: e.matmul(psum[:, bank, :], lhsT=dg[:, k, :], rhs=upad[:, tb * 512 + k:tb * 512 + k + 512], start=(k == 0), stop=(k == 30)),
                             reads=[Bdg, Bup], writes=[pb[bank]])
                    P.op("act", lambda e, tb=tb, bank=bank, cc=cc: e.activation(out=ucv[:, cc, tb * 512:(tb + 1) * 512], in_=psum[:, bank, :], func=AF.Identity, scale=1.0, bias=pvs("cdb", cc)),
                         reads=[pb[bank], Bc], writes=[Bucv])
        for tb in range(2):
            tsl_ = slice(tb * 512, (tb + 1) * 512)
            P.op("act", lambda e, tsl_=tsl_: e.activation(out=usq, in_=ucv[:, :, tsl_], func=AF.Square), reads=[Bucv], writes=[Busq])
            P.op("pool", lambda e, tsl_=tsl_: e.tensor_copy(out=ub, in_=ucv[:, :, tsl_]), reads=[Bucv], writes=[Bub])
            for c in range(8):
                P.op("pe", lambda e, c=c: e.matmul(psum[:, 6, :], lhsT=onesb[:], rhs=ub[:, c, :], start=(c == 0), stop=(c == 7)), reads=[Bub, Bc], writes=[pb[6]])
            for c in range(8):
                P.op("pe", lambda e, c=c: e.matmul(psum[:, 7, :], lhsT=onesb[:], rhs=usq[:, c, :], start=(c == 0), stop=(c == 7)), reads=[Busq, Bc], writes=[pb[7]])
            P.op("dve", lambda e: e.tensor_scalar(out=mu, in0=psum[:, 6, :], scalar1=1.0 / 1024, scalar2=None, op0=ALU.mult), reads=[pb[6]], writes=[Bmu])
            P.op("dve", lambda e: e.tensor_tensor(out=m2, in0=mu, in1=mu, op=ALU.mult), reads=[Bmu], writes=[Bm2])
            P.op("dve", lambda e: e.scalar_tensor_tensor(out=m2, in0=psum[:, 7, :], scalar=1.0 / 1024, in1=m2, op0=ALU.mult, op1=ALU.subtract), reads=[pb[7], Bm2], writes=[Bm2])
            P.op("act", lambda e: e.activation(out=rstd, in_=m2, func=AF.Sqrt, scale=1.0, bias=epsc[:, 0:1]), reads=[Bm2, Bc], writes=[Brs])
            P.op("dve", lambda e: e.reciprocal(out=rstd, in_=rstd), reads=[Brs], writes=[Brs])
            for c in range(8):
                j = c % 2
                P.op("dve", lambda e, c=c, j=j, tsl_=tsl_: e.tensor_tensor(out=tq[j], in0=ucv[:, c, tsl_], in1=mu, op=ALU.subtract), reads=[Bucv, Bmu], writes=[Btq[j]])
                P.op("dve", lambda e, j=j: e.tensor_tensor(out=tq[j], in0=tq[j], in1=rstd, op=ALU.mult), reads=[Btq[j], Brs], writes=[Btq[j]])
                P.op("act", lambda e, c=c, j=j: e.activation(out=ybt[j], in_=tq[j], func=AF.Silu, scale=pvs("lng", c), bias=pvs("lnb", c)), reads=[Btq[j], Bc], writes=[Bybt[j]])
                P.dma("sp", lambda e, c=c, j=j, tsl_=tsl_: e.dma_start(out=ybT_s[c][:, tsl_], in_=ybt[j]), reads=[Bybt[j]], writes=[Bybs], key="ybt_st%d" % j)
        for blk in range(8):
            s = loadw([(5120 + blk * 512, 512)])
            for q in range(4):
                g = blk * 4 + q
                for tb in range(2):
                    bank = nextbank()
                    mm16(bank, 512, s, q * 128, tb * 512)
                    g2 = bank % 2
                    P.op("act", lambda e, bank=bank, g2=g2: e.activation(out=sgt[g2], in_=psum[:, bank, :], func=AF.Sigmoid), reads=[pb[bank]], writes=[Bsg[g2]])
                    P.dma("sp", lambda e, g=g, tb=tb, g2=g2: e.dma_start(out=sg_s[g // 16, g % 16][:, tb * 512:(tb + 1) * 512], in_=sgt[g2]), reads=[Bsg[g2]], writes=[Bsgs], key="sgt_st%d" % g2)
        P.barrier()

    def phase_C(ci):
        A = Arena()
        Y = A.take(2 * 16 * 512, BF16).rearrange("p (a b c) -> p a b c", a=2, b=16)
        tsc = [A.take(16 * 128, BF16).rearrange("p (a b) -> p a b", b=128) for _ in range(2)]
        tss = [A.take(16 * 128, BF16).rearrange("p (a b) -> p a b", b=128) for _ in range(2)]
        kr = [A.take(512, F32) for _ in range(2)]
        ki = [A.take(512, F32) for _ in range(2)]
        t1, t2, t3, t4 = [A.take(512, F32) for _ in range(4)]
        rt = [A.take(32 * 512, BF16).rearrange("p (a b) -> p a b", b=512) for _ in range(2)]
        vt = [A.take(512, F32) for _ in range(2)]
        xg = [A.take(512, F32) for _ in range(2)]
        zt = [A.take(512, F32) for _ in range(2)]
        zb = [A.take(512, BF16) for _ in range(2)]
        assert A.off <= TAIL_OFF, A.off
        pb = PB()
        BY = Buf("Y")
        Btc = [Buf("tsc0"), Buf("tsc1")]
        Bts = [Buf("tss0"), Buf("tss1")]
        Bkr = [Buf("kr0"), Buf("kr1")]
        Bki = [Buf("ki0"), Buf("ki1")]
        Bt = [Buf("t1"), Buf("t2"), Buf("t3"), Buf("t4")]
        Brt = [Buf("rt0"), Buf("rt1")]
        Bvt = [Buf("vt0"), Buf("vt1")]
        Bxg = [Buf("xg0"), Buf("xg1")]
        Bzt = [Buf("zt0"), Buf("zt1")]
        Bzb = [Buf("zb0"), Buf("zb1")]
        Bzs, Byas = Buf("z_s"), Buf("yaT_s")
        fcnt = 0
        rcnt = 0
        ecnt = 0
        for hh in range(2):
            hs = slice(hh * 512, (hh + 1) * 512)
            for fc in range(16):
                s = fcnt % 2
                fcnt += 1
                P.dma("sp", lambda e, s=s, fc=fc: e.dma_start(out=tsc[s], in_=tabD_d[0][fc]), writes=[Btc[s]])
                P.dma("sp", lambda e, s=s, fc=fc: e.dma_start(out=tss[s], in_=tabD_d[1][fc]), writes=[Bts[s]])
                P.dma("sp", lambda e, s=s, fc=fc, hs=hs: e.dma_start(out=kr[s], in_=KT_s[ci, 0, fc * 128:(fc + 1) * 128, hs]), writes=[Bkr[s]])
                P.dma("sp", lambda e, s=s, fc=fc, hs=hs: e.dma_start(out=ki[s], in_=KT_s[ci, 1, fc * 128:(fc + 1) * 128, hs]), writes=[Bki[s]])
                bR, bI = 2 * s, 2 * s + 1
                for tc in range(16):
                    P.op("pe", lambda e, tc=tc, s=s, bR=bR, hs=hs: e.matmul(psum[:, bR, :], lhsT=tsc[s][:, tc, :], rhs=uT[:, tc, hs], start=(tc == 0), stop=(tc == 15)),
                         reads=[Btc[s], BuT[hh]], writes=[pb[bR]])
                for tc in range(16):
                    P.op("pe", lambda e, tc=tc, s=s, bI=bI, hs=hs: e.matmul(psum[:, bI, :], lhsT=tss[s][:, tc, :], rhs=uT[:, tc, hs], start=(tc == 0), stop=(tc == 15)),
                         reads=[Bts[s], BuT[hh]], writes=[pb[bI]])
                P.op("dve", lambda e, s=s, bR=bR: e.tensor_tensor(out=t1, in0=psum[:, bR, :], in1=kr[s], op=ALU.mult), reads=[pb[bR], Bkr[s]], writes=[Bt[0]])
                P.op("dve", lambda e, s=s, bI=bI: e.tensor_tensor(out=t2, in0=psum[:, bI, :], in1=ki[s], op=ALU.mult), reads=[pb[bI], Bki[s]], writes=[Bt[1]])
                P.op("dve", lambda e, s=s, bR=bR: e.tensor_tensor(out=t3, in0=psum[:, bR, :], in1=ki[s], op=ALU.mult), reads=[pb[bR], Bki[s]], writes=[Bt[2]])
                P.op("dve", lambda e, s=s, bI=bI: e.tensor_tensor(out=t4, in0=psum[:, bI, :], in1=kr[s], op=ALU.mult), reads=[pb[bI], Bkr[s]], writes=[Bt[3]])
                P.op("pool", lambda e, fc=fc: e.tensor_tensor(out=Y[:, 0, fc, :], in0=t1, in1=t2, op=ALU.subtract), reads=[Bt[0], Bt[1]], writes=[BY])
                P.op("pool", lambda e, fc=fc: e.tensor_tensor(out=Y[:, 1, fc, :], in0=t3, in1=t4, op=ALU.add), reads=[Bt[2], Bt[3]], writes=[BY])
            ntb = 4 if ci == 0 else 2
            for tb in range(ntb):
                ts_ = slice(tb * 512, (tb + 1) * 512)
                r = rcnt % 2
                rcnt += 1

                def ld(e, r=r, ts_=ts_):
                    return [e.dma_start(out=rt[r][:, part * 16:(part + 1) * 16, :], in_=tabR_d[part].rearrange("(fc p) t -> p fc t", p=128)[:, :, ts_]) for part in range(2)]
                P.dma("sp", ld, writes=[Brt[r]], n=2)
                for c4 in range(4):
                    cc = hh * 4 + c4
                    j = ecnt % 2
                    ecnt += 1
                    bank = 4 + j
                    src_v = (v_s if ci == 0 else z_s)[cc][:, ts_]
                    src_x = (x1_s if ci == 0 else x2_s)[cc][:, ts_]
                    P.dma("sp", lambda e, j=j, src_v=src_v: e.dma_start(out=vt[j], in_=src_v), writes=[Bvt[j]])
                    P.dma("sp", lambda e, j=j, src_x=src_x: e.dma_start(out=xg[j], in_=src_x), writes=[Bxg[j]])
                    idx = 0
                    for part in range(2):
                        for fc in range(16):
                            P.op("pe", lambda e, part=part, fc=fc, c4=c4, r=r, bank=bank, idx=idx: e.matmul(psum[:, bank, :], lhsT=Y[:, part, fc, c4 * 128:(c4 + 1) * 128], rhs=rt[r][:, part * 16 + fc, :], start=(idx == 0), stop=(idx == 31)),
                                 reads=[BY, Brt[r]], writes=[pb[bank]])
                            idx += 1
                    P.op("dve", lambda e, j=j, cc=cc: e.tensor_scalar(out=vt[j], in0=vt[j], scalar1=pvs("hyb", ci * 8 + cc), scalar2=None, op0=ALU.mult), reads=[Bvt[j], Bc], writes=[Bvt[j]])
                    P.op("dve", lambda e, j=j, cc=cc, bank=bank: e.scalar_tensor_tensor(out=zt[j], in0=psum[:, bank, :], scalar=rl1s[:, ci * 8 + cc:ci * 8 + cc + 1], in1=vt[j], op0=ALU.mult, op1=ALU.add),
                         reads=[pb[bank], Brl1, Bvt[j]], writes=[Bzt[j]])
                    if ci == 0:
                        P.op("dve", lambda e, j=j: e.tensor_tensor(out=zt[j], in0=zt[j], in1=xg[j], op=ALU.mult), reads=[Bzt[j], Bxg[j]], writes=[Bzt[j]])
                        if tb < 2:
                            P.dma("sp", lambda e, j=j, cc=cc, ts_=ts_: e.dma_start(out=z_s[cc][:, ts_], in_=zt[j]), reads=[Bzt[j]], writes=[Bzs], key="zt_st%d" % j)
                        P.op("pool", lambda e, j=j: e.tensor_copy(out=zb[j], in_=zt[j]), reads=[Bzt[j]], writes=[Bzb[j]])
                        psT = psum[:, 6 + j, :].bitcast(BF16)[:, 0:512].rearrange("p (a b) -> p a b", b=128)
                        for i in range(4):
                            P.op("pe", lambda e, i=i, j=j, psT=psT: e.transpose(out=psT[:, i, :], in_=zb[j][:, i * 128:(i + 1) * 128], identity=identb[:]),
                                 reads=[Bzb[j], Bc], writes=[pb[6 + j]])
                        P.op("act", lambda e, psT=psT, tb=tb, cc=cc: e.copy(out=uT[:, tb * 4:(tb + 1) * 4, cc * 128:(cc + 1) * 128], in_=psT), reads=[pb[6 + j]], writes=[BuT[hh]])
                    else:
                        P.op("dve", lambda e, j=j: e.tensor_tensor(out=zb[j], in0=zt[j], in1=xg[j], op=ALU.mult), reads=[Bzt[j], Bxg[j]], writes=[Bzb[j]])
                        P.dma("sp", lambda e, j=j, cc=cc, ts_=ts_: e.dma_start(out=yaT_s[cc][:, ts_], in_=zb[j]), reads=[Bzb[j]], writes=[Byas], key="zb_st%d" % j)
        P.barrier()

    def phase_M():
        A = Arena()
        h2T = A.take(16 * 1024, BF16).rearrange("p (a b) -> p a b", b=1024)
        mT = A.take(16 * 1024, BF16).rearrange("p (a b) -> p a b", b=1024)
        wsl = [A.take(16 * 512, BF16).rearrange("p (a b) -> p a b", b=512) for _ in range(2)]
        mark = A.off
        yaT = A.take(8 * 1024, BF16).rearrange("p (a b) -> p a b", b=1024)
        ybT = A.take(8 * 1024, BF16).rearrange("p (a b) -> p a b", b=1024)
        ga = [A.take(512, F32) for _ in range(2)]
        gb = [A.take(512, F32) for _ in range(2)]
        m1 = [A.take(512, F32) for _ in range(2)]
        m2_ = [A.take(512, F32) for _ in range(2)]
        A.off = mark
        oT = A.take(16 * 512, F32).rearrange("p (a b) -> p a b", b=512)
        sq = A.take(16 * 512, BF16).rearrange("p (a b) -> p a b", b=512)
        rstd = A.take(512, F32)
        rstd2 = A.take(512, F32)
        xt = [A.take(512, F32) for _ in range(2)]
        tt = [A.take(512, F32) for _ in range(2)]
        pb = PB()
        Bya, Byb, BmT, Bh2 = Buf("yaT"), Buf("ybT"), Buf("mT"), Buf("h2T")
        Bws = [Buf("wslM0"), Buf("wslM1")]
        Bga = [Buf("ga0"), Buf("ga1")]
        Bgb = [Buf("gb0"), Buf("gb1")]
        Bm1 = [Buf("m10"), Buf("m11")]
        Bm2 = [Buf("m20"), Buf("m21")]
        BoT, Bsq, Brs, Brs2 = Buf("oT"), Buf("sqM"), Buf("rstdM"), Buf("rstdM2")
        Bxt = [Buf("xtM0"), Buf("xtM1")]
        Btt = [Buf("ttM0"), Buf("ttM1")]
        Br1s = Buf("r1T_s")
        P.dma("sp", lambda e: e.dma_start(out=yaT, in_=yaT_s.rearrange("c p t -> p c t")), writes=[Bya])
        P.dma("sp", lambda e: e.dma_start(out=ybT, in_=ybT_s.rearrange("c p t -> p c t")), writes=[Byb])
        wc = 0
        bc = 0
        for nb in range(4):
            s = wc % 2
            wc += 1

            def ldw(e, s=s, nb=nb):
                r = []
                for k in range(8):
                    r.append(e.dma_start(out=wsl[s][:, k, :], in_=hy_proj_d[k * 128:(k + 1) * 128, nb * 512:(nb + 1) * 512]))
                    r.append(e.dma_start(out=wsl[s][:, 8 + k, :], in_=cf_proj_d[k * 128:(k + 1) * 128, nb * 512:(nb + 1) * 512]))
                return r
            P.dma("pool", ldw, writes=[Bws[s]], n=16)
            for q in range(4):
                dch = nb * 4 + q
                for tb in range(2):
                    ts_ = slice(tb * 512, (tb + 1) * 512)
                    j = bc % 2
                    bc += 1
                    bA, bB = 2 * j, 2 * j + 1
                    P.dma("sp", lambda e, j=j, dch=dch, ts_=ts_: e.dma_start(out=ga[j], in_=sg_s[0, dch][:, ts_]), writes=[Bga[j]])
                    P.dma("sp", lambda e, j=j, dch=dch, ts_=ts_: e.dma_start(out=gb[j], in_=sg_s[1, dch][:, ts_]), writes=[Bgb[j]])
                    for k in range(8):
                        P.op("pe", lambda e, k=k, s=s, q=q, ts_=ts_, bA=bA: e.matmul(psum[:, bA, :], lhsT=wsl[s][:, k, q * 128:(q + 1) * 128], rhs=yaT[:, k, ts_], start=(k == 0), stop=(k == 7)),
                             reads=[Bws[s], Bya], writes=[pb[bA]])
                    for k in range(8):
                        P.op("pe", lambda e, k=k, s=s, q=q, ts_=ts_, bB=bB: e.matmul(psum[:, bB, :], lhsT=wsl[s][:, 8 + k, q * 128:(q + 1) * 128], rhs=ybT[:, k, ts_], start=(k == 0), stop=(k == 7)),
                             reads=[Bws[s], Byb], writes=[pb[bB]])
                    P.op("dve", lambda e, j=j, bA=bA: e.tensor_tensor(out=m1[j], in0=psum[:, bA, :], in1=ga[j], op=ALU.mult), reads=[pb[bA], Bga[j]], writes=[Bm1[j]])
                    P.op("dve", lambda e, j=j, bB=bB: e.tensor_tensor(out=m2_[j], in0=psum[:, bB, :], in1=gb[j], op=ALU.mult), reads=[pb[bB], Bgb[j]], writes=[Bm2[j]])
                    P.op("dve", lambda e, j=j, dch=dch, ts_=ts_: e.tensor_tensor(out=mT[:, dch, ts_], in0=m1[j], in1=m2_[j], op=ALU.add), reads=[Bm1[j], Bm2[j]], writes=[BmT])
        P.barrier()
        for tb in range(2):
            ts_ = slice(tb * 512, (tb + 1) * 512)
            for nb in range(4):
                s = wc % 2
                wc += 1
                P.dma("pool", lambda e, s=s, nb=nb: e.dma_start(out=wsl[s], in_=w_out_d[:, nb * 512:(nb + 1) * 512].rearrange("(k p) n -> p k n", p=128)), writes=[Bws[s]])
                for q in range(4):
                    nch = nb * 4 + q
                    bank = 4 + bc % 2
                    bc += 1
                    for k in range(16):
                        P.op("pe", lambda e, k=k, s=s, q=q, ts_=ts_, bank=bank: e.matmul(psum[:, bank, :], lhsT=wsl[s][:, k, q * 128:(q + 1) * 128], rhs=mT[:, k, ts_], start=(k == 0), stop=(k == 15)),
                             reads=[Bws[s], BmT], writes=[pb[bank]])
                    evac(oT[:, nch, :], psum[:, bank, :], [pb[bank]], [BoT])
                P.op("act", lambda e, nb=nb: e.activation(out=sq[:, nb * 4:(nb + 1) * 4, :], in_=oT[:, nb * 4:(nb + 1) * 4, :], func=AF.Square), reads=[BoT], writes=[Bsq])
            rms_stats(sq, 16, 512, 6, pb[6], Bsq, rstd, Brs)
            for nch in range(16):
                j = nch % 2
                P.dma("sp", lambda e, j=j, nch=nch, ts_=ts_: e.dma_start(out=xt[j], in_=xT_s[nch][:, ts_]), writes=[Bxt[j]])
                P.op("dve", lambda e, j=j, nch=nch: e.scalar_tensor_tensor(out=tt[j], in0=oT[:, nch, :], scalar=pvs("g2", nch), in1=rstd, op0=ALU.mult, op1=ALU.mult),
                     reads=[BoT, Brs, Bc], writes=[Btt[j]])
                P.op("dve", lambda e, j=j, nch=nch: e.tensor_tensor(out=oT[:, nch, :], in0=tt[j], in1=xt[j], op=ALU.add), reads=[Btt[j], Bxt[j], BoT], writes=[BoT])
            P.dma("sp", lambda e, ts_=ts_: e.dma_start(out=r1T_s[:, :, ts_].rearrange("c p t -> p c t"), in_=oT), reads=[BoT], writes=[Br1s], key="oT_st")
            for g4 in range(4):
                P.op("act", lambda e, g4=g4: e.activation(out=sq[:, g4 * 4:(g4 + 1) * 4, :], in_=oT[:, g4 * 4:(g4 + 1) * 4, :], func=AF.Square), reads=[BoT], writes=[Bsq])
            rms_stats(sq, 16, 512, 7, pb[7], Bsq, rstd2, Brs2)
            for nch in range(16):
                P.op("dve", lambda e, nch=nch, ts_=ts_: e.scalar_tensor_tensor(out=h2T[:, nch, ts_], in0=oT[:, nch, :], scalar=pvs("g3", nch), in1=rstd2, op0=ALU.mult, op1=ALU.mult),
                     reads=[BoT, Brs2, Bc], writes=[Bh2])
        P.barrier()

    def phase_G():
        A = Arena()
        h2T = A.take(16 * 1024, BF16).rearrange("p (a b) -> p a b", b=1024)
        wsl = [A.take(16 * 512, BF16).rearrange("p (a b) -> p a b", b=512) for _ in range(3)]
        st = [A.take(512, F32) for _ in range(2)]
        ab = [A.take(512, BF16) for _ in range(2)]
        pb = PB()
        Bh2 = Buf("h2T")
        Bws = [Buf("wslG%d" % i) for i in range(3)]
        Bst = [Buf("st0"), Buf("st1")]
        Bab = [Buf("ab0"), Buf("ab1")]
        Bas = Buf("act_s")
        bc = 0
        for blk in range(22):
            s = blk % 3

            def ldw(e, s=s, blk=blk):
                return [e.dma_start(out=wsl[s][:, :, 0:256], in_=w_gu_d[:, blk * 256:(blk + 1) * 256].rearrange("(k p) n -> p k n", p=128)),
                        e.dma_start(out=wsl[s][:, :, 256:512], in_=w_gu_d[:, FF + blk * 256:FF + (blk + 1) * 256].rearrange("(k p) n -> p k n", p=128))]
            P.dma("pool", ldw, writes=[Bws[s]], n=2)
            for i in range(2):
                kch = blk * 2 + i
                for tb in range(2):
                    ts_ = slice(tb * 512, (tb + 1) * 512)
                    j = bc % 2
                    bc += 1
                    bG, bU = 2 * j, 2 * j + 1
                    for k in range(16):
                        P.op("pe", lambda e, k=k, s=s, i=i, ts_=ts_, bG=bG: e.matmul(psum[:, bG, :], lhsT=wsl[s][:, k, i * 128:(i + 1) * 128], rhs=h2T[:, k, ts_], start=(k == 0), stop=(k == 15)),
                             reads=[Bws[s], Bh2], writes=[pb[bG]])
                    for k in range(16):
                        P.op("pe", lambda e, k=k, s=s, i=i, ts_=ts_, bU=bU: e.matmul(psum[:, bU, :], lhsT=wsl[s][:, k, 256 + i * 128:256 + (i + 1) * 128], rhs=h2T[:, k, ts_], start=(k == 0), stop=(k == 15)),
                             reads=[Bws[s], Bh2], writes=[pb[bU]])
                    P.op("act", lambda e, j=j, bG=bG: e.activation(out=st[j], in_=psum[:, bG, :], func=AF.Silu), reads=[pb[bG]], writes=[Bst[j]])
                    P.op("dve", lambda e, j=j, bU=bU: e.tensor_tensor(out=ab[j], in0=psum[:, bU, :], in1=st[j], op=ALU.mult), reads=[pb[bU], Bst[j]], writes=[Bab[j]])
                    P.dma("sp", lambda e, j=j, kch=kch, ts_=ts_: e.dma_start(out=act_s[kch][:, ts_], in_=ab[j]), reads=[Bab[j]], writes=[Bas], key="ab_st%d" % j)
        P.barrier()

    def phase_Dn():
        A = Arena()
        oT = A.take(16 * 1024, F32).rearrange("p (a b) -> p a b", b=1024)
        wd = [A.take(4 * 512, BF16).rearrange("p (a b) -> p a b", b=512) for _ in range(3)]
        ab = [A.take(4 * 1024, BF16).rearrange("p (a b) -> p a b", b=1024) for _ in range(3)]
        sq = A.take(16 * 512, BF16).rearrange("p (a b) -> p a b", b=512)
        rstd = A.take(512, F32)
        xt = [A.take(512, F32) for _ in range(2)]
        tt = [A.take(512, F32) for _ in range(2)]
        fo = A.take(16 * 512, F32).rearrange("p (a b) -> p a b", b=512)
        ot = [A.take(2048, F32) for _ in range(2)]
        pb = PB()
        BoT, Bsq, Brs, Bfo = Buf("oTD"), Buf("sqD"), Buf("rstdD"), Buf("fo")
        Bwd = [Buf("wd%d" % i) for i in range(3)]
        Bab = [Buf("abD%d" % i) for i in range(3)]
        Bxt = [Buf("xtD0"), Buf("xtD1")]
        Btt = [Buf("ttD0"), Buf("ttD1")]
        Bot = [Buf("ot0"), Buf("ot1")]
        Byy = Buf("yout")
        c3 = 0
        for ps_ in range(4):
            for kg in range(11):
                s = c3 % 3
                c3 += 1
                def ldwd(e, s=s, kg=kg, ps_=ps_):
                    return [e.dma_start(out=wd[s][:, k4, :], in_=w_down_d[(kg * 4 + k4) * 128:(kg * 4 + k4 + 1) * 128, ps_ * 512:(ps_ + 1) * 512]) for k4 in range(4)]
                P.dma("pool", ldwd, writes=[Bwd[s]], n=4)
                P.dma("sp", lambda e, s=s, kg=kg: e.dma_start(out=ab[s], in_=act_s[kg * 4:(kg + 1) * 4].rearrange("k p t -> p k t")), writes=[Bab[s]])
                for k4 in range(4):
                    for q in range(4):
                        for tb in range(2):
                            bank = q * 2 + tb
                            P.op("pe", lambda e, s=s, k4=k4, q=q, tb=tb, bank=bank, kg=kg: e.matmul(psum[:, bank, :], lhsT=wd[s][:, k4, q * 128:(q + 1) * 128], rhs=ab[s][:, k4, tb * 512:(tb + 1) * 512], start=(kg == 0 and k4 == 0), stop=(kg == 10 and k4 == 3)),
                                 reads=[Bwd[s], Bab[s]], writes=[pb[bank]])
            for q in range(4):
                for tb in range(2):
                    bank = q * 2 + tb
                    evac(oT[:, ps_ * 4 + q, tb * 512:(tb + 1) * 512], psum[:, bank, :], [pb[bank]], [BoT])
        ocnt = 0
        for tb in range(2):
            ts_ = slice(tb * 512, (tb + 1) * 512)
            for g4 in range(4):
                P.op("act", lambda e, g4=g4, ts_=ts_: e.activation(out=sq[:, g4 * 4:(g4 + 1) * 4, :], in_=oT[:, g4 * 4:(g4 + 1) * 4, ts_], func=AF.Square), reads=[BoT], writes=[Bsq])
            rms_stats(sq, 16, 512, tb, pb[tb], Bsq, rstd, Brs)
            for nch in range(16):
                j = nch % 2
                P.dma("sp", lambda e, j=j, nch=nch, ts_=ts_: e.dma_start(out=xt[j], in_=r1T_s[nch][:, ts_]), writes=[Bxt[j]])
                P.op("dve", lambda e, j=j, nch=nch, ts_=ts_: e.scalar_tensor_tensor(out=tt[j], in0=oT[:, nch, ts_], scalar=pvs("g4", nch), in1=rstd, op0=ALU.mult, op1=ALU.mult),
                     reads=[BoT, Brs, Bc], writes=[Btt[j]])
                P.op("dve", lambda e, j=j, nch=nch: e.tensor_tensor(out=fo[:, nch, :], in0=tt[j], in1=xt[j], op=ALU.add), reads=[Btt[j], Bxt[j]], writes=[Bfo])
            for t4 in range(4):
                o2 = ocnt % 2
                ocnt += 1
                for g4 in range(4):
                    bank = 2 + (t4 * 4 + g4) % 4
                    for q in range(4):
                        nch = g4 * 4 + q
                        P.op("pe", lambda e, bank=bank, q=q, nch=nch, t4=t4: e.transpose(out=psum[:, bank, q * 128:(q + 1) * 128], in_=fo[:, nch, t4 * 128:(t4 + 1) * 128], identity=identf[:]),
                             reads=[Bfo, Bc], writes=[pb[bank]])
                    evac(ot[o2][:, g4 * 512:(g4 + 1) * 512], psum[:, bank, :], [pb[bank]], [Bot[o2]])
                row = tb * 512 + t4 * 128
                P.dma("sp", lambda e, o2=o2, row=row: e.dma_start(out=y_d[row:row + 128, :], in_=ot[o2]), reads=[Bot[o2]], writes=[Byy], key="ot_st%d" % o2)
        P.op("sp", None, reads=[Byy])
        P.barrier()

    if "N1" in stages:
        phase_N1()
    if "P" in stages:
        phase_P()
    if "C" in stages:
        phase_C(0)
        phase_C(1)
    if "M" in stages:
        phase_M()
    if "G" in stages:
        phase_G()
    if "Dn" in stages:
        phase_Dn()

    if "Dn" not in stages and "F" in stages:
        By = Buf("y")
        Bfin = Buf("fin")
        fin = nc.alloc_sbuf_tensor("fin", [128, 16], F32)
        P.op("dve", lambda e: e.tensor_copy(out=fin[:], in_=rl1s[:]), reads=[Brl1], writes=[Bfin])
        P.dma("sp", lambda e: e.dma_start(out=y_d[0:128, 0:16], in_=fin[:]), reads=[Bfin], writes=[By], key="ystore")
        P.op("sp", None, reads=[By])
    if dbg is not None:
        srcs = {"KT": lambda: KT_s[0, 1], "KT2": lambda: KT_s[1, 0],
                "v": lambda: v_s.rearrange("c p t -> (c p) t"), "x1": lambda: x1_s.rearrange("c p t -> (c p) t"),
                "x2": lambda: x2_s.rearrange("c p t -> (c p) t"), "z": lambda: z_s.rearrange("c p t -> (c p) t"),
                "xT": lambda: xT_s.rearrange("c p t -> (c p) t"), "r1": lambda: r1T_s.rearrange("c p t -> (c p) t"),
                "sg": lambda: sg_s[0].rearrange("c p t -> (c p) t")}
        Bdbg = Buf("dbg")
        P.dma("sp", lambda e: e.dma_start(out=dbg_d, in_=srcs[dbg_name]()), writes=[Bdbg], key="dbgst")
        P.op("sp", None, reads=[Bdbg])
    P.emit()
    return nc


def core_inputs(inp, b, rev):
    C = consts()
    d = {}
    xb = inp["x"][b]
    d["x"] = np.ascontiguousarray(xb[::-1] if rev else xb)
    d["pvec"] = make_pvec(inp, rev)
    d["w_in"] = inp["w_in"][0]
    d["hy_proj"] = inp["hy_proj"][0]
    d["cf_proj"] = inp["cf_proj"][0]
    d["w_out"] = inp["w_out"][0]
    d["w_gu"] = inp["ffn_w_gu"][0]
    d["w_down"] = inp["ffn_w_down"][0]
    d["fw1"] = inp["hy_filt_w1"][0]
    fm = np.zeros((64, 68), np.float32)
    fm[:, 0:64] = inp["hy_filt_w2"][0]
    fm[:, 64] = inp["hy_filt_b1"][0]
    fm[:, 65] = inp["hy_filt_fr1"][0]
    fm[:, 66] = inp["hy_filt_b2"][0]
    fm[:, 67] = inp["hy_filt_fr2"][0]
    d["fmlp"] = fm
    w3 = inp["hy_filt_w3"][0]
    if rev:
        w3 = w3.reshape(64, 2, 2, 1024)[:, :, ::-1].reshape(64, 4096)
    d["fw3"] = np.ascontiguousarray(w3)
    for k in ["zT", "tabF_c", "tabF_s", "tabD_c", "tabD_s", "tabR", "ident_bf", "ident_f", "ones_bf", "tau_row", "adel_row"]:
        d[k] = C[k]
    return d


_NC = {}


def kernel(**inputs):
    inp = {k: np.asarray(v, dtype=np.float32) for k, v in inputs.items()}
    if "nc" not in _NC:
        _NC["nc"] = build()
    nc = _NC["nc"]
    in_maps = []
    for c in range(8):
        b, j = c // 2, c % 2
        in_maps.append(core_inputs(inp, b, j == 1))
    res = run_bass_kernel_spmd(nc, in_maps, core_ids=list(range(8)))
    out = np.zeros((4, 2048, 2048), np.float32)
    for c in range(8):
        b, j = c // 2, c % 2
        y = np.asarray(res.results[c]["y"], dtype=np.float32)
        if j == 0:
            out[b, 0:1024] = y
        else:
            out[b, 1024:2048] = y[::-1]
    return out
```

```python
import math
import numpy as np
import ml_dtypes
import concourse.bass as bass
import concourse.mybir as mybir
from concourse.bass_utils import run_bass_kernel_spmd

F32 = mybir.dt.float32
BF16 = mybir.dt.bfloat16
AF = mybir.ActivationFunctionType
ALU = mybir.AluOpType
AX = mybir.AxisListType

D = 2048
L = 2048
T1 = 1024
HWID = 1024
NIN = 9216
FF = 5632
EPS = 1e-6
NPV = 480
HY_MIN_DECAY = math.log(1e-2) / 1.5
HY_MAX_DECAY = math.log(1e-2) / 0.3


class Buf:
    __slots__ = ("name", "last_w", "readers")

    def __init__(self, name=""):
        self.name = name
        self.last_w = None
        self.readers = []


class Op:
    __slots__ = ("eng", "fn", "deps", "signal", "sig", "is_dma", "sem", "semval", "idx", "key")


class Prog:
    ENG_BLOCK = {"pe": "tensor", "act": "scalar", "dve": "vector", "pool": "gpsimd", "sp": "sync"}

    def __init__(self, nc):
        self.nc = nc
        self.ops = {k: [] for k in self.ENG_BLOCK}
        self.prog_sem = {k: nc.alloc_semaphore(name="prog_" + k) for k in self.ENG_BLOCK}
        self.dma_sems = {}
        self.nops = 0
        self.last_compute = {}
        self.phase_dmas = {}

    def _deps(self, o, reads, writes):
        deps = []

        def add(d):
            if d is None or d is o:
                return
            if (not d.is_dma) and (not o.is_dma) and d.eng == "pe" and o.eng == "pe":
                return
            for x in deps:
                if x is d:
                    return
            deps.append(d)

        for b in reads:
            add(b.last_w)
        for b in writes:
            add(b.last_w)
            for r in b.readers:
                add(r)
        return deps

    def _commit(self, o, reads, writes):
        for b in reads:
            if not o.is_dma:
                b.readers = [r for r in b.readers if r.is_dma or r.eng != o.eng]
            b.readers.append(o)
        for b in writes:
            b.last_w = o
            b.readers = []
        for d in o.deps:
            d.signal = True
        self.ops[o.eng].append(o)

    def op(self, eng, fn, reads=(), writes=()):
        o = Op()
        o.eng, o.fn, o.signal, o.sig, o.is_dma, o.sem, o.semval = eng, fn, False, 0, False, None, 0
        o.idx = self.nops
        self.nops += 1
        o.deps = self._deps(o, reads, writes)
        self._commit(o, reads, writes)
        if fn is not None:
            self.last_compute[eng] = o
        return o

    def dma(self, queue, fn, reads=(), writes=(), key=None, n=1):
        o = Op()
        o.eng, o.fn, o.signal, o.sig, o.is_dma = queue, fn, False, 0, True
        o.idx = self.nops
        self.nops += 1
        o.deps = self._deps(o, reads, writes)
        if key is None:
            key = writes[0].name
        if key not in self.dma_sems:
            self.dma_sems[key] = [self.nc.alloc_semaphore(name="d_" + key), 0]
        ent = self.dma_sems[key]
        ent[1] += 16 * n
        o.sem, o.semval, o.key = ent[0], ent[1], key
        self._commit(o, reads, writes)
        self.phase_dmas[key] = o
        return o

    def barrier(self):
        lasts = dict(self.last_compute)
        dmas = list(self.phase_dmas.values())
        for e in self.ENG_BLOCK:
            o = Op()
            o.eng, o.fn, o.signal, o.sig, o.is_dma, o.sem, o.semval = e, None, False, 0, False, None, 0
            o.idx = self.nops
            self.nops += 1
            o.deps = [v for k, v in lasts.items() if k != e] + dmas
            for d in o.deps:
                d.signal = True
            self.ops[e].append(o)
        self.phase_dmas = {}

    def emit(self):
        nc = self.nc
        for e, lst in self.ops.items():
            c = 0
            for o in lst:
                if o.is_dma:
                    continue
                if o.signal:
                    c += 1
                    o.sig = c
        with nc.Block() as block:
            for ename, bname in self.ENG_BLOCK.items():
                if not self.ops[ename]:
                    continue
                deco = getattr(block, bname)

                def body(eng, ename=ename):
                    waited = {}
                    mysem = self.prog_sem[ename]
                    for o in self.ops[ename]:
                        for d in o.deps:
                            if d.is_dma:
                                sem, val = d.sem, d.semval
                            else:
                                sem, val = self.prog_sem[d.eng], d.sig
                            k = id(sem)
                            if waited.get(k, 0) < val:
                                eng.wait_ge(sem, val)
                                waited[k] = val
                        if o.fn is None:
                            continue
                        r = o.fn(eng)
                        if o.is_dma:
                            if not isinstance(r, (list, tuple)):
                                r = [r]
                            for ins in r:
                                ins.then_inc(o.sem, 16)
                        elif o.signal:
                            r.then_inc(mysem, 1)

                deco(body)


_CONST = {}


def _lhsT_layout(M):
    A = M.reshape(16, 128, 16, 128)
    return np.ascontiguousarray(A.transpose(2, 1, 0, 3))


def consts():
    if _CONST:
        return _CONST
    bf = ml_dtypes.bfloat16
    n = np.arange(2048, dtype=np.float64)
    phi = math.pi / 4096.0
    th = phi * np.outer(n, 2 * n + 1)
    _CONST["tabF_c"] = _lhsT_layout(np.cos(th)).astype(bf)
    _CONST["tabF_s"] = _lhsT_layout(-np.sin(th)).astype(bf)
    ps = (phi / 2) * np.outer(2 * n + 1, 2 * n + 1)
    Ct = np.cos(ps)
    St = -np.sin(ps)
    _CONST["tabD_c"] = _lhsT_layout(Ct).astype(bf)
    _CONST["tabD_s"] = _lhsT_layout(St).astype(bf)
    _CONST["tabR"] = np.ascontiguousarray(np.stack([Ct, St], 0)).astype(bf)
    _CONST["ident_bf"] = np.eye(128).astype(bf)
    _CONST["ident_f"] = np.eye(128).astype(np.float32)
    _CONST["ones_bf"] = np.ones((128, 128)).astype(bf)
    _CONST["tau_row"] = n.astype(np.float32)[None, :]
    deltas = np.linspace(HY_MIN_DECAY, HY_MAX_DECAY, HWID, dtype=np.float32)
    _CONST["adel_row"] = np.abs(deltas)[None, :].astype(np.float32)
    f32 = np.float32
    t = np.linspace(0.0, 1.0, L, dtype=f32)[:, None]
    w = (2.0 * math.pi * np.arange(L, dtype=f32)[:, None] / L).astype(f32)
    f = np.linspace(1e-4, 15, 16, dtype=f32)[None, :]
    z = np.concatenate([t, np.cos(f * w), -np.sin(f * w)], axis=-1).astype(f32)
    _CONST["zT"] = np.ascontiguousarray(z.T)
    negt = -(np.arange(2048, dtype=np.float64) / (L - 1))
    _CONST["negt"] = negt.reshape(16, 128).T.astype(np.float32)
    nd = -(np.abs(deltas).astype(np.float64) / (L - 1))
    _CONST["negdel"] = nd.reshape(8, 128).T.astype(np.float32)
    return _CONST


def pm(v, nch):
    return np.ascontiguousarray(np.asarray(v).reshape(nch, 128).T)


def make_pvec(inp, rev):
    C = consts()
    pv = np.zeros((128, NPV), np.float32)
    pv[:, 0:16] = pm(inp["mix_pre_g"][0], 16)
    pv[:, 16:32] = pm(inp["mix_post_g"][0], 16)
    pv[:, 32:48] = pm(inp["ffn_pre_g"][0], 16)
    pv[:, 48:64] = pm(inp["ffn_post_g"][0], 16)
    hcw = inp["hy_conv_w"][0]
    if rev:
        hcw = hcw[::-1]
    pv[:, 64:136] = hcw.T.reshape(24, 128, 3).transpose(1, 0, 2).reshape(128, 72)
    pv[:, 136:160] = pm(inp["hy_conv_b"][0], 24)
    cdw = inp["cf_dw_w"][0]
    if rev:
        cdw = cdw[::-1]
    pv[:, 160:408] = cdw.T.reshape(8, 128, 31).transpose(1, 0, 2).reshape(128, 248)
    pv[:, 408:416] = pm(inp["cf_dw_b"][0], 8)
    pv[:, 416:424] = pm(inp["cf_ln_g"][0], 8)
    pv[:, 424:432] = pm(inp["cf_ln_b"][0], 8)
    hb = inp["hy_bias"][0]
    pv[:, 432:448] = hb.reshape(2, 8, 128).transpose(2, 0, 1).reshape(128, 16)
    pv[:, 448] = 0.0 if rev else 1.0
    pv[:, 449] = 1.0 if rev else 0.0
    pv[:, 450:466] = C["negt"]
    pv[:, 466:474] = C["negdel"]
    return pv


PV = dict(g1=0, g2=16, g3=32, g4=48, hcw=64, hcb=136, cdw=160, cdb=408, lng=416, lnb=424, hyb=432,
          flags=448, negt=450, negdel=466)


def build(stages=("F", "N1", "P", "C", "M", "G", "Dn"), dbg=None):
    nc = bass.Bass("TRN2", target_bir_lowering=False)
    P = Prog(nc)

    def din(n, s, dt=F32):
        return nc.dram_tensor(n, list(s), dt, kind="ExternalInput").ap()

    def dscr(n, s, dt=F32):
        return nc.dram_tensor(n, list(s), dt).ap()

    x_d = din("x", [2048, 2048])
    pvec_d = din("pvec", [128, NPV])
    w_in_d = din("w_in", [D, NIN])
    hy_proj_d = din("hy_proj", [HWID, D])
    cf_proj_d = din("cf_proj", [HWID, D])
    w_out_d = din("w_out", [D, D])
    w_gu_d = din("w_gu", [D, 2 * FF])
    w_down_d = din("w_down", [FF, D])
    w1_d = din("fw1", [33, 64])
    fm_d = din("fmlp", [64, 68])
    w3_d = din("fw3", [64, 4096])
    zT_d = din("zT", [33, 2048])
    tabF_d = [din("tabF_c", [16, 128, 16, 128], BF16), din("tabF_s", [16, 128, 16, 128], BF16)]
    tabD_d = [din("tabD_c", [16, 128, 16, 128], BF16), din("tabD_s", [16, 128, 16, 128], BF16)]
    tabR_d = din("tabR", [2, 2048, 2048], BF16)
    identb_d = din("ident_bf", [128, 128], BF16)
    identf_d = din("ident_f", [128, 128])
    onesb_d = din("ones_bf", [128, 128], BF16)
    tau_d = din("tau_row", [1, 2048])
    adel_d = din("adel_row", [1, 1024])
    y_d = nc.dram_tensor("y", [T1, D], F32, kind="ExternalOutput").ap()
    dbg_d = None
    dbg_name = None
    if dbg is not None:
        dbg_name, dshape = dbg
        dbg_d = nc.dram_tensor("dbg", list(dshape), F32, kind="ExternalOutput").ap()

    KT_s = dscr("KT_s", [2, 2, 2048, 1024])
    xT_s = dscr("xT_s", [16, 128, T1])
    v_s = dscr("v_s", [8, 128, 2048])
    x1_s = dscr("x1_s", [8, 128, 2048])
    x2_s = dscr("x2_s", [8, 128, T1])
    yaT_s = dscr("yaT_s", [8, 128, T1], BF16)
    ybT_s = dscr("ybT_s", [8, 128, T1], BF16)
    sg_s = dscr("sg_s", [2, 16, 128, T1])
    r1T_s = dscr("r1T_s", [16, 128, T1])
    act_s = dscr("act_s", [44, 128, T1], BF16)

    pv = nc.alloc_sbuf_tensor("pv", [128, NPV], F32)
    identb = nc.alloc_sbuf_tensor("identb", [128, 128], BF16)
    identf = nc.alloc_sbuf_tensor("identf", [128, 128], F32)
    onesb = nc.alloc_sbuf_tensor("onesb", [128, 128], BF16)
    rl1s = nc.alloc_sbuf_tensor("rl1s", [128, 16], F32)
    epsc = nc.alloc_sbuf_tensor("epsc", [128, 1], F32)
    zeroc = nc.alloc_sbuf_tensor("zeroc", [128, 1], F32)
    ARENA_BYTES = 200 * 1024
    arena = nc.alloc_sbuf_tensor("arena", [128, ARENA_BYTES // 2], BF16)
    psum = nc.alloc_psum_tensor("psum", [128, 8, 512], F32)
    Bc = Buf("const")
    Brl1 = Buf("rl1s")

    class Arena:
        def __init__(self):
            self.off = 0

        def take(self, nelem, dt):
            sz = 2 if dt == BF16 else 4
            nb = (nelem * sz + 63) // 64 * 64
            assert self.off + nb <= ARENA_BYTES, (self.off, nb)
            v = arena[:, self.off // 2:(self.off + nb) // 2]
            self.off += nb
            if dt == F32:
                v = v.bitcast(F32)
            return v[:, 0:nelem]

    def pvs(name, i=0, n=1):
        o = PV[name] + i
        return pv[:, o:o + n]

    def PB():
        return [Buf("pb%d" % i) for i in range(8)]

    P.dma("sp", lambda e: e.dma_start(out=pv[:], in_=pvec_d), writes=[Bc], key="c0")
    P.dma("sp", lambda e: e.dma_start(out=identb[:], in_=identb_d), writes=[Bc], key="c1")
    P.dma("sp", lambda e: e.dma_start(out=identf[:], in_=identf_d), writes=[Bc], key="c2")
    P.dma("sp", lambda e: e.dma_start(out=onesb[:], in_=onesb_d), writes=[Bc], key="c3")
    P.op("dve", lambda e: e.memset(epsc[:], EPS), writes=[Bc])
    P.op("dve", lambda e: e.memset(zeroc[:], 0.0), writes=[Bc])
    P.barrier()
    Bc = Buf("const")

    def phase_F():
        A = Arena()
        zT = A.take(2048, F32)
        h1T = A.take(2048, F32)
        h2T = A.take(2048, BF16)
        w3f = A.take(4096, F32)
        w3 = A.take(4096, BF16)
        w1 = A.take(64, F32)
        fm = A.take(68, F32)
        cc12 = A.take(2, F32)
        tau_bc = A.take(2048, F32)
        adel_bc = A.take(1024, F32)
        arg = [A.take(512, F32) for _ in range(2)]
        l1p = A.take(128, F32)
        l1q = A.take(32, F32)
        l1t = A.take(16, F32)
        junk = A.take(512, F32)
        wincm = A.take(2048, F32)
        wintm = [A.take(1024, F32) for _ in range(2)]
        tmpf = [A.take(512, F32) for _ in range(2)]
        tmpb = [A.take(512, F32) for _ in range(2)]
        tmpa = [A.take(512, F32) for _ in range(2)]
        ke = A.take(16 * 1024, BF16).rearrange("p (a b) -> p a b", b=1024)
        kd = A.take(16 * 1024, BF16).rearrange("p (a b) -> p a b", b=1024)
        tsl = [A.take(16 * 128, BF16).rearrange("p (a b) -> p a b", b=128) for _ in range(4)]
        ksb = [A.take(1024, F32) for _ in range(2)]
        pb = PB()
        Bz, Bw = Buf("Fz"), Buf("Fw")
        Bh1, Bh2 = Buf("h1"), Buf("h2")
        Barg = [Buf("arg0"), Buf("arg1")]
        Bjk = Buf("jk")
        P.dma("sp", lambda e: e.dma_start(out=zT[0:33, :], in_=zT_d), writes=[Bz], key="f0")
        P.dma("sp", lambda e: e.dma_start(out=w1[0:33, :], in_=w1_d), writes=[Bw], key="f1")
        P.dma("sp", lambda e: e.dma_start(out=fm[0:64, :], in_=fm_d), writes=[Bw], key="f2")
        Bw3f = Buf("w3f")
        P.dma("sp", lambda e: e.dma_start(out=w3f[0:64, :], in_=w3_d), writes=[Bw3f], key="f3")
        P.op("dve", lambda e: e.memset(w3, 0.0), writes=[Bw])
        P.op("dve", lambda e: e.tensor_copy(out=w3[0:64, :], in_=w3f[0:64, :]), reads=[Bw3f, Bw], writes=[Bw])
        P.op("dve", lambda e: e.memset(h2T, 0.0), writes=[Bh2])
        P.dma("sp", lambda e: e.dma_start(out=tau_bc, in_=tau_d.partition_broadcast(128)), writes=[Bw], key="f4")
        P.dma("sp", lambda e: e.dma_start(out=adel_bc, in_=adel_d.partition_broadcast(128)), writes=[Bw], key="f5")
        Bcc = Buf("cc")
        P.op("dve", lambda e: e.tensor_tensor(out=cc12[0:64, 0:1], in0=fm[0:64, 64:65], in1=fm[0:64, 65:66], op=ALU.mult),
             reads=[Bw], writes=[Bcc])
        P.op("dve", lambda e: e.tensor_tensor(out=cc12[0:64, 1:2], in0=fm[0:64, 66:67], in1=fm[0:64, 67:68], op=ALU.mult),
             reads=[Bw, Bcc], writes=[Bcc])
        cnt = 0
        for layer in range(2):
            src = zT if layer == 0 else h1T
            dst = h1T if layer == 0 else h2T
            Bs = Bz if layer == 0 else Bh1
            Bd = Bh1 if layer == 0 else Bh2
            kk = 33 if layer == 0 else 64
            wl = w1[0:33, 0:64] if layer == 0 else fm[0:64, 0:64]
            frc = fm[0:64, 65:66] if layer == 0 else fm[0:64, 67:68]
            cb_ = cc12[0:64, layer:layer + 1]
            for tb in range(4):
                bk = cnt % 2
                cnt += 1
                ts = slice(tb * 512, (tb + 1) * 512)
                P.op("pe", lambda e, bk=bk, ts=ts, src=src, kk=kk, wl=wl: e.matmul(psum[0:64, bk, :], lhsT=wl, rhs=src[0:kk, ts], start=True, stop=True),
                     reads=[Bs, Bw], writes=[pb[bk]])
                P.op("act", lambda e, bk=bk, frc=frc, cb_=cb_: e.activation(out=arg[bk][0:64, :], in_=psum[0:64, bk, :], func=AF.Identity, scale=frc, bias=cb_),
                     reads=[pb[bk], Bw, Bcc], writes=[Barg[bk]])
                for _ in range(2):
                    for (thr, cmp_, per) in ((math.pi, ALU.is_gt, -2 * math.pi), (-math.pi, ALU.is_lt, 2 * math.pi)):
                        P.op("dve", lambda e, bk=bk, thr=thr, cmp_=cmp_, per=per: e.tensor_scalar(out=junk[0:64, :], in0=arg[bk][0:64, :], scalar1=thr, scalar2=per, op0=cmp_, op1=ALU.mult),
                             reads=[Barg[bk]], writes=[Bjk])
                        P.op("dve", lambda e, bk=bk: e.tensor_tensor(out=arg[bk][0:64, :], in0=arg[bk][0:64, :], in1=junk[0:64, :], op=ALU.add),
                             reads=[Barg[bk], Bjk], writes=[Barg[bk]])
                P.op("act", lambda e, bk=bk, dst=dst, ts=ts: e.activation(out=dst[0:64, ts], in_=arg[bk][0:64, :], func=AF.Sin),
                     reads=[Barg[bk]], writes=[Bd])
        Bwin, Bl1p = Buf("wincm"), Buf("l1p")
        Btmp = [Buf("tmpa0"), Buf("tmpa1")]
        Bjunk = Buf("junk")
        cnt = 0
        for cc in range(8):
            P.op("act", lambda e, cc=cc: e.activation(out=wincm, in_=tau_bc, func=AF.Exp, scale=pvs("negdel", cc)),
                 reads=[Bw, Bc], writes=[Bwin])
            for o in range(2):
                for dr in range(2):
                    q = o * 16 + dr * 8 + cc
                    for tb in range(4):
                        bk = 2 + cnt % 2
                        tm = cnt % 2
                        cnt += 1
                        ts = slice(tb * 512, (tb + 1) * 512)
                        P.op("pe", lambda e, bk=bk, q=q, ts=ts: e.matmul(psum[:, bk, :], lhsT=w3[:, q * 128:(q + 1) * 128], rhs=h2T[:, ts], start=True, stop=True),
                             reads=[Bw, Bh2], writes=[pb[bk]])
                        P.op("dve", lambda e, bk=bk, tm=tm, ts=ts: e.tensor_tensor(out=tmpa[tm], in0=psum[:, bk, :], in1=wincm[:, ts], op=ALU.mult),
                             reads=[pb[bk], Bwin], writes=[Btmp[tm]])
                        if tb == 0:
                            P.op("dve", lambda e, tm=tm, dr=dr: e.tensor_scalar(out=tmpa[tm][:, 0:1], in0=tmpa[tm][:, 0:1], scalar1=pvs("flags", dr), scalar2=None, op0=ALU.mult),
                                 reads=[Btmp[tm], Bc], writes=[Btmp[tm]])
                        P.op("act", lambda e, tm=tm, q=q, tb=tb: e.activation(out=junk, in_=tmpa[tm], func=AF.Abs, accum_out=l1p[:, q * 4 + tb:q * 4 + tb + 1]),
                             reads=[Btmp[tm]], writes=[Bjunk, Bl1p])
        P.op("dve", lambda e: e.tensor_reduce(out=l1q, in_=l1p.rearrange("p (a b) -> p a b", b=4), axis=AX.X, op=ALU.add),
             reads=[Bl1p], writes=[Bl1p])
        for o in range(2):
            P.op("dve", lambda e, o=o: e.tensor_tensor(out=l1t[:, o * 8:(o + 1) * 8], in0=l1q[:, o * 16:o * 16 + 8], in1=l1q[:, o * 16 + 8:o * 16 + 16], op=ALU.add),
                 reads=[Bl1p], writes=[Bl1p])
        P.op("dve", lambda e: e.reciprocal(out=l1t, in_=l1t), reads=[Bl1p], writes=[Bl1p])
        P.op("dve", lambda e: e.tensor_scalar(out=rl1s[:], in0=l1t, scalar1=1.0 / 2048.0, scalar2=None, op0=ALU.mult),
             reads=[Bl1p], writes=[Brl1])
        Bwt = [Buf("wintm0"), Buf("wintm1")]
        Btf = [Buf("tmpf0"), Buf("tmpf1")]
        Btb = [Buf("tmpb0"), Buf("tmpb1")]
        Bke, Bkd = Buf("ke"), Buf("kd")
        Bts = [Buf("tsl%d" % i) for i in range(4)]
        Bks = [Buf("ksb0"), Buf("ksb1")]
        BKT = Buf("KT")
        cnt = 0
        tcnt = 0
        kcnt = 0
        for o in range(2):
            for tc in range(16):
                wi = tc % 2
                P.op("act", lambda e, wi=wi, tc=tc: e.activation(out=wintm[wi], in_=adel_bc, func=AF.Exp, scale=pvs("negt", tc)),
                     reads=[Bw, Bc], writes=[Bwt[wi]])
                for half in range(2):
                    colf = (o * 2 + 0) * 1024 + half * 512
                    colb = (o * 2 + 1) * 1024 + half * 512
                    i2 = cnt % 2
                    cnt += 1
                    bF, bB = 4 + 2 * i2, 5 + 2 * i2
                    tsl_ = slice(tc * 128, (tc + 1) * 128)
                    hs = slice(half * 512, (half + 1) * 512)
                    P.op("pe", lambda e, bF=bF, tsl_=tsl_, colf=colf: e.matmul(psum[:, bF, :], lhsT=h2T[:, tsl_], rhs=w3[:, colf:colf + 512], start=True, stop=True),
                         reads=[Bw, Bh2], writes=[pb[bF]])
                    P.op("pe", lambda e, bB=bB, tsl_=tsl_, colb=colb: e.matmul(psum[:, bB, :], lhsT=h2T[:, tsl_], rhs=w3[:, colb:colb + 512], start=True, stop=True),
                         reads=[Bw, Bh2], writes=[pb[bB]])
                    P.op("dve", lambda e, bF=bF, i2=i2, wi=wi, hs=hs: e.tensor_tensor(out=tmpf[i2], in0=psum[:, bF, :], in1=wintm[wi][:, hs], op=ALU.mult),
                         reads=[pb[bF], Bwt[wi]], writes=[Btf[i2]])
                    P.op("dve", lambda e, bB=bB, i2=i2, wi=wi, hs=hs: e.tensor_tensor(out=tmpb[i2], in0=psum[:, bB, :], in1=wintm[wi][:, hs], op=ALU.mult),
                         reads=[pb[bB], Bwt[wi]], writes=[Btb[i2]])
                    if tc == 0:
                        P.op("dve", lambda e, i2=i2: e.tensor_scalar(out=tmpf[i2][0:1, :], in0=tmpf[i2][0:1, :], scalar1=pv[0:1, PV["flags"]:PV["flags"] + 1], scalar2=None, op0=ALU.mult),
                             reads=[Btf[i2], Bc], writes=[Btf[i2]])
                        P.op("dve", lambda e, i2=i2: e.tensor_scalar(out=tmpb[i2][0:1, :], in0=tmpb[i2][0:1, :], scalar1=pv[0:1, PV["flags"] + 1:PV["flags"] + 2], scalar2=None, op0=ALU.mult),
                             reads=[Btb[i2], Bc], writes=[Btb[i2]])
                    P.op("pool", lambda e, i2=i2, tc=tc, hs=hs: e.tensor_tensor(out=ke[:, tc, hs], in0=tmpf[i2], in1=tmpb[i2], op=ALU.add),
                         reads=[Btf[i2], Btb[i2]], writes=[Bke])
                    P.op("pool", lambda e, i2=i2, tc=tc, hs=hs: e.tensor_tensor(out=kd[:, tc, hs], in0=tmpf[i2], in1=tmpb[i2], op=ALU.subtract),
                         reads=[Btf[i2], Btb[i2]], writes=[Bkd])
            for part in range(2):
                src, Bsrc = (ke, Bke) if part == 0 else (kd, Bkd)
                for fc in range(16):
                    s3 = tcnt % 4
                    tcnt += 1
                    P.dma("sp", lambda e, s3=s3, part=part, fc=fc: e.dma_start(out=tsl[s3], in_=tabF_d[part][fc]), writes=[Bts[s3]])
                    ks = kcnt % 2
                    kcnt += 1
                    for cb in range(2):
                        bk = (kcnt * 2 + cb) % 4
                        for tc in range(16):
                            P.op("pe", lambda e, bk=bk, s3=s3, tc=tc, cb=cb, src=src: e.matmul(psum[:, bk, :], lhsT=tsl[s3][:, tc, :], rhs=src[:, tc, cb * 512:(cb + 1) * 512], start=(tc == 0), stop=(tc == 15)),
                                 reads=[Bts[s3], Bsrc], writes=[pb[bk]])
                        eng = "act" if cb == 0 else "dve"
                        if eng == "act":
                            P.op("act", lambda e, bk=bk, ks=ks, cb=cb: e.copy(out=ksb[ks][:, cb * 512:(cb + 1) * 512], in_=psum[:, bk, :]),
                                 reads=[pb[bk]], writes=[Bks[ks]])
                        else:
                            P.op("dve", lambda e, bk=bk, ks=ks, cb=cb: e.tensor_copy(out=ksb[ks][:, cb * 512:(cb + 1) * 512], in_=psum[:, bk, :]),
                                 reads=[pb[bk]], writes=[Bks[ks]])
                    P.dma("sp", lambda e, ks=ks, o=o, part=part, fc=fc: e.dma_start(out=KT_s[o, part, fc * 128:(fc + 1) * 128, :], in_=ksb[ks]),
                          reads=[Bks[ks]], writes=[BKT], key="ksb_st%d" % ks)
        P.barrier()

    if "F" in stages:
        phase_F()

    TAIL_OFF = 168 * 1024
    uT = arena[:, TAIL_OFF // 2:(TAIL_OFF + 32768) // 2].rearrange("p (a b) -> p a b", b=1024)
    BuT = [Buf("uT0"), Buf("uT1")]
    z_s = dscr("z_s", [8, 128, T1])
    cpy_cnt = [0]

    def evac(out, in_, Bin, Bout):
        cpy_cnt[0] += 1
        if cpy_cnt[0] % 2:
            P.op("act", lambda e: e.copy(out=out, in_=in_), reads=Bin, writes=Bout)
        else:
            P.op("dve", lambda e: e.tensor_copy(out=out, in_=in_), reads=Bin, writes=Bout)

    def rms_stats(sq3, nch, ncol, bank, pbk, Bsq, rstd, Brstd):
        for c in range(nch):
            P.op("pe", lambda e, c=c: e.matmul(psum[:, bank, 0:ncol], lhsT=onesb[:], rhs=sq3[:, c, :], start=(c == 0), stop=(c == nch - 1)),
                 reads=[Bsq, Bc], writes=[pbk])
        P.op("act", lambda e: e.activation(out=rstd, in_=psum[:, bank, 0:ncol], func=AF.Sqrt, scale=1.0 / (nch * 128), bias=epsc[:, 0:1]),
             reads=[pbk, Bc], writes=[Brstd])
        P.op("dve", lambda e: e.reciprocal(out=rstd, in_=rstd), reads=[Brstd], writes=[Brstd])

    def phase_N1():
        A = Arena()
        hT = A.take(16 * 2048, BF16).rearrange("p (a b) -> p a b", b=2048)
        xt = [A.take(2048, F32) for _ in range(2)]
        xTb = A.take(16 * 512, F32).rearrange("p (a b) -> p a b", b=512)
        sq = A.take(16 * 512, BF16).rearrange("p (a b) -> p a b", b=512)
        rstd = A.take(512, F32)
        pb = PB()
        Bxt = [Buf("xt0"), Buf("xt1")]
        BxTb, Bsq, Brs, BhT, BxTs = Buf("xTb"), Buf("sq"), Buf("rstd"), Buf("hT"), Buf("xTs")
        bcnt = 0
        for tb in range(4):
            for i in range(4):
                tile = tb * 4 + i
                s = tile % 2
                P.dma("sp", lambda e, s=s, tile=tile: e.dma_start(out=xt[s], in_=x_d[tile * 128:(tile + 1) * 128, :]), writes=[Bxt[s]])
                for g4 in range(4):
                    bank = bcnt % 4
                    bcnt += 1
                    for q in range(4):
                        dc = g4 * 4 + q
                        P.op("pe", lambda e, s=s, bank=bank, q=q, dc=dc: e.transpose(out=psum[:, bank, q * 128:(q + 1) * 128], in_=xt[s][:, dc * 128:(dc + 1) * 128], identity=identf[:]),
                             reads=[Bxt[s], Bc], writes=[pb[bank]])
                    evac(xTb[:, g4 * 4:(g4 + 1) * 4, i * 128:(i + 1) * 128], psum[:, bank, :].rearrange("p (a b) -> p a b", b=128), [pb[bank]], [BxTb])
            for g4 in range(4):
                P.op("act", lambda e, g4=g4: e.activation(out=sq[:, g4 * 4:(g4 + 1) * 4, :], in_=xTb[:, g4 * 4:(g4 + 1) * 4, :], func=AF.Square),
                     reads=[BxTb], writes=[Bsq])
            rms_stats(sq, 16, 512, 4, pb[4], Bsq, rstd, Brs)
            for dc in range(16):
                P.op("dve", lambda e, dc=dc, tb=tb: e.scalar_tensor_tensor(out=hT[:, dc, tb * 512:(tb + 1) * 512], in0=xTb[:, dc, :], scalar=pvs("g1", dc), in1=rstd, op0=ALU.mult, op1=ALU.mult),
                     reads=[BxTb, Brs, Bc], writes=[BhT])
            if tb < 2:
                P.dma("sp", lambda e, tb=tb: e.dma_start(out=xT_s[:, :, tb * 512:(tb + 1) * 512].rearrange("c p t -> p c t"), in_=xTb),
                      reads=[BxTb], writes=[BxTs], key="xTb_st")
        P.barrier()

    def phase_P():
        A = Arena()
        hT = A.take(16 * 2048, BF16).rearrange("p (a b) -> p a b", b=2048)
        wsl = [A.take(16 * 512, BF16).rearrange("p (a b) -> p a b", b=512) for _ in range(2)]
        sgt = [A.take(512, F32) for _ in range(2)]
        mark = A.off
        p_sb = [A.take(2050, F32) for _ in range(2)]
        acc = [A.take(2048, F32) for _ in range(2)]
        accb = A.take(2048, BF16)
        assert A.off <= TAIL_OFF, A.off
        A.off = mark
        upad = A.take(15 + 1056 + 1, BF16)
        dg = A.take(31 * 128, BF16).rearrange("p (a b) -> p a b", b=128)
        ucv = A.take(8 * 1024, F32).rearrange("p (a b) -> p a b", b=1024)
        ub = A.take(8 * 512, BF16).rearrange("p (a b) -> p a b", b=512)
        usq = A.take(8 * 512, BF16).rearrange("p (a b) -> p a b", b=512)
        mu = A.take(512, F32)
        m2 = A.take(512, F32)
        rstd = A.take(512, F32)
        tq = sgt
        ybt = [A.take(512, BF16) for _ in range(2)]
        assert A.off <= TAIL_OFF, A.off
        pb = PB()
        BhT = Buf("hT")
        Bws = [Buf("wsl0"), Buf("wsl1")]
        Bp = [Buf("p_sb0"), Buf("p_sb1")]
        Bacc = [Buf("acc0"), Buf("acc1")]
        Baccb, Bup, Bdg = Buf("accb"), Buf("upad"), Buf("dg")
        Bsg = [Buf("sgt0"), Buf("sgt1")]
        Bucv, Bub, Busq, Bmu, Bm2, Brs = Buf("ucv"), Buf("ub"), Buf("usq"), Buf("mu"), Buf("m2"), Buf("rstdP")
        Btq = Bsg
        Bybt = [Buf("ybt0"), Buf("ybt1")]
        Bvs, Bx1s, Bx2s, Bybs, Bsgs = Buf("v_s"), Buf("x1_s"), Buf("x2_s"), Buf("ybT_s"), Buf("sg_s")
        for i in range(2):
            P.op("dve", lambda e, i=i: e.memset(p_sb[i], 0.0), writes=[Bp[i]])
        wcnt = [0]
        bcnt = [0]

        def loadw(cols):
            s = wcnt[0] % 2
            wcnt[0] += 1

            def fn(e, s=s, cols=cols):
                r = []
                o = 0
                for (c0, n) in cols:
                    r.append(e.dma_start(out=wsl[s][:, :, o:o + n], in_=w_in_d[:, c0:c0 + n].rearrange("(k p) n -> p k n", p=128)))
                    o += n
                return r
            P.dma("pool", fn, writes=[Bws[s]], n=len(cols))
            return s

        def mm16(bank, n, s, col, t0):
            for k in range(16):
                P.op("pe", lambda e, k=k: e.matmul(psum[:, bank, 0:n], lhsT=wsl[s][:, k, col:col + 128], rhs=hT[:, k, t0:t0 + n], start=(k == 0), stop=(k == 15)),
                     reads=[Bws[s], BhT], writes=[pb[bank]])

        def nextbank(lo=0, n=4):
            b = lo + bcnt[0] % n
            bcnt[0] += 1
            return b

        for blk in range(6):
            s = loadw([(blk * 512, 512)])
            for q in range(4):
                ch = blk * 4 + q
                kind = ch // 8
                i2 = ch % 2
                groups = [(0, 512), (512, 512), (1024, 512), (1536, 512)] if kind < 2 else [(0, 512), (512, 512), (1024, 32)]
                for (t0, n) in groups:
                    bank = nextbank()
                    mm16(bank, n, s, q * 128, t0)
                    evac(p_sb[i2][:, 1 + t0:1 + t0 + n], psum[:, bank, 0:n], [pb[bank]], [Bp[i2]])
                nt = 2048 if kind < 2 else 1024
                a = acc[i2][:, 0:nt]
                P.op("act", lambda e, a=a, i2=i2, nt=nt, ch=ch: e.activation(out=a, in_=p_sb[i2][:, 1:1 + nt], func=AF.Identity, scale=pvs("hcw", ch * 3 + 1), bias=pvs("hcb", ch)),
                     reads=[Bp[i2], Bc], writes=[Bacc[i2]])
                P.op("dve", lambda e, a=a, i2=i2, nt=nt, ch=ch: e.scalar_tensor_tensor(out=a, in0=p_sb[i2][:, 0:nt], scalar=pvs("hcw", ch * 3 + 0), in1=a, op0=ALU.mult, op1=ALU.add),
                     reads=[Bp[i2], Bc, Bacc[i2]], writes=[Bacc[i2]])
                P.op("dve", lambda e, a=a, i2=i2, nt=nt, ch=ch: e.scalar_tensor_tensor(out=a, in0=p_sb[i2][:, 2:2 + nt], scalar=pvs("hcw", ch * 3 + 2), in1=a, op0=ALU.mult, op1=ALU.add),
                     reads=[Bp[i2], Bc, Bacc[i2]], writes=[Bacc[i2]])
                dst = (v_s, x1_s, x2_s)[kind][ch % 8]
                Bd = (Bvs, Bx1s, Bx2s)[kind]
                P.dma("sp", lambda e, a=a, dst=dst: e.dma_start(out=dst, in_=a), reads=[Bacc[i2]], writes=[Bd], key="acc_st%d" % i2)
                if kind == 0:
                    P.op("pool", lambda e, i2=i2: e.tensor_copy(out=accb, in_=acc[i2]), reads=[Bacc[i2]], writes=[Baccb])
                    psT = psum[:, 4:6, :].rearrange("p a b -> p (a b)").bitcast(BF16).rearrange("p (a b) -> p a b", b=128)
                    for tc in range(16):
                        P.op("pe", lambda e, tc=tc: e.transpose(out=psT[:, tc, :], in_=accb[:, tc * 128:(tc + 1) * 128], identity=identb[:]),
                             reads=[Baccb, Bc], writes=[pb[4], pb[5]])
                    P.op("act", lambda e, ch=ch: e.copy(out=uT[:, :, ch * 128:(ch + 1) * 128], in_=psT), reads=[pb[4], pb[5]], writes=[BuT[ch // 4]])
        P.barrier()
        P.op("dve", lambda e: e.memset(upad, 0.0), writes=[Bup])
        for pg in range(4):
            s = loadw([(3072 + pg * 256, 256), (4096 + pg * 256, 256)])
            for i in range(2):
                cc = pg * 2 + i
                for (t0, n) in [(0, 512), (512, 512), (1024, 32)]:
                    bA = nextbank()
                    mm16(bA, n, s, i * 128, t0)
                    bB = nextbank()
                    mm16(bB, n, s, 256 + i * 128, t0)
                    g2 = bB % 2
                    P.op("act", lambda e, bB=bB, n=n, g2=g2: e.activation(out=sgt[g2][:, 0:n], in_=psum[:, bB, 0:n], func=AF.Sigmoid),
                         reads=[pb[bB]], writes=[Bsg[g2]])
                    P.op("dve", lambda e, bA=bA, n=n, g2=g2, t0=t0: e.tensor_tensor(out=upad[:, 15 + t0:15 + t0 + n], in0=psum[:, bA, 0:n], in1=sgt[g2][:, 0:n], op=ALU.mult),
                         reads=[pb[bA], Bsg[g2]], writes=[Bup])
                for k in range(31):
                    P.op("act", lambda e, k=k, cc=cc: e.activation(out=dg[:, k, :], in_=identb[:], func=AF.Identity, scale=pvs("cdw", cc * 31 + k), bias=zeroc[:, 0:1]),
                         reads=[Bc], writes=[Bdg])
                for tb in range(2):
                    bank = nextbank()
                    for k in range(31):
                        P.op("pe", lambda e, k=k, tb=tb, bank=bank: e.matmul(psum[:, bank, :], lhsT=dg[:, k, :], rhs=upad[:, tb * 512 + k:tb * 512 + k + 512], start=(k == 0), stop=(k == 30)),
                             reads=[Bdg, Bup], writes=[pb[bank]])
                    P.op("act", lambda e, tb=tb, bank=bank, cc=cc: e.activation(out=ucv[:, cc, tb * 512:(tb + 1) * 512], in_=psum[:, bank, :], func=AF.Identity, scale=1.0, bias=pvs("cdb", cc)),
                         reads=[pb[bank], Bc], writes=[Bucv])
        for tb in range(2):
            tsl_ = slice(tb * 512, (tb + 1) * 512)
            P.op("act", lambda e, tsl_=tsl_: e.activation(out=usq, in_=ucv[:, :, tsl_], func=AF.Square), reads=[Bucv], writes=[Busq])
            P.op("pool", lambda e, tsl_=tsl_: e.tensor_copy(out=ub, in_=ucv[:, :, tsl_]), reads=[Bucv], writes=[Bub])
            for c in range(8):
                P.op("pe", lambda e, c=c: e.matmul(psum[:, 6, :], lhsT=onesb[:], rhs=ub[:, c, :], start=(c == 0), stop=(c == 7)), reads=[Bub, Bc], writes=[pb[6]])
            for c in range(8):
                P.op("pe", lambda e, c=c: e.matmul(psum[:, 7, :], lhsT=onesb[:], rhs=usq[:, c, :], start=(c == 0), stop=(c == 7)), reads=[Busq, Bc], writes=[pb[7]])
            P.op("dve", lambda e: e.tensor_scalar(out=mu, in0=psum[:, 6, :], scalar1=1.0 / 1024, scalar2=None, op0=ALU.mult), reads=[pb[6]], writes=[Bmu])
            P.op("dve", lambda e: e.tensor_tensor(out=m2, in0=mu, in1=mu, op=ALU.mult), reads=[Bmu], writes=[Bm2])
            P.op("dve", lambda e: e.scalar_tensor_tensor(out=m2, in0=psum[:, 7, :], scalar=1.0 / 1024, in1=m2, op0=ALU.mult, op1=ALU.subtract), reads=[pb[7], Bm2], writes=[Bm2])
            P.op("act", lambda e: e.activation(out=rstd, in_=m2, func=AF.Sqrt, scale=1.0, bias=epsc[:, 0:1]), reads=[Bm2, Bc], writes=[Brs])
            P.op("dve", lambda e: e.reciprocal(out=rstd, in_=rstd), reads=[Brs], writes=[Brs])
            for c in range(8):
                j = c % 2
                P.op("dve", lambda e, c=c, j=j, tsl_=tsl_: e.tensor_tensor(out=tq[j], in0=ucv[:, c, tsl_], in1=mu, op=ALU.subtract), reads=[Bucv, Bmu], writes=[Btq[j]])
                P.op("dve", lambda e, j=j: e.tensor_tensor(out=tq[j], in0=tq[j], in1=rstd, op=ALU.mult), reads=[Btq[j], Brs], writes=[Btq[j]])
                P.op("act", lambda e, c=c, j=j: e.activation(out=ybt[j], in_=tq[j], func=AF.Silu, scale=pvs("lng", c), bias=pvs("lnb", c)), reads=[Btq[j], Bc], writes=[Bybt[j]])
                P.dma("sp", lambda e, c=c, j=j, tsl_=tsl_: e.dma_start(out=ybT_s[c][:, tsl_], in_=ybt[j]), reads=[Bybt[j]], writes=[Bybs], key="ybt_st%d" % j)
        for blk in range(8):
            s = loadw([(5120 + blk * 512, 512)])
            for q in range(4):
                g = blk * 4 + q
                for tb in range(2):
                    bank = nextbank()
                    mm16(bank, 512, s, q * 128, tb * 512)
                    g2 = bank % 2
                    P.op("act", lambda e, bank=bank, g2=g2: e.activation(out=sgt[g2], in_=psum[:, bank, :], func=AF.Sigmoid), reads=[pb[bank]], writes=[Bsg[g2]])
                    P.dma("sp", lambda e, g=g, tb=tb, g2=g2: e.dma_start(out=sg_s[g // 16, g % 16][:, tb * 512:(tb + 1) * 512], in_=sgt[g2]), reads=[Bsg[g2]], writes=[Bsgs], key="sgt_st%d" % g2)
        P.barrier()

    def phase_C(ci):
        A = Arena()
        Y = A.take(2 * 16 * 512, BF16).rearrange("p (a b c) -> p a b c", a=2, b=16)
        tsc = [A.take(16 * 128, BF16).rearrange("p (a b) -> p a b", b=128) for _ in range(2)]
        tss = [A.take(16 * 128, BF16).rearrange("p (a b) -> p a b", b=128) for _ in range(2)]
        kr = [A.take(512, F32) for _ in range(2)]
        ki = [A.take(512, F32) for _ in range(2)]
        t1, t2, t3, t4 = [A.take(512, F32) for _ in range(4)]
        rt = [A.take(32 * 512, BF16).rearrange("p (a b) -> p a b", b=512) for _ in range(2)]
        vt = [A.take(512, F32) for _ in range(2)]
        xg = [A.take(512, F32) for _ in range(2)]
        zt = [A.take(512, F32) for _ in range(2)]
        zb = [A.take(512, BF16) for _ in range(2)]
        assert A.off <= TAIL_OFF, A.off
        pb = PB()
        BY = Buf("Y")
        Btc = [Buf("tsc0"), Buf("tsc1")]
        Bts = [Buf("tss0"), Buf("tss1")]
        Bkr = [Buf("kr0"), Buf("kr1")]
        Bki = [Buf("ki0"), Buf("ki1")]
        Bt = [Buf("t1"), Buf("t2"), Buf("t3"), Buf("t4")]
        Brt = [Buf("rt0"), Buf("rt1")]
        Bvt = [Buf("vt0"), Buf("vt1")]
        Bxg = [Buf("xg0"), Buf("xg1")]
        Bzt = [Buf("zt0"), Buf("zt1")]
        Bzb = [Buf("zb0"), Buf("zb1")]
        Bzs, Byas = Buf("z_s"), Buf("yaT_s")
        fcnt = 0
        rcnt = 0
        ecnt = 0
        for hh in range(2):
            hs = slice(hh * 512, (hh + 1) * 512)
            for fc in range(16):
                s = fcnt % 2
                fcnt += 1
                P.dma("sp", lambda e, s=s, fc=fc: e.dma_start(out=tsc[s], in_=tabD_d[0][fc]), writes=[Btc[s]])
                P.dma("sp", lambda e, s=s, fc=fc: e.dma_start(out=tss[s], in_=tabD_d[1][fc]), writes=[Bts[s]])
                P.dma("sp", lambda e, s=s, fc=fc, hs=hs: e.dma_start(out=kr[s], in_=KT_s[ci, 0, fc * 128:(fc + 1) * 128, hs]), writes=[Bkr[s]])
                P.dma("sp", lambda e, s=s, fc=fc, hs=hs: e.dma_start(out=ki[s], in_=KT_s[ci, 1, fc * 128:(fc + 1) * 128, hs]), writes=[Bki[s]])
                bR, bI = 2 * s, 2 * s + 1
                for tc in range(16):
                    P.op("pe", lambda e, tc=tc, s=s, bR=bR, hs=hs: e.matmul(psum[:, bR, :], lhsT=tsc[s][:, tc, :], rhs=uT[:, tc, hs], start=(tc == 0), stop=(tc == 15)),
                         reads=[Btc[s], BuT[hh]], writes=[pb[bR]])
                for tc in range(16):
                    P.op("pe", lambda e, tc=tc, s=s, bI=bI, hs=hs: e.matmul(psum[:, bI, :], lhsT=tss[s][:, tc, :], rhs=uT[:, tc, hs], start=(tc == 0), stop=(tc == 15)),
                         reads=[Bts[s], BuT[hh]], writes=[pb[bI]])
                P.op("dve", lambda e, s=s, bR=bR: e.tensor_tensor(out=t1, in0=psum[:, bR, :], in1=kr[s], op=ALU.mult), reads=[pb[bR], Bkr[s]], writes=[Bt[0]])
                P.op("dve", lambda e, s=s, bI=bI: e.tensor_tensor(out=t2, in0=psum[:, bI, :], in1=ki[s], op=ALU.mult), reads=[pb[bI], Bki[s]], writes=[Bt[1]])
                P.op("dve", lambda e, s=s, bR=bR: e.tensor_tensor(out=t3, in0=psum[:, bR, :], in1=ki[s], op=ALU.mult), reads=[pb[bR], Bki[s]], writes=[Bt[2]])
                P.op("dve", lambda e, s=s, bI=bI: e.tensor_tensor(out=t4, in0=psum[:, bI, :], in1=kr[s], op=ALU.mult), reads=[pb[bI], Bkr[s]], writes=[Bt[3]])
                P.op("pool", lambda e, fc=fc: e.tensor_tensor(out=Y[:, 0, fc, :], in0=t1, in1=t2, op=ALU.subtract), reads=[Bt[0], Bt[1]], writes=[BY])
                P.op("pool", lambda e, fc=fc: e.tensor_tensor(out=Y[:, 1, fc, :], in0=t3, in1=t4, op=ALU.add), reads=[Bt[2], Bt[3]], writes=[BY])
            ntb = 4 if ci == 0 else 2
            for tb in range(ntb):
                ts_ = slice(tb * 512, (tb + 1) * 512)
                r = rcnt % 2
                rcnt += 1

                def ld(e, r=r, ts_=ts_):
                    return [e.dma_start(out=rt[r][:, part * 16:(part + 1) * 16, :], in_=tabR_d[part].rearrange("(fc p) t -> p fc t", p=128)[:, :, ts_]) for part in range(2)]
                P.dma("sp", ld, writes=[Brt[r]], n=2)
                for c4 in range(4):
                    cc = hh * 4 + c4
                    j = ecnt % 2
                    ecnt += 1
                    bank = 4 + j
                    src_v = (v_s if ci == 0 else z_s)[cc][:, ts_]
                    src_x = (x1_s if ci == 0 else x2_s)[cc][:, ts_]
                    P.dma("sp", lambda e, j=j, src_v=src_v: e.dma_start(out=vt[j], in_=src_v), writes=[Bvt[j]])
                    P.dma("sp", lambda e, j=j, src_x=src_x: e.dma_start(out=xg[j], in_=src_x), writes=[Bxg[j]])
                    idx = 0
                    for part in range(2):
                        for fc in range(16):
                            P.op("pe", lambda e, part=part, fc=fc, c4=c4, r=r, bank=bank, idx=idx: e.matmul(psum[:, bank, :], lhsT=Y[:, part, fc, c4 * 128:(c4 + 1) * 128], rhs=rt[r][:, part * 16 + fc, :], start=(idx == 0), stop=(idx == 31)),
                                 reads=[BY, Brt[r]], writes=[pb[bank]])
                            idx += 1
                    P.op("dve", lambda e, j=j, cc=cc: e.tensor_scalar(out=vt[j], in0=vt[j], scalar1=pvs("hyb", ci * 8 + cc), scalar2=None, op0=ALU.mult), reads=[Bvt[j], Bc], writes=[Bvt[j]])
                    P.op("dve", lambda e, j=j, cc=cc, bank=bank: e.scalar_tensor_tensor(out=zt[j], in0=psum[:, bank, :], scalar=rl1s[:, ci * 8 + cc:ci * 8 + cc + 1], in1=vt[j], op0=ALU.mult, op1=ALU.add),
                         reads=[pb[bank], Brl1, Bvt[j]], writes=[Bzt[j]])
                    if ci == 0:
                        P.op("dve", lambda e, j=j: e.tensor_tensor(out=zt[j], in0=zt[j], in1=xg[j], op=ALU.mult), reads=[Bzt[j], Bxg[j]], writes=[Bzt[j]])
                        if tb < 2:
                            P.dma("sp", lambda e, j=j, cc=cc, ts_=ts_: e.dma_start(out=z_s[cc][:, ts_], in_=zt[j]), reads=[Bzt[j]], writes=[Bzs], key="zt_st%d" % j)
                        P.op("pool", lambda e, j=j: e.tensor_copy(out=zb[j], in_=zt[j]), reads=[Bzt[j]], writes=[Bzb[j]])
                        psT = psum[:, 6 + j, :].bitcast(BF16)[:, 0:512].rearrange("p (a b) -> p a b", b=128)
                        for i in range(4):
                            P.op("pe", lambda e, i=i, j=j, psT=psT: e.transpose(out=psT[:, i, :], in_=zb[j][:, i * 128:(i + 1) * 128], identity=identb[:]),
                                 reads=[Bzb[j], Bc], writes=[pb[6 + j]])
                        P.op("act", lambda e, psT=psT, tb=tb, cc=cc: e.copy(out=uT[:, tb * 4:(tb + 1) * 4, cc * 128:(cc + 1) * 128], in_=psT), reads=[pb[6 + j]], writes=[BuT[hh]])
                    else:
                        P.op("dve", lambda e, j=j: e.tensor_tensor(out=zb[j], in0=zt[j], in1=xg[j], op=ALU.mult), reads=[Bzt[j], Bxg[j]], writes=[Bzb[j]])
                        P.dma("sp", lambda e, j=j, cc=cc, ts_=ts_: e.dma_start(out=yaT_s[cc][:, ts_], in_=zb[j]), reads=[Bzb[j]], writes=[Byas], key="zb_st%d" % j)
        P.barrier()

    def phase_M():
        A = Arena()
        h2T = A.take(16 * 1024, BF16).rearrange("p (a b) -> p a b", b=1024)
        mT = A.take(16 * 1024, BF16).rearrange("p (a b) -> p a b", b=1024)
        wsl = [A.take(16 * 512, BF16).rearrange("p (a b) -> p a b", b=512) for _ in range(2)]
        mark = A.off
        yaT = A.take(8 * 1024, BF16).rearrange("p (a b) -> p a b", b=1024)
        ybT = A.take(8 * 1024, BF16).rearrange("p (a b) -> p a b", b=1024)
        ga = [A.take(512, F32) for _ in range(2)]
        gb = [A.take(512, F32) for _ in range(2)]
        m1 = [A.take(512, F32) for _ in range(2)]
        m2_ = [A.take(512, F32) for _ in range(2)]
        A.off = mark
        oT = A.take(16 * 512, F32).rearrange("p (a b) -> p a b", b=512)
        sq = A.take(16 * 512, BF16).rearrange("p (a b) -> p a b", b=512)
        rstd = A.take(512, F32)
        rstd2 = A.take(512, F32)
        xt = [A.take(512, F32) for _ in range(2)]
        tt = [A.take(512, F32) for _ in range(2)]
        pb = PB()
        Bya, Byb, BmT, Bh2 = Buf("yaT"), Buf("ybT"), Buf("mT"), Buf("h2T")
        Bws = [Buf("wslM0"), Buf("wslM1")]
        Bga = [Buf("ga0"), Buf("ga1")]
        Bgb = [Buf("gb0"), Buf("gb1")]
        Bm1 = [Buf("m10"), Buf("m11")]
        Bm2 = [Buf("m20"), Buf("m21")]
        BoT, Bsq, Brs, Brs2 = Buf("oT"), Buf("sqM"), Buf("rstdM"), Buf("rstdM2")
        Bxt = [Buf("xtM0"), Buf("xtM1")]
        Btt = [Buf("ttM0"), Buf("ttM1")]
        Br1s = Buf("r1T_s")
        P.dma("sp", lambda e: e.dma_start(out=yaT, in_=yaT_s.rearrange("c p t -> p c t")), writes=[Bya])
        P.dma("sp", lambda e: e.dma_start(out=ybT, in_=ybT_s.rearrange("c p t -> p c t")), writes=[Byb])
        wc = 0
        bc = 0
        for nb in range(4):
            s = wc % 2
            wc += 1

            def ldw(e, s=s, nb=nb):
                r = []
                for k in range(8):
                    r.append(e.dma_start(out=wsl[s][:, k, :], in_=hy_proj_d[k * 128:(k + 1) * 128, nb * 512:(nb + 1) * 512]))
                    r.append(e.dma_start(out=wsl[s][:, 8 + k, :], in_=cf_proj_d[k * 128:(k + 1) * 128, nb * 512:(nb + 1) * 512]))
                return r
            P.dma("pool", ldw, writes=[Bws[s]], n=16)
            for q in range(4):
                dch = nb * 4 + q
                for tb in range(2):
                    ts_ = slice(tb * 512, (tb + 1) * 512)
                    j = bc % 2
                    bc += 1
                    bA, bB = 2 * j, 2 * j + 1
                    P.dma("sp", lambda e, j=j, dch=dch, ts_=ts_: e.dma_start(out=ga[j], in_=sg_s[0, dch][:, ts_]), writes=[Bga[j]])
                    P.dma("sp", lambda e, j=j, dch=dch, ts_=ts_: e.dma_start(out=gb[j], in_=sg_s[1, dch][:, ts_]), writes=[Bgb[j]])
                    for k in range(8):
                        P.op("pe", lambda e, k=k, s=s, q=q, ts_=ts_, bA=bA: e.matmul(psum[:, bA, :], lhsT=wsl[s][:, k, q * 128:(q + 1) * 128], rhs=yaT[:, k, ts_], start=(k == 0), stop=(k == 7)),
                             reads=[Bws[s], Bya], writes=[pb[bA]])
                    for k in range(8):
                        P.op("pe", lambda e, k=k, s=s, q=q, ts_=ts_, bB=bB: e.matmul(psum[:, bB, :], lhsT=wsl[s][:, 8 + k, q * 128:(q + 1) * 128], rhs=ybT[:, k, ts_], start=(k == 0), stop=(k == 7)),
                             reads=[Bws[s], Byb], writes=[pb[bB]])
                    P.op("dve", lambda e, j=j, bA=bA: e.tensor_tensor(out=m1[j], in0=psum[:, bA, :], in1=ga[j], op=ALU.mult), reads=[pb[bA], Bga[j]], writes=[Bm1[j]])
                    P.op("dve", lambda e, j=j, bB=bB: e.tensor_tensor(out=m2_[j], in0=psum[:, bB, :], in1=gb[j], op=ALU.mult), reads=[pb[bB], Bgb[j]], writes=[Bm2[j]])
                    P.op("dve", lambda e, j=j, dch=dch, ts_=ts_: e.tensor_tensor(out=mT[:, dch, ts_], in0=m1[j], in1=m2_[j], op=ALU.add), reads=[Bm1[j], Bm2[j]], writes=[BmT])
        P.barrier()
        for tb in range(2):
            ts_ = slice(tb * 512, (tb + 1) * 512)
            for nb in range(4):
                s = wc % 2
                wc += 1
                P.dma("pool", lambda e, s=s, nb=nb: e.dma_start(out=wsl[s], in_=w_out_d[:, nb * 512:(nb + 1) * 512].rearrange("(k p) n -> p k n", p=128)), writes=[Bws[s]])
                for q in range(4):
                    nch = nb * 4 + q
                    bank = 4 + bc % 2
                    bc += 1
                    for k in range(16):
                        P.op("pe", lambda e, k=k, s=s, q=q, ts_=ts_, bank=bank: e.matmul(psum[:, bank, :], lhsT=wsl[s][:, k, q * 128:(q + 1) * 128], rhs=mT[:, k, ts_], start=(k == 0), stop=(k == 15)),
                             reads=[Bws[s], BmT], writes=[pb[bank]])
                    evac(oT[:, nch, :], psum[:, bank, :], [pb[bank]], [BoT])
                P.op("act", lambda e, nb=nb: e.activation(out=sq[:, nb * 4:(nb + 1) * 4, :], in_=oT[:, nb * 4:(nb + 1) * 4, :], func=AF.Square), reads=[BoT], writes=[Bsq])
            rms_stats(sq, 16, 512, 6, pb[6], Bsq, rstd, Brs)
            for nch in range(16):
                j = nch % 2
                P.dma("sp", lambda e, j=j, nch=nch, ts_=ts_: e.dma_start(out=xt[j], in_=xT_s[nch][:, ts_]), writes=[Bxt[j]])
                P.op("dve", lambda e, j=j, nch=nch: e.scalar_tensor_tensor(out=tt[j], in0=oT[:, nch, :], scalar=pvs("g2", nch), in1=rstd, op0=ALU.mult, op1=ALU.mult),
                     reads=[BoT, Brs, Bc], writes=[Btt[j]])
                P.op("dve", lambda e, j=j, nch=nch: e.tensor_tensor(out=oT[:, nch, :], in0=tt[j], in1=xt[j], op=ALU.add), reads=[Btt[j], Bxt[j], BoT], writes=[BoT])
            P.dma("sp", lambda e, ts_=ts_: e.dma_start(out=r1T_s[:, :, ts_].rearrange("c p t -> p c t"), in_=oT), reads=[BoT], writes=[Br1s], key="oT_st")
            for g4 in range(4):
                P.op("act", lambda e, g4=g4: e.activation(out=sq[:, g4 * 4:(g4 + 1) * 4, :], in_=oT[:, g4 * 4:(g4 + 1) * 4, :], func=AF.Square), reads=[BoT], writes=[Bsq])
            rms_stats(sq, 16, 512, 7, pb[7], Bsq, rstd2, Brs2)
            for nch in range(16):
                P.op("dve", lambda e, nch=nch, ts_=ts_: e.scalar_tensor_tensor(out=h2T[:, nch, ts_], in0=oT[:, nch, :], scalar=pvs("g3", nch), in1=rstd2, op0=ALU.mult, op1=ALU.mult),
                     reads=[BoT, Brs2, Bc], writes=[Bh2])
        P.barrier()

    def phase_G():
        A = Arena()
        h2T = A.take(16 * 1024, BF16).rearrange("p (a b) -> p a b", b=1024)
        wsl = [A.take(16 * 512, BF16).rearrange("p (a b) -> p a b", b=512) for _ in range(3)]
        st = [A.take(512, F32) for _ in range(2)]
        ab = [A.take(512, BF16) for _ in range(2)]
        pb = PB()
        Bh2 = Buf("h2T")
        Bws = [Buf("wslG%d" % i) for i in range(3)]
        Bst = [Buf("st0"), Buf("st1")]
        Bab = [Buf("ab0"), Buf("ab1")]
        Bas = Buf("act_s")
        bc = 0
        for blk in range(22):
            s = blk % 3

            def ldw(e, s=s, blk=blk):
                return [e.dma_start(out=wsl[s][:, :, 0:256], in_=w_gu_d[:, blk * 256:(blk + 1) * 256].rearrange("(k p) n -> p k n", p=128)),
                        e.dma_start(out=wsl[s][:, :, 256:512], in_=w_gu_d[:, FF + blk * 256:FF + (blk + 1) * 256].rearrange("(k p) n -> p k n", p=128))]
            P.dma("pool", ldw, writes=[Bws[s]], n=2)
            for i in range(2):
                kch = blk * 2 + i
                for tb in range(2):
                    ts_ = slice(tb * 512, (tb + 1) * 512)
                    j = bc % 2
                    bc += 1
                    bG, bU = 2 * j, 2 * j + 1
                    for k in range(16):
                        P.op("pe", lambda e, k=k, s=s, i=i, ts_=ts_, bG=bG: e.matmul(psum[:, bG, :], lhsT=wsl[s][:, k, i * 128:(i + 1) * 128], rhs=h2T[:, k, ts_], start=(k == 0), stop=(k == 15)),
                             reads=[Bws[s], Bh2], writes=[pb[bG]])
                    for k in range(16):
                        P.op("pe", lambda e, k=k, s=s, i=i, ts_=ts_, bU=bU: e.matmul(psum[:, bU, :], lhsT=wsl[s][:, k, 256 + i * 128:256 + (i + 1) * 128], rhs=h2T[:, k, ts_], start=(k == 0), stop=(k == 15)),
                             reads=[Bws[s], Bh2], writes=[pb[bU]])
                    P.op("act", lambda e, j=j, bG=bG: e.activation(out=st[j], in_=psum[:, bG, :], func=AF.Silu), reads=[pb[bG]], writes=[Bst[j]])
                    P.op("dve", lambda e, j=j, bU=bU: e.tensor_tensor(out=ab[j], in0=psum[:, bU, :], in1=st[j], op=ALU.mult), reads=[pb[bU], Bst[j]], writes=[Bab[j]])
                    P.dma("sp", lambda e, j=j, kch=kch, ts_=ts_: e.dma_start(out=act_s[kch][:, ts_], in_=ab[j]), reads=[Bab[j]], writes=[Bas], key="ab_st%d" % j)
        P.barrier()

    def phase_Dn():
        A = Arena()
        oT = A.take(16 * 1024, F32).rearrange("p (a b) -> p a b", b=1024)
        wd = [A.take(4 * 512, BF16).rearrange("p (a b) -> p a b", b=512) for _ in range(3)]
        ab = [A.take(4 * 1024, BF16).rearrange("p (a b) -> p a b", b=1024) for _ in range(3)]
        sq = A.take(16 * 512, BF16).rearrange("p (a b) -> p a b", b=512)
        rstd = A.take(512, F32)
        xt = [A.take(512, F32) for _ in range(2)]
        tt = [A.take(512, F32) for _ in range(2)]
        fo = A.take(16 * 512, F32).rearrange("p (a b) -> p a b", b=512)
        ot = [A.take(2048, F32) for _ in range(2)]
        pb = PB()
        BoT, Bsq, Brs, Bfo = Buf("oTD"), Buf("sqD"), Buf("rstdD"), Buf("fo")
        Bwd = [Buf("wd%d" % i) for i in range(3)]
        Bab = [Buf("abD%d" % i) for i in range(3)]
        Bxt = [Buf("xtD0"), Buf("xtD1")]
        Btt = [Buf("ttD0"), Buf("ttD1")]
        Bot = [Buf("ot0"), Buf("ot1")]
        Byy = Buf("yout")
        c3 = 0
        for ps_ in range(4):
            for kg in range(11):
                s = c3 % 3
                c3 += 1
                def ldwd(e, s=s, kg=kg, ps_=ps_):
                    return [e.dma_start(out=wd[s][:, k4, :], in_=w_down_d[(kg * 4 + k4) * 128:(kg * 4 + k4 + 1) * 128, ps_ * 512:(ps_ + 1) * 512]) for k4 in range(4)]
                P.dma("pool", ldwd, writes=[Bwd[s]], n=4)
                P.dma("sp", lambda e, s=s, kg=kg: e.dma_start(out=ab[s], in_=act_s[kg * 4:(kg + 1) * 4].rearrange("k p t -> p k t")), writes=[Bab[s]])
                for k4 in range(4):
                    for q in range(4):
                        for tb in range(2):
                            bank = q * 2 + tb
                            P.op("pe", lambda e, s=s, k4=k4, q=q, tb=tb, bank=bank, kg=kg: e.matmul(psum[:, bank, :], lhsT=wd[s][:, k4, q * 128:(q + 1) * 128], rhs=ab[s][:, k4, tb * 512:(tb + 1) * 512], start=(kg == 0 and k4 == 0), stop=(kg == 10 and k4 == 3)),
                                 reads=[Bwd[s], Bab[s]], writes=[pb[bank]])
            for q in range(4):
                for tb in range(2):
                    bank = q * 2 + tb
                    evac(oT[:, ps_ * 4 + q, tb * 512:(tb + 1) * 512], psum[:, bank, :], [pb[bank]], [BoT])
        ocnt = 0
        for tb in range(2):
            ts_ = slice(tb * 512, (tb + 1) * 512)
            for g4 in range(4):
                P.op("act", lambda e, g4=g4, ts_=ts_: e.activation(out=sq[:, g4 * 4:(g4 + 1) * 4, :], in_=oT[:, g4 * 4:(g4 + 1) * 4, ts_], func=AF.Square), reads=[BoT], writes=[Bsq])
            rms_stats(sq, 16, 512, tb, pb[tb], Bsq, rstd, Brs)
            for nch in range(16):
                j = nch % 2
                P.dma("sp", lambda e, j=j, nch=nch, ts_=ts_: e.dma_start(out=xt[j], in_=r1T_s[nch][:, ts_]), writes=[Bxt[j]])
                P.op("dve", lambda e, j=j, nch=nch, ts_=ts_: e.scalar_tensor_tensor(out=tt[j], in0=oT[:, nch, ts_], scalar=pvs("g4", nch), in1=rstd, op0=ALU.mult, op1=ALU.mult),
                     reads=[BoT, Brs, Bc], writes=[Btt[j]])
                P.op("dve", lambda e, j=j, nch=nch: e.tensor_tensor(out=fo[:, nch, :], in0=tt[j], in1=xt[j], op=ALU.add), reads=[Btt[j], Bxt[j]], writes=[Bfo])
            for t4 in range(4):
                o2 = ocnt % 2
                ocnt += 1
                for g4 in range(4):
                    bank = 2 + (t4 * 4 + g4) % 4
                    for q in range(4):
                        nch = g4 * 4 + q
                        P.op("pe", lambda e, bank=bank, q=q, nch=nch, t4=t4: e.transpose(out=psum[:, bank, q * 128:(q + 1) * 128], in_=fo[:, nch, t4 * 128:(t4 + 1) * 128], identity=identf[:]),
                             reads=[Bfo, Bc], writes=[pb[bank]])
                    evac(ot[o2][:, g4 * 512:(g4 + 1) * 512], psum[:, bank, :], [pb[bank]], [Bot[o2]])
                row = tb * 512 + t4 * 128
                P.dma("sp", lambda e, o2=o2, row=row: e.dma_start(out=y_d[row:row + 128, :], in_=ot[o2]), reads=[Bot[o2]], writes=[Byy], key="ot_st%d" % o2)
        P.op("sp", None, reads=[Byy])
        P.barrier()

    if "N1" in stages:
        phase_N1()
    if "P" in stages:
        phase_P()
    if "C" in stages:
        phase_C(0)
        phase_C(1)
    if "M" in stages:
        phase_M()
    if "G" in stages:
        phase_G()
    if "Dn" in stages:
        phase_Dn()

    if "Dn" not in stages and "F" in stages:
        By = Buf("y")
        Bfin = Buf("fin")
        fin = nc.alloc_sbuf_tensor("fin", [128, 16], F32)
        P.op("dve", lambda e: e.tensor_copy(out=fin[:], in_=rl1s[:]), reads=[Brl1], writes=[Bfin])
        P.dma("sp", lambda e: e.dma_start(out=y_d[0:128, 0:16], in_=fin[:]), reads=[Bfin], writes=[By], key="ystore")
        P.op("sp", None, reads=[By])
    if dbg is not None:
        srcs = {"KT": lambda: KT_s[0, 1], "KT2": lambda: KT_s[1, 0],
                "v": lambda: v_s.rearrange("c p t -> (c p) t"), "x1": lambda: x1_s.rearrange("c p t -> (c p) t"),
                "x2": lambda: x2_s.rearrange("c p t -> (c p) t"), "z": lambda: z_s.rearrange("c p t -> (c p) t"),
                "xT": lambda: xT_s.rearrange("c p t -> (c p) t"), "r1": lambda: r1T_s.rearrange("c p t -> (c p) t"),
                "sg": lambda: sg_s[0].rearrange("c p t -> (c p) t")}
        Bdbg = Buf("dbg")
        P.dma("sp", lambda e: e.dma_start(out=dbg_d, in_=srcs[dbg_name]()), writes=[Bdbg], key="dbgst")
        P.op("sp", None, reads=[Bdbg])
    P.emit()
    return nc


def core_inputs(inp, b, rev):
    C = consts()
    d = {}
    xb = inp["x"][b]
    d["x"] = np.ascontiguousarray(xb[::-1] if rev else xb)
    d["pvec"] = make_pvec(inp, rev)
    d["w_in"] = inp["w_in"][0]
    d["hy_proj"] = inp["hy_proj"][0]
    d["cf_proj"] = inp["cf_proj"][0]
    d["w_out"] = inp["w_out"][0]
    d["w_gu"] = inp["ffn_w_gu"][0]
    d["w_down"] = inp["ffn_w_down"][0]
    d["fw1"] = inp["hy_filt_w1"][0]
    fm = np.zeros((64, 68), np.float32)
    fm[:, 0:64] = inp["hy_filt_w2"][0]
    fm[:, 64] = inp["hy_filt_b1"][0]
    fm[:, 65] = inp["hy_filt_fr1"][0]
    fm[:, 66] = inp["hy_filt_b2"][0]
    fm[:, 67] = inp["hy_filt_fr2"][0]
    d["fmlp"] = fm
    w3 = inp["hy_filt_w3"][0]
    if rev:
        w3 = w3.reshape(64, 2, 2, 1024)[:, :, ::-1].reshape(64, 4096)
    d["fw3"] = np.ascontiguousarray(w3)
    for k in ["zT", "tabF_c", "tabF_s", "tabD_c", "tabD_s", "tabR", "ident_bf", "ident_f", "ones_bf", "tau_row", "adel_row"]:
        d[k] = C[k]
    return d


_NC = {}


def kernel(**inputs):
    inp = {k: np.asarray(v, dtype=np.float32) for k, v in inputs.items()}
    if "nc" not in _NC:
        _NC["nc"] = build()
    nc = _NC["nc"]
    in_maps = []
    for c in range(8):
        b, j = c // 2, c % 2
        in_maps.append(core_inputs(inp, b, j == 1))
    res = run_bass_kernel_spmd(nc, in_maps, core_ids=list(range(8)))
    out = np.zeros((4, 2048, 2048), np.float32)
    for c in range(8):
        b, j = c // 2, c % 2
        y = np.asarray(res.results[c]["y"], dtype=np.float32)
        if j == 0:
            out[b, 0:1024] = y
        else:
            out[b, 1024:2048] = y[::-1]
    return out
```

```python
import math
import numpy as np
import ml_dtypes
import concourse.bass as bass
import concourse.mybir as mybir
from concourse.bass_utils import run_bass_kernel_spmd

F32 = mybir.dt.float32
BF16 = mybir.dt.bfloat16
AF = mybir.ActivationFunctionType
ALU = mybir.AluOpType
AX = mybir.AxisListType

D = 2048
L = 2048
T1 = 1024
HWID = 1024
NIN = 9216
FF = 5632
EPS = 1e-6
NPV = 480
HY_MIN_DECAY = math.log(1e-2) / 1.5
HY_MAX_DECAY = math.log(1e-2) / 0.3


class Buf:
    __slots__ = ("name", "last_w", "readers")

    def __init__(self, name=""):
        self.name = name
        self.last_w = None
        self.readers = []


class Op:
    __slots__ = ("eng", "fn", "deps", "signal", "sig", "is_dma", "sem", "semval", "idx", "key")


class Prog:
    ENG_BLOCK = {"pe": "tensor", "act": "scalar", "dve": "vector", "pool": "gpsimd", "sp": "sync"}

    def __init__(self, nc):
        self.nc = nc
        self.ops = {k: [] for k in self.ENG_BLOCK}
        self.prog_sem = {k: nc.alloc_semaphore(name="prog_" + k) for k in self.ENG_BLOCK}
        self.dma_sems = {}
        self.nops = 0
        self.last_compute = {}
        self.phase_dmas = {}

    def _deps(self, o, reads, writes):
        deps = []

        def add(d):
            if d is None or d is o:
                return
            if (not d.is_dma) and (not o.is_dma) and d.eng == "pe" and o.eng == "pe":
                return
            for x in deps:
                if x is d:
                    return
            deps.append(d)

        for b in reads:
            add(b.last_w)
        for b in writes:
            add(b.last_w)
            for r in b.readers:
                add(r)
        return deps

    def _commit(self, o, reads, writes):
        for b in reads:
            if not o.is_dma:
                b.readers = [r for r in b.readers if r.is_dma or r.eng != o.eng]
            b.readers.append(o)
        for b in writes:
            b.last_w = o
            b.readers = []
        for d in o.deps:
            d.signal = True
        self.ops[o.eng].append(o)

    def op(self, eng, fn, reads=(), writes=()):
        o = Op()
        o.eng, o.fn, o.signal, o.sig, o.is_dma, o.sem, o.semval = eng, fn, False, 0, False, None, 0
        o.idx = self.nops
        self.nops += 1
        o.deps = self._deps(o, reads, writes)
        self._commit(o, reads, writes)
        if fn is not None:
            self.last_compute[eng] = o
        return o

    def dma(self, queue, fn, reads=(), writes=(), key=None, n=1):
        o = Op()
        o.eng, o.fn, o.signal, o.sig, o.is_dma = queue, fn, False, 0, True
        o.idx = self.nops
        self.nops += 1
        o.deps = self._deps(o, reads, writes)
        if key is None:
            key = writes[0].name
        if key not in self.dma_sems:
            self.dma_sems[key] = [self.nc.alloc_semaphore(name="d_" + key), 0]
        ent = self.dma_sems[key]
        ent[1] += 16 * n
        o.sem, o.semval, o.key = ent[0], ent[1], key
        self._commit(o, reads, writes)
        self.phase_dmas[key] = o
        return o

    def barrier(self):
        lasts = dict(self.last_compute)
        dmas = list(self.phase_dmas.values())
        for e in self.ENG_BLOCK:
            o = Op()
            o.eng, o.fn, o.signal, o.sig, o.is_dma, o.sem, o.semval = e, None, False, 0, False, None, 0
            o.idx = self.nops
            self.nops += 1
            o.deps = [v for k, v in lasts.items() if k != e] + dmas
            for d in o.deps:
                d.signal = True
            self.ops[e].append(o)
        self.phase_dmas = {}

    def emit(self):
        nc = self.nc
        for e, lst in self.ops.items():
            c = 0
            for o in lst:
                if o.is_dma:
                    continue
                if o.signal:
                    c += 1
                    o.sig = c
        with nc.Block() as block:
            for ename, bname in self.ENG_BLOCK.items():
                if not self.ops[ename]:
                    continue
                deco = getattr(block, bname)

                def body(eng, ename=ename):
                    waited = {}
                    mysem = self.prog_sem[ename]
                    for o in self.ops[ename]:
                        for d in o.deps:
                            if d.is_dma:
                                sem, val = d.sem, d.semval
                            else:
                                sem, val = self.prog_sem[d.eng], d.sig
                            k = id(sem)
                            if waited.get(k, 0) < val:
                                eng.wait_ge(sem, val)
                                waited[k] = val
                        if o.fn is None:
                            continue
                        r = o.fn(eng)
                        if o.is_dma:
                            if not isinstance(r, (list, tuple)):
                                r = [r]
                            for ins in r:
                                ins.then_inc(o.sem, 16)
                        elif o.signal:
                            r.then_inc(mysem, 1)

                deco(body)


_CONST = {}


def _lhsT_layout(M):
    A = M.reshape(16, 128, 16, 128)
    return np.ascontiguousarray(A.transpose(2, 1, 0, 3))


def consts():
    if _CONST:
        return _CONST
    bf = ml_dtypes.bfloat16
    n = np.arange(2048, dtype=np.float64)
    phi = math.pi / 4096.0
    th = phi * np.outer(n, 2 * n + 1)
    _CONST["tabF_c"] = _lhsT_layout(np.cos(th)).astype(bf)
    _CONST["tabF_s"] = _lhsT_layout(-np.sin(th)).astype(bf)
    ps = (phi / 2) * np.outer(2 * n + 1, 2 * n + 1)
    Ct = np.cos(ps)
    St = -np.sin(ps)
    _CONST["tabD_c"] = _lhsT_layout(Ct).astype(bf)
    _CONST["tabD_s"] = _lhsT_layout(St).astype(bf)
    _CONST["tabR"] = np.ascontiguousarray(np.stack([Ct, St], 0)).astype(bf)
    _CONST["ident_bf"] = np.eye(128).astype(bf)
    _CONST["ident_f"] = np.eye(128).astype(np.float32)
    _CONST["ones_bf"] = np.ones((128, 128)).astype(bf)
    _CONST["tau_row"] = n.astype(np.float32)[None, :]
    deltas = np.linspace(HY_MIN_DECAY, HY_MAX_DECAY, HWID, dtype=np.float32)
    _CONST["adel_row"] = np.abs(deltas)[None, :].astype(np.float32)
    f32 = np.float32
    t = np.linspace(0.0, 1.0, L, dtype=f32)[:, None]
    w = (2.0 * math.pi * np.arange(L, dtype=f32)[:, None] / L).astype(f32)
    f = np.linspace(1e-4, 15, 16, dtype=f32)[None, :]
    z = np.concatenate([t, np.cos(f * w), -np.sin(f * w)], axis=-1).astype(f32)
    _CONST["zT"] = np.ascontiguousarray(z.T)
    negt = -(np.arange(2048, dtype=np.float64) / (L - 1))
    _CONST["negt"] = negt.reshape(16, 128).T.astype(np.float32)
    nd = -(np.abs(deltas).astype(np.float64) / (L - 1))
    _CONST["negdel"] = nd.reshape(8, 128).T.astype(np.float32)
    return _CONST


def pm(v, nch):
    return np.ascontiguousarray(np.asarray(v).reshape(nch, 128).T)


def make_pvec(inp, rev):
    C = consts()
    pv = np.zeros((128, NPV), np.float32)
    pv[:, 0:16] = pm(inp["mix_pre_g"][0], 16)
    pv[:, 16:32] = pm(inp["mix_post_g"][0], 16)
    pv[:, 32:48] = pm(inp["ffn_pre_g"][0], 16)
    pv[:, 48:64] = pm(inp["ffn_post_g"][0], 16)
    hcw = inp["hy_conv_w"][0]
    if rev:
        hcw = hcw[::-1]
    pv[:, 64:136] = hcw.T.reshape(24, 128, 3).transpose(1, 0, 2).reshape(128, 72)
    pv[:, 136:160] = pm(inp["hy_conv_b"][0], 24)
    cdw = inp["cf_dw_w"][0]
    if rev:
        cdw = cdw[::-1]
    pv[:, 160:408] = cdw.T.reshape(8, 128, 31).transpose(1, 0, 2).reshape(128, 248)
    pv[:, 408:416] = pm(inp["cf_dw_b"][0], 8)
    pv[:, 416:424] = pm(inp["cf_ln_g"][0], 8)
    pv[:, 424:432] = pm(inp["cf_ln_b"][0], 8)
    hb = inp["hy_bias"][0]
    pv[:, 432:448] = hb.reshape(2, 8, 128).transpose(2, 0, 1).reshape(128, 16)
    pv[:, 448] = 0.0 if rev else 1.0
    pv[:, 449] = 1.0 if rev else 0.0
    pv[:, 450:466] = C["negt"]
    pv[:, 466:474] = C["negdel"]
    return pv


PV = dict(g1=0, g2=16, g3=32, g4=48, hcw=64, hcb=136, cdw=160, cdb=408, lng=416, lnb=424, hyb=432,
          flags=448, negt=450, negdel=466)


def build(stages=("F", "N1", "P", "C", "M", "G", "Dn"), dbg=None):
    nc = bass.Bass("TRN2", target_bir_lowering=False)
    P = Prog(nc)

    def din(n, s, dt=F32):
        return nc.dram_tensor(n, list(s), dt, kind="ExternalInput").ap()

    def dscr(n, s, dt=F32):
        return nc.dram_tensor(n, list(s), dt).ap()

    x_d = din("x", [2048, 2048])
    pvec_d = din("pvec", [128, NPV])
    w_in_d = din("w_in", [D, NIN])
    hy_proj_d = din("hy_proj", [HWID, D])
    cf_proj_d = din("cf_proj", [HWID, D])
    w_out_d = din("w_out", [D, D])
    w_gu_d = din("w_gu", [D, 2 * FF])
    w_down_d = din("w_down", [FF, D])
    w1_d = din("fw1", [33, 64])
    fm_d = din("fmlp", [64, 68])
    w3_d = din("fw3", [64, 4096])
    zT_d = din("zT", [33, 2048])
    tabF_d = [din("tabF_c", [16, 128, 16, 128], BF16), din("tabF_s", [16, 128, 16, 128], BF16)]
    tabD_d = [din("tabD_c", [16, 128, 16, 128], BF16), din("tabD_s", [16, 128, 16, 128], BF16)]
    tabR_d = din("tabR", [2, 2048, 2048], BF16)
    identb_d = din("ident_bf", [128, 128], BF16)
    identf_d = din("ident_f", [128, 128])
    onesb_d = din("ones_bf", [128, 128], BF16)
    tau_d = din("tau_row", [1, 2048])
    adel_d = din("adel_row", [1, 1024])
    y_d = nc.dram_tensor("y", [T1, D], F32, kind="ExternalOutput").ap()
    dbg_d = None
    dbg_name = None
    if dbg is not None:
        dbg_name, dshape = dbg
        dbg_d = nc.dram_tensor("dbg", list(dshape), F32, kind="ExternalOutput").ap()

    KT_s = dscr("KT_s", [2, 2, 2048, 1024])
    xT_s = dscr("xT_s", [16, 128, T1])
    v_s = dscr("v_s", [8, 128, 2048])
    x1_s = dscr("x1_s", [8, 128, 2048])
    x2_s = dscr("x2_s", [8, 128, T1])
    yaT_s = dscr("yaT_s", [8, 128, T1], BF16)
    ybT_s = dscr("ybT_s", [8, 128, T1], BF16)
    sg_s = dscr("sg_s", [2, 16, 128, T1])
    r1T_s = dscr("r1T_s", [16, 128, T1])
    act_s = dscr("act_s", [44, 128, T1], BF16)

    pv = nc.alloc_sbuf_tensor("pv", [128, NPV], F32)
    identb = nc.alloc_sbuf_tensor("identb", [128, 128], BF16)
    identf = nc.alloc_sbuf_tensor("identf", [128, 128], F32)
    onesb = nc.alloc_sbuf_tensor("onesb", [128, 128], BF16)
    rl1s = nc.alloc_sbuf_tensor("rl1s", [128, 16], F32)
    epsc = nc.alloc_sbuf_tensor("epsc", [128, 1], F32)
    zeroc = nc.alloc_sbuf_tensor("zeroc", [128, 1], F32)
    ARENA_BYTES = 200 * 1024
    arena = nc.alloc_sbuf_tensor("arena", [128, ARENA_BYTES // 2], BF16)
    psum = nc.alloc_psum_tensor("psum", [128, 8, 512], F32)
    Bc = Buf("const")
    Brl1 = Buf("rl1s")

    class Arena:
        def __init__(self):
            self.off = 0

        def take(self, nelem, dt):
            sz = 2 if dt == BF16 else 4
            nb = (nelem * sz + 63) // 64 * 64
            assert self.off + nb <= ARENA_BYTES, (self.off, nb)
            v = arena[:, self.off // 2:(self.off + nb) // 2]
            self.off += nb
            if dt == F32:
                v = v.bitcast(F32)
            return v[:, 0:nelem]

    def pvs(name, i=0, n=1):
        o = PV[name] + i
        return pv[:, o:o + n]

    def PB():
        return [Buf("pb%d" % i) for i in range(8)]

    P.dma("sp", lambda e: e.dma_start(out=pv[:], in_=pvec_d), writes=[Bc], key="c0")
    P.dma("sp", lambda e: e.dma_start(out=identb[:], in_=identb_d), writes=[Bc], key="c1")
    P.dma("sp", lambda e: e.dma_start(out=identf[:], in_=identf_d), writes=[Bc], key="c2")
    P.dma("sp", lambda e: e.dma_start(out=onesb[:], in_=onesb_d), writes=[Bc], key="c3")
    P.op("dve", lambda e: e.memset(epsc[:], EPS), writes=[Bc])
    P.op("dve", lambda e: e.memset(zeroc[:], 0.0), writes=[Bc])
    P.barrier()
    Bc = Buf("const")

    def phase_F():
        A = Arena()
        zT = A.take(2048, F32)
        h1T = A.take(2048, F32)
        h2T = A.take(2048, BF16)
        w3f = A.take(4096, F32)
        w3 = A.take(4096, BF16)
        w1 = A.take(64, F32)
        fm = A.take(68, F32)
        cc12 = A.take(2, F32)
        tau_bc = A.take(2048, F32)
        adel_bc = A.take(1024, F32)
        arg = [A.take(512, F32) for _ in range(2)]
        l1p = A.take(128, F32)
        l1q = A.take(32, F32)
        l1t = A.take(16, F32)
        junk = A.take(512, F32)
        wincm = A.take(2048, F32)
        wintm = [A.take(1024, F32) for _ in range(2)]
        tmpf = [A.take(512, F32) for _ in range(2)]
        tmpb = [A.take(512, F32) for _ in range(2)]
        tmpa = [A.take(512, F32) for _ in range(2)]
        ke = A.take(16 * 1024, BF16).rearrange("p (a b) -> p a b", b=1024)
        kd = A.take(16 * 1024, BF16).rearrange("p (a b) -> p a b", b=1024)
        tsl = [A.take(16 * 128, BF16).rearrange("p (a b) -> p a b", b=128) for _ in range(4)]
        ksb = [A.take(1024, F32) for _ in range(2)]
        pb = PB()
        Bz, Bw = Buf("Fz"), Buf("Fw")
        Bh1, Bh2 = Buf("h1"), Buf("h2")
        Barg = [Buf("arg0"), Buf("arg1")]
        Bjk = Buf("jk")
        P.dma("sp", lambda e: e.dma_start(out=zT[0:33, :], in_=zT_d), writes=[Bz], key="f0")
        P.dma("sp", lambda e: e.dma_start(out=w1[0:33, :], in_=w1_d), writes=[Bw], key="f1")
        P.dma("sp", lambda e: e.dma_start(out=fm[0:64, :], in_=fm_d), writes=[Bw], key="f2")
        Bw3f = Buf("w3f")
        P.dma("sp", lambda e: e.dma_start(out=w3f[0:64, :], in_=w3_d), writes=[Bw3f], key="f3")
        P.op("dve", lambda e: e.memset(w3, 0.0), writes=[Bw])
        P.op("dve", lambda e: e.tensor_copy(out=w3[0:64, :], in_=w3f[0:64, :]), reads=[Bw3f, Bw], writes=[Bw])
        P.op("dve", lambda e: e.memset(h2T, 0.0), writes=[Bh2])
        P.dma("sp", lambda e: e.dma_start(out=tau_bc, in_=tau_d.partition_broadcast(128)), writes=[Bw], key="f4")
        P.dma("sp", lambda e: e.dma_start(out=adel_bc, in_=adel_d.partition_broadcast(128)), writes=[Bw], key="f5")
        Bcc = Buf("cc")
        P.op("dve", lambda e: e.tensor_tensor(out=cc12[0:64, 0:1], in0=fm[0:64, 64:65], in1=fm[0:64, 65:66], op=ALU.mult),
             reads=[Bw], writes=[Bcc])
        P.op("dve", lambda e: e.tensor_tensor(out=cc12[0:64, 1:2], in0=fm[0:64, 66:67], in1=fm[0:64, 67:68], op=ALU.mult),
             reads=[Bw, Bcc], writes=[Bcc])
        cnt = 0
        for layer in range(2):
            src = zT if layer == 0 else h1T
            dst = h1T if layer == 0 else h2T
            Bs = Bz if layer == 0 else Bh1
            Bd = Bh1 if layer == 0 else Bh2
            kk = 33 if layer == 0 else 64
            wl = w1[0:33, 0:64] if layer == 0 else fm[0:64, 0:64]
            frc = fm[0:64, 65:66] if layer == 0 else fm[0:64, 67:68]
            cb_ = cc12[0:64, layer:layer + 1]
            for tb in range(4):
                bk = cnt % 2
                cnt += 1
                ts = slice(tb * 512, (tb + 1) * 512)
                P.op("pe", lambda e, bk=bk, ts=ts, src=src, kk=kk, wl=wl: e.matmul(psum[0:64, bk, :], lhsT=wl, rhs=src[0:kk, ts], start=True, stop=True),
                     reads=[Bs, Bw], writes=[pb[bk]])
                P.op("act", lambda e, bk=bk, frc=frc, cb_=cb_: e.activation(out=arg[bk][0:64, :], in_=psum[0:64, bk, :], func=AF.Identity, scale=frc, bias=cb_),
                     reads=[pb[bk], Bw, Bcc], writes=[Barg[bk]])
                for _ in range(2):
                    for (thr, cmp_, per) in ((math.pi, ALU.is_gt, -2 * math.pi), (-math.pi, ALU.is_lt, 2 * math.pi)):
                        P.op("dve", lambda e, bk=bk, thr=thr, cmp_=cmp_, per=per: e.tensor_scalar(out=junk[0:64, :], in0=arg[bk][0:64, :], scalar1=thr, scalar2=per, op0=cmp_, op1=ALU.mult),
                             reads=[Barg[bk]], writes=[Bjk])
                        P.op("dve", lambda e, bk=bk: e.tensor_tensor(out=arg[bk][0:64, :], in0=arg[bk][0:64, :], in1=junk[0:64, :], op=ALU.add),
                             reads=[Barg[bk], Bjk], writes=[Barg[bk]])
                P.op("act", lambda e, bk=bk, dst=dst, ts=ts: e.activation(out=dst[0:64, ts], in_=arg[bk][0:64, :], func=AF.Sin),
                     reads=[Barg[bk]], writes=[Bd])
        Bwin, Bl1p = Buf("wincm"), Buf("l1p")
        Btmp = [Buf("tmpa0"), Buf("tmpa1")]
        Bjunk = Buf("junk")
        cnt = 0
        for cc in range(8):
            P.op("act", lambda e, cc=cc: e.activation(out=wincm, in_=tau_bc, func=AF.Exp, scale=pvs("negdel", cc)),
                 reads=[Bw, Bc], writes=[Bwin])
            for o in range(2):
                for dr in range(2):
                    q = o * 16 + dr * 8 + cc
                    for tb in range(4):
                        bk = 2 + cnt % 2
                        tm = cnt % 2
                        cnt += 1
                        ts = slice(tb * 512, (tb + 1) * 512)
                        P.op("pe", lambda e, bk=bk, q=q, ts=ts: e.matmul(psum[:, bk, :], lhsT=w3[:, q * 128:(q + 1) * 128], rhs=h2T[:, ts], start=True, stop=True),
                             reads=[Bw, Bh2], writes=[pb[bk]])
                        P.op("dve", lambda e, bk=bk, tm=tm, ts=ts: e.tensor_tensor(out=tmpa[tm], in0=psum[:, bk, :], in1=wincm[:, ts], op=ALU.mult),
                             reads=[pb[bk], Bwin], writes=[Btmp[tm]])
                        if tb == 0:
                            P.op("dve", lambda e, tm=tm, dr=dr: e.tensor_scalar(out=tmpa[tm][:, 0:1], in0=tmpa[tm][:, 0:1], scalar1=pvs("flags", dr), scalar2=None, op0=ALU.mult),
                                 reads=[Btmp[tm], Bc], writes=[Btmp[tm]])
                        P.op("act", lambda e, tm=tm, q=q, tb=tb: e.activation(out=junk, in_=tmpa[tm], func=AF.Abs, accum_out=l1p[:, q * 4 + tb:q * 4 + tb + 1]),
                             reads=[Btmp[tm]], writes=[Bjunk, Bl1p])
        P.op("dve", lambda e: e.tensor_reduce(out=l1q, in_=l1p.rearrange("p (a b) -> p a b", b=4), axis=AX.X, op=ALU.add),
             reads=[Bl1p], writes=[Bl1p])
        for o in range(2):
            P.op("dve", lambda e, o=o: e.tensor_tensor(out=l1t[:, o * 8:(o + 1) * 8], in0=l1q[:, o * 16:o * 16 + 8], in1=l1q[:, o * 16 + 8:o * 16 + 16], op=ALU.add),
                 reads=[Bl1p], writes=[Bl1p])
        P.op("dve", lambda e: e.reciprocal(out=l1t, in_=l1t), reads=[Bl1p], writes=[Bl1p])
        P.op("dve", lambda e: e.tensor_scalar(out=rl1s[:], in0=l1t, scalar1=1.0 / 2048.0, scalar2=None, op0=ALU.mult),
             reads=[Bl1p], writes=[Brl1])
        Bwt = [Buf("wintm0"), Buf("wintm1")]
        Btf = [Buf("tmpf0"), Buf("tmpf1")]
        Btb = [Buf("tmpb0"), Buf("tmpb1")]
        Bke, Bkd = Buf("ke"), Buf("kd")
        Bts = [Buf("tsl%d" % i) for i in range(4)]
        Bks = [Buf("ksb0"), Buf("ksb1")]
        BKT = Buf("KT")
        cnt = 0
        tcnt = 0
        kcnt = 0
        iters = [(part, fc) for part in range(2) for fc in range(16)]
        PF = 3
        tbase = 0
        for o in range(2):
            def issue_tsl(i, tbase=tbase):
                part_, fc_ = iters[i]
                s3_ = (tbase + i) % 4
                P.dma("sp", lambda e, s3_=s3_, part_=part_, fc_=fc_: e.dma_start(out=tsl[s3_], in_=tabF_d[part_][fc_]), writes=[Bts[s3_]])
            for i in range(PF):
                issue_tsl(i)
            for tc in range(16):
                wi = tc % 2
                P.op("act", lambda e, wi=wi, tc=tc: e.activation(out=wintm[wi], in_=adel_bc, func=AF.Exp, scale=pvs("negt", tc)),
                     reads=[Bw, Bc], writes=[Bwt[wi]])
                for half in range(2):
                    colf = (o * 2 + 0) * 1024 + half * 512
                    colb = (o * 2 + 1) * 1024 + half * 512
                    i2 = cnt % 2
                    cnt += 1
                    bF, bB = 4 + 2 * i2, 5 + 2 * i2
                    tsl_ = slice(tc * 128, (tc + 1) * 128)
                    hs = slice(half * 512, (half + 1) * 512)
                    P.op("pe", lambda e, bF=bF, tsl_=tsl_, colf=colf: e.matmul(psum[:, bF, :], lhsT=h2T[:, tsl_], rhs=w3[:, colf:colf + 512], start=True, stop=True),
                         reads=[Bw, Bh2], writes=[pb[bF]])
                    P.op("pe", lambda e, bB=bB, tsl_=tsl_, colb=colb: e.matmul(psum[:, bB, :], lhsT=h2T[:, tsl_], rhs=w3[:, colb:colb + 512], start=True, stop=True),
                         reads=[Bw, Bh2], writes=[pb[bB]])
                    P.op("dve", lambda e, bF=bF, i2=i2, wi=wi, hs=hs: e.tensor_tensor(out=tmpf[i2], in0=psum[:, bF, :], in1=wintm[wi][:, hs], op=ALU.mult),
                         reads=[pb[bF], Bwt[wi]], writes=[Btf[i2]])
                    P.op("dve", lambda e, bB=bB, i2=i2, wi=wi, hs=hs: e.tensor_tensor(out=tmpb[i2], in0=psum[:, bB, :], in1=wintm[wi][:, hs], op=ALU.mult),
                         reads=[pb[bB], Bwt[wi]], writes=[Btb[i2]])
                    if tc == 0:
                        P.op("dve", lambda e, i2=i2: e.tensor_scalar(out=tmpf[i2][0:1, :], in0=tmpf[i2][0:1, :], scalar1=pv[0:1, PV["flags"]:PV["flags"] + 1], scalar2=None, op0=ALU.mult),
                             reads=[Btf[i2], Bc], writes=[Btf[i2]])
                        P.op("dve", lambda e, i2=i2: e.tensor_scalar(out=tmpb[i2][0:1, :], in0=tmpb[i2][0:1, :], scalar1=pv[0:1, PV["flags"] + 1:PV["flags"] + 2], scalar2=None, op0=ALU.mult),
                             reads=[Btb[i2], Bc], writes=[Btb[i2]])
                    P.op("pool", lambda e, i2=i2, tc=tc, hs=hs: e.tensor_tensor(out=ke[:, tc, hs], in0=tmpf[i2], in1=tmpb[i2], op=ALU.add),
                         reads=[Btf[i2], Btb[i2]], writes=[Bke])
                    P.op("pool", lambda e, i2=i2, tc=tc, hs=hs: e.tensor_tensor(out=kd[:, tc, hs], in0=tmpf[i2], in1=tmpb[i2], op=ALU.subtract),
                         reads=[Btf[i2], Btb[i2]], writes=[Bkd])
            for i, (part, fc) in enumerate(iters):
                src, Bsrc = (ke, Bke) if part == 0 else (kd, Bkd)
                if i + PF < len(iters):
                    issue_tsl(i + PF)
                s3 = (tbase + i) % 4
                ks = kcnt % 2
                kcnt += 1
                for cb in range(2):
                    bk = (kcnt * 2 + cb) % 4
                    for tc in range(16):
                        P.op("pe", lambda e, bk=bk, s3=s3, tc=tc, cb=cb, src=src: e.matmul(psum[:, bk, :], lhsT=tsl[s3][:, tc, :], rhs=src[:, tc, cb * 512:(cb + 1) * 512], start=(tc == 0), stop=(tc == 15)),
                             reads=[Bts[s3], Bsrc], writes=[pb[bk]])
                    if cb == 0:
                        P.op("act", lambda e, bk=bk, ks=ks, cb=cb: e.copy(out=ksb[ks][:, cb * 512:(cb + 1) * 512], in_=psum[:, bk, :]),
                             reads=[pb[bk]], writes=[Bks[ks]])
                    else:
                        P.op("dve", lambda e, bk=bk, ks=ks, cb=cb: e.tensor_copy(out=ksb[ks][:, cb * 512:(cb + 1) * 512], in_=psum[:, bk, :]),
                             reads=[pb[bk]], writes=[Bks[ks]])
                P.dma("sp", lambda e, ks=ks, o=o, part=part, fc=fc: e.dma_start(out=KT_s[o, part, fc * 128:(fc + 1) * 128, :], in_=ksb[ks]),
                      reads=[Bks[ks]], writes=[BKT], key="ksb_st%d" % ks)
            tbase += len(iters)
        P.barrier()

    if "F" in stages:
        phase_F()

    TAIL_OFF = 168 * 1024
    uT = arena[:, TAIL_OFF // 2:(TAIL_OFF + 32768) // 2].rearrange("p (a b) -> p a b", b=1024)
    BuT = [Buf("uT0"), Buf("uT1")]
    z_s = dscr("z_s", [8, 128, T1])
    cpy_cnt = [0]

    def evac(out, in_, Bin, Bout):
        cpy_cnt[0] += 1
        if cpy_cnt[0] % 2:
            P.op("act", lambda e: e.copy(out=out, in_=in_), reads=Bin, writes=Bout)
        else:
            P.op("dve", lambda e: e.tensor_copy(out=out, in_=in_), reads=Bin, writes=Bout)

    def rms_stats(sq3, nch, ncol, bank, pbk, Bsq, rstd, Brstd):
        for c in range(nch):
            P.op("pe", lambda e, c=c: e.matmul(psum[:, bank, 0:ncol], lhsT=onesb[:], rhs=sq3[:, c, :], start=(c == 0), stop=(c == nch - 1)),
                 reads=[Bsq, Bc], writes=[pbk])
        P.op("act", lambda e: e.activation(out=rstd, in_=psum[:, bank, 0:ncol], func=AF.Sqrt, scale=1.0 / (nch * 128), bias=epsc[:, 0:1]),
             reads=[pbk, Bc], writes=[Brstd])
        P.op("dve", lambda e: e.reciprocal(out=rstd, in_=rstd), reads=[Brstd], writes=[Brstd])

    def phase_N1():
        A = Arena()
        hT = A.take(16 * 2048, BF16).rearrange("p (a b) -> p a b", b=2048)
        xt = [A.take(2048, F32) for _ in range(2)]
        xTb = A.take(16 * 512, F32).rearrange("p (a b) -> p a b", b=512)
        sq = A.take(16 * 512, BF16).rearrange("p (a b) -> p a b", b=512)
        rstd = A.take(512, F32)
        pb = PB()
        Bxt = [Buf("xt0"), Buf("xt1")]
        BxTb, Bsq, Brs, BhT, BxTs = Buf("xTb"), Buf("sq"), Buf("rstd"), Buf("hT"), Buf("xTs")
        bcnt = 0
        for tb in range(4):
            for i in range(4):
                tile = tb * 4 + i
                s = tile % 2
                P.dma("sp", lambda e, s=s, tile=tile: e.dma_start(out=xt[s], in_=x_d[tile * 128:(tile + 1) * 128, :]), writes=[Bxt[s]])
                for g4 in range(4):
                    bank = bcnt % 4
                    bcnt += 1
                    for q in range(4):
                        dc = g4 * 4 + q
                        P.op("pe", lambda e, s=s, bank=bank, q=q, dc=dc: e.transpose(out=psum[:, bank, q * 128:(q + 1) * 128], in_=xt[s][:, dc * 128:(dc + 1) * 128], identity=identf[:]),
                             reads=[Bxt[s], Bc], writes=[pb[bank]])
                    evac(xTb[:, g4 * 4:(g4 + 1) * 4, i * 128:(i + 1) * 128], psum[:, bank, :].rearrange("p (a b) -> p a b", b=128), [pb[bank]], [BxTb])
            for g4 in range(4):
                P.op("act", lambda e, g4=g4: e.activation(out=sq[:, g4 * 4:(g4 + 1) * 4, :], in_=xTb[:, g4 * 4:(g4 + 1) * 4, :], func=AF.Square),
                     reads=[BxTb], writes=[Bsq])
            rms_stats(sq, 16, 512, 4, pb[4], Bsq, rstd, Brs)
            for dc in range(16):
                P.op("dve", lambda e, dc=dc, tb=tb: e.scalar_tensor_tensor(out=hT[:, dc, tb * 512:(tb + 1) * 512], in0=xTb[:, dc, :], scalar=pvs("g1", dc), in1=rstd, op0=ALU.mult, op1=ALU.mult),
                     reads=[BxTb, Brs, Bc], writes=[BhT])
            if tb < 2:
                P.dma("sp", lambda e, tb=tb: e.dma_start(out=xT_s[:, :, tb * 512:(tb + 1) * 512].rearrange("c p t -> p c t"), in_=xTb),
                      reads=[BxTb], writes=[BxTs], key="xTb_st")
        P.barrier()

    def phase_P():
        A = Arena()
        hT = A.take(16 * 2048, BF16).rearrange("p (a b) -> p a b", b=2048)
        wsl = [A.take(16 * 512, BF16).rearrange("p (a b) -> p a b", b=512) for _ in range(2)]
        sgt = [A.take(512, F32) for _ in range(2)]
        mark = A.off
        p_sb = [A.take(2050, F32) for _ in range(2)]
        acc = [A.take(2048, F32) for _ in range(2)]
        accb = A.take(2048, BF16)
        assert A.off <= TAIL_OFF, A.off
        A.off = mark
        upad = A.take(15 + 1056 + 1, BF16)
        dg = A.take(31 * 128, BF16).rearrange("p (a b) -> p a b", b=128)
        ucv = A.take(8 * 1024, F32).rearrange("p (a b) -> p a b", b=1024)
        ub = A.take(8 * 512, BF16).rearrange("p (a b) -> p a b", b=512)
        usq = A.take(8 * 512, BF16).rearrange("p (a b) -> p a b", b=512)
        mu = A.take(512, F32)
        m2 = A.take(512, F32)
        rstd = A.take(512, F32)
        tq = sgt
        ybt = [A.take(512, BF16) for _ in range(2)]
        assert A.off <= TAIL_OFF, A.off
        pb = PB()
        BhT = Buf("hT")
        Bws = [Buf("wsl0"), Buf("wsl1")]
        Bp = [Buf("p_sb0"), Buf("p_sb1")]
        Bacc = [Buf("acc0"), Buf("acc1")]
        Baccb, Bup, Bdg = Buf("accb"), Buf("upad"), Buf("dg")
        Bsg = [Buf("sgt0"), Buf("sgt1")]
        Bucv, Bub, Busq, Bmu, Bm2, Brs = Buf("ucv"), Buf("ub"), Buf("usq"), Buf("mu"), Buf("m2"), Buf("rstdP")
        Btq = Bsg
        Bybt = [Buf("ybt0"), Buf("ybt1")]
        Bvs, Bx1s, Bx2s, Bybs, Bsgs = Buf("v_s"), Buf("x1_s"), Buf("x2_s"), Buf("ybT_s"), Buf("sg_s")
        for i in range(2):
            P.op("dve", lambda e, i=i: e.memset(p_sb[i], 0.0), writes=[Bp[i]])
        wcnt = [0]
        bcnt = [0]

        def loadw(cols):
            s = wcnt[0] % 2
            wcnt[0] += 1

            def fn(e, s=s, cols=cols):
                r = []
                o = 0
                for (c0, n) in cols:
                    r.append(e.dma_start(out=wsl[s][:, :, o:o + n], in_=w_in_d[:, c0:c0 + n].rearrange("(k p) n -> p k n", p=128)))
                    o += n
                return r
            P.dma("pool", fn, writes=[Bws[s]], n=len(cols))
            return s

        def mm16(bank, n, s, col, t0):
            for k in range(16):
                P.op("pe", lambda e, k=k: e.matmul(psum[:, bank, 0:n], lhsT=wsl[s][:, k, col:col + 128], rhs=hT[:, k, t0:t0 + n], start=(k == 0), stop=(k == 15)),
                     reads=[Bws[s], BhT], writes=[pb[bank]])

        def nextbank(lo=0, n=4):
            b = lo + bcnt[0] % n
            bcnt[0] += 1
            return b

        for blk in range(6):
            s = loadw([(blk * 512, 512)])
            for q in range(4):
                ch = blk * 4 + q
                kind = ch // 8
                i2 = ch % 2
                groups = [(0, 512), (512, 512), (1024, 512), (1536, 512)] if kind < 2 else [(0, 512), (512, 512), (1024, 32)]
                for (t0, n) in groups:
                    bank = nextbank()
                    mm16(bank, n, s, q * 128, t0)
                    evac(p_sb[i2][:, 1 + t0:1 + t0 + n], psum[:, bank, 0:n], [pb[bank]], [Bp[i2]])
                nt = 2048 if kind < 2 else 1024
                a = acc[i2][:, 0:nt]
                P.op("act", lambda e, a=a, i2=i2, nt=nt, ch=ch: e.activation(out=a, in_=p_sb[i2][:, 1:1 + nt], func=AF.Identity, scale=pvs("hcw", ch * 3 + 1), bias=pvs("hcb", ch)),
                     reads=[Bp[i2], Bc], writes=[Bacc[i2]])
                P.op("dve", lambda e, a=a, i2=i2, nt=nt, ch=ch: e.scalar_tensor_tensor(out=a, in0=p_sb[i2][:, 0:nt], scalar=pvs("hcw", ch * 3 + 0), in1=a, op0=ALU.mult, op1=ALU.add),
                     reads=[Bp[i2], Bc, Bacc[i2]], writes=[Bacc[i2]])
                P.op("dve", lambda e, a=a, i2=i2, nt=nt, ch=ch: e.scalar_tensor_tensor(out=a, in0=p_sb[i2][:, 2:2 + nt], scalar=pvs("hcw", ch * 3 + 2), in1=a, op0=ALU.mult, op1=ALU.add),
                     reads=[Bp[i2], Bc, Bacc[i2]], writes=[Bacc[i2]])
                dst = (v_s, x1_s, x2_s)[kind][ch % 8]
                Bd = (Bvs, Bx1s, Bx2s)[kind]
                P.dma("sp", lambda e, a=a, dst=dst: e.dma_start(out=dst, in_=a), reads=[Bacc[i2]], writes=[Bd], key="acc_st%d" % i2)
                if kind == 0:
                    P.op("pool", lambda e, i2=i2: e.tensor_copy(out=accb, in_=acc[i2]), reads=[Bacc[i2]], writes=[Baccb])
                    psT = psum[:, 4:6, :].rearrange("p a b -> p (a b)").bitcast(BF16).rearrange("p (a b) -> p a b", b=128)
                    for tc in range(16):
                        P.op("pe", lambda e, tc=tc: e.transpose(out=psT[:, tc, :], in_=accb[:, tc * 128:(tc + 1) * 128], identity=identb[:]),
                             reads=[Baccb, Bc], writes=[pb[4], pb[5]])
                    P.op("act", lambda e, ch=ch: e.copy(out=uT[:, :, ch * 128:(ch + 1) * 128], in_=psT), reads=[pb[4], pb[5]], writes=[BuT[ch // 4]])
        P.barrier()
        P.op("dve", lambda e: e.memset(upad, 0.0), writes=[Bup])
        for pg in range(4):
            s = loadw([(3072 + pg * 256, 256), (4096 + pg * 256, 256)])
            for i in range(2):
                cc = pg * 2 + i
                for (t0, n) in [(0, 512), (512, 512), (1024, 32)]:
                    bA = nextbank()
                    mm16(bA, n, s, i * 128, t0)
                    bB = nextbank()
                    mm16(bB, n, s, 256 + i * 128, t0)
                    g2 = bB % 2
                    P.op("act", lambda e, bB=bB, n=n, g2=g2: e.activation(out=sgt[g2][:, 0:n], in_=psum[:, bB, 0:n], func=AF.Sigmoid),
                         reads=[pb[bB]], writes=[Bsg[g2]])
                    P.op("dve", lambda e, bA=bA, n=n, g2=g2, t0=t0: e.tensor_tensor(out=upad[:, 15 + t0:15 + t0 + n], in0=psum[:, bA, 0:n], in1=sgt[g2][:, 0:n], op=ALU.mult),
                         reads=[pb[bA], Bsg[g2]], writes=[Bup])
                for k in range(31):
                    P.op("act", lambda e, k=k, cc=cc: e.activation(out=dg[:, k, :], in_=identb[:], func=AF.Identity, scale=pvs("cdw", cc * 31 + k), bias=zeroc[:, 0:1]),
                         reads=[Bc], writes=[Bdg])
                for tb in range(2):
                    bank = nextbank()
                    for k in range(31):
                        P.op("pe", lambda e, k=k, tb=tb, bank=bank: e.matmul(psum[:, bank, :], lhsT=dg[:, k, :], rhs=upad[:, tb * 512 + k:tb * 512 + k + 512], start=(k == 0), stop=(k == 30)),
                             reads=[Bdg, Bup], writes=[pb[bank]])
                    P.op("act", lambda e, tb=tb, bank=bank, cc=cc: e.activation(out=ucv[:, cc, tb * 512:(tb + 1) * 512], in_=psum[:, bank, :], func=AF.Identity, scale=1.0, bias=pvs("cdb", cc)),
                         reads=[pb[bank], Bc], writes=[Bucv])
        for tb in range(2):
            tsl_ = slice(tb * 512, (tb + 1) * 512)
            P.op("act", lambda e, tsl_=tsl_: e.activation(out=usq, in_=ucv[:, :, tsl_], func=AF.Square), reads=[Bucv], writes=[Busq])
            P.op("pool", lambda e, tsl_=tsl_: e.tensor_copy(out=ub, in_=ucv[:, :, tsl_]), reads=[Bucv], writes=[Bub])
            for c in range(8):
                P.op("pe", lambda e, c=c: e.matmul(psum[:, 6, :], lhsT=onesb[:], rhs=ub[:, c, :], start=(c == 0), stop=(c == 7)), reads=[Bub, Bc], writes=[pb[6]])
            for c in range(8):
                P.op("pe", lambda e, c=c: e.matmul(psum[:, 7, :], lhsT=onesb[:], rhs=usq[:, c, :], start=(c == 0), stop=(c == 7)), reads=[Busq, Bc], writes=[pb[7]])
            P.op("dve", lambda e: e.tensor_scalar(out=mu, in0=psum[:, 6, :], scalar1=1.0 / 1024, scalar2=None, op0=ALU.mult), reads=[pb[6]], writes=[Bmu])
            P.op("dve", lambda e: e.tensor_tensor(out=m2, in0=mu, in1=mu, op=ALU.mult), reads=[Bmu], writes=[Bm2])
            P.op("dve", lambda e: e.scalar_tensor_tensor(out=m2, in0=psum[:, 7, :], scalar=1.0 / 1024, in1=m2, op0=ALU.mult, op1=ALU.subtract), reads=[pb[7], Bm2], writes=[Bm2])
            P.op("act", lambda e: e.activation(out=rstd, in_=m2, func=AF.Sqrt, scale=1.0, bias=epsc[:, 0:1]), reads=[Bm2, Bc], writes=[Brs])
            P.op("dve", lambda e: e.reciprocal(out=rstd, in_=rstd), reads=[Brs], writes=[Brs])
            for c in range(8):
                j = c % 2
                P.op("dve", lambda e, c=c, j=j, tsl_=tsl_: e.tensor_tensor(out=tq[j], in0=ucv[:, c, tsl_], in1=mu, op=ALU.subtract), reads=[Bucv, Bmu], writes=[Btq[j]])
                P.op("dve", lambda e, j=j: e.tensor_tensor(out=tq[j], in0=tq[j], in1=rstd, op=ALU.mult), reads=[Btq[j], Brs], writes=[Btq[j]])
                P.op("act", lambda e, c=c, j=j: e.activation(out=ybt[j], in_=tq[j], func=AF.Silu, scale=pvs("lng", c), bias=pvs("lnb", c)), reads=[Btq[j], Bc], writes=[Bybt[j]])
                P.dma("sp", lambda e, c=c, j=j, tsl_=tsl_: e.dma_start(out=ybT_s[c][:, tsl_], in_=ybt[j]), reads=[Bybt[j]], writes=[Bybs], key="ybt_st%d" % j)
        for blk in range(8):
            s = loadw([(5120 + blk * 512, 512)])
            for q in range(4):
                g = blk * 4 + q
                for tb in range(2):
                    bank = nextbank()
                    mm16(bank, 512, s, q * 128, tb * 512)
                    g2 = bank % 2
                    P.op("act", lambda e, bank=bank, g2=g2: e.activation(out=sgt[g2], in_=psum[:, bank, :], func=AF.Sigmoid), reads=[pb[bank]], writes=[Bsg[g2]])
                    P.dma("sp", lambda e, g=g, tb=tb, g2=g2: e.dma_start(out=sg_s[g // 16, g % 16][:, tb * 512:(tb + 1) * 512], in_=sgt[g2]), reads=[Bsg[g2]], writes=[Bsgs], key="sgt_st%d" % g2)
        P.barrier()

    def phase_C(ci):
        A = Arena()
        Y = A.take(2 * 16 * 512, BF16).rearrange("p (a b c) -> p a b c", a=2, b=16)
        tsc = [A.take(16 * 128, BF16).rearrange("p (a b) -> p a b", b=128) for _ in range(2)]
        tss = [A.take(16 * 128, BF16).rearrange("p (a b) -> p a b", b=128) for _ in range(2)]
        kr = [A.take(512, F32) for _ in range(2)]
        ki = [A.take(512, F32) for _ in range(2)]
        t1, t2, t3, t4 = [A.take(512, F32) for _ in range(4)]
        rt = [A.take(32 * 512, BF16).rearrange("p (a b) -> p a b", b=512) for _ in range(2)]
        vt = [A.take(512, F32) for _ in range(2)]
        xg = [A.take(512, F32) for _ in range(2)]
        zt = [A.take(512, F32) for _ in range(2)]
        zb = [A.take(512, BF16) for _ in range(2)]
        assert A.off <= TAIL_OFF, A.off
        pb = PB()
        BY = Buf("Y")
        Btc = [Buf("tsc0"), Buf("tsc1")]
        Bts = [Buf("tss0"), Buf("tss1")]
        Bkr = [Buf("kr0"), Buf("kr1")]
        Bki = [Buf("ki0"), Buf("ki1")]
        Bt = [Buf("t1"), Buf("t2"), Buf("t3"), Buf("t4")]
        Brt = [Buf("rt0"), Buf("rt1")]
        Bvt = [Buf("vt0"), Buf("vt1")]
        Bxg = [Buf("xg0"), Buf("xg1")]
        Bzt = [Buf("zt0"), Buf("zt1")]
        Bzb = [Buf("zb0"), Buf("zb1")]
        Bzs, Byas = Buf("z_s"), Buf("yaT_s")
        fcnt = 0
        rcnt = 0
        ecnt = 0
        ntb = 4 if ci == 0 else 2

        def load_rt(r, tb):
            ts2 = slice(tb * 512, (tb + 1) * 512)

            def ld(e, r=r, ts2=ts2):
                return [e.dma_start(out=rt[r][:, part * 16:(part + 1) * 16, :], in_=tabR_d[part].rearrange("(fc p) t -> p fc t", p=128)[:, :, ts2]) for part in range(2)]
            P.dma("sp", ld, writes=[Brt[r]], n=2)

        for hh in range(2):
            hs = slice(hh * 512, (hh + 1) * 512)
            rslots = [(rcnt + t) % 2 for t in range(ntb)]
            rcnt += ntb
            load_rt(rslots[0], 0)
            for fc in range(16):
                s = fcnt % 2
                fcnt += 1
                P.dma("sp", lambda e, s=s, fc=fc: e.dma_start(out=tsc[s], in_=tabD_d[0][fc]), writes=[Btc[s]])
                P.dma("sp", lambda e, s=s, fc=fc: e.dma_start(out=tss[s], in_=tabD_d[1][fc]), writes=[Bts[s]])
                P.dma("sp", lambda e, s=s, fc=fc, hs=hs: e.dma_start(out=kr[s], in_=KT_s[ci, 0, fc * 128:(fc + 1) * 128, hs]), writes=[Bkr[s]])
                P.dma("sp", lambda e, s=s, fc=fc, hs=hs: e.dma_start(out=ki[s], in_=KT_s[ci, 1, fc * 128:(fc + 1) * 128, hs]), writes=[Bki[s]])
                bR, bI = 2 * s, 2 * s + 1
                for tc in range(16):
                    P.op("pe", lambda e, tc=tc, s=s, bR=bR, hs=hs: e.matmul(psum[:, bR, :], lhsT=tsc[s][:, tc, :], rhs=uT[:, tc, hs], start=(tc == 0), stop=(tc == 15)),
                         reads=[Btc[s], BuT[hh]], writes=[pb[bR]])
                for tc in range(16):
                    P.op("pe", lambda e, tc=tc, s=s, bI=bI, hs=hs: e.matmul(psum[:, bI, :], lhsT=tss[s][:, tc, :], rhs=uT[:, tc, hs], start=(tc == 0), stop=(tc == 15)),
                         reads=[Bts[s], BuT[hh]], writes=[pb[bI]])
                P.op("dve", lambda e, s=s, bR=bR: e.tensor_tensor(out=t1, in0=psum[:, bR, :], in1=kr[s], op=ALU.mult), reads=[pb[bR], Bkr[s]], writes=[Bt[0]])
                P.op("dve", lambda e, s=s, bI=bI: e.tensor_tensor(out=t2, in0=psum[:, bI, :], in1=ki[s], op=ALU.mult), reads=[pb[bI], Bki[s]], writes=[Bt[1]])
                P.op("dve", lambda e, s=s, bR=bR: e.tensor_tensor(out=t3, in0=psum[:, bR, :], in1=ki[s], op=ALU.mult), reads=[pb[bR], Bki[s]], writes=[Bt[2]])
                P.op("dve", lambda e, s=s, bI=bI: e.tensor_tensor(out=t4, in0=psum[:, bI, :], in1=kr[s], op=ALU.mult), reads=[pb[bI], Bkr[s]], writes=[Bt[3]])
                P.op("pool", lambda e, fc=fc: e.tensor_tensor(out=Y[:, 0, fc, :], in0=t1, in1=t2, op=ALU.subtract), reads=[Bt[0], Bt[1]], writes=[BY])
                P.op("pool", lambda e, fc=fc: e.tensor_tensor(out=Y[:, 1, fc, :], in0=t3, in1=t4, op=ALU.add), reads=[Bt[2], Bt[3]], writes=[BY])
            for tb in range(ntb):
                ts_ = slice(tb * 512, (tb + 1) * 512)
                if tb + 1 < ntb:
                    load_rt(rslots[tb + 1], tb + 1)
                r = rslots[tb]
                for c4 in range(4):
                    cc = hh * 4 + c4
                    j = ecnt % 2
                    ecnt += 1
                    bank = 4 + j
                    src_v = (v_s if ci == 0 else z_s)[cc][:, ts_]
                    src_x = (x1_s if ci == 0 else x2_s)[cc][:, ts_]
                    P.dma("sp", lambda e, j=j, src_v=src_v: e.dma_start(out=vt[j], in_=src_v), writes=[Bvt[j]])
                    P.dma("sp", lambda e, j=j, src_x=src_x: e.dma_start(out=xg[j], in_=src_x), writes=[Bxg[j]])
                    idx = 0
                    for part in range(2):
                        for fc in range(16):
                            P.op("pe", lambda e, part=part, fc=fc, c4=c4, r=r, bank=bank, idx=idx: e.matmul(psum[:, bank, :], lhsT=Y[:, part, fc, c4 * 128:(c4 + 1) * 128], rhs=rt[r][:, part * 16 + fc, :], start=(idx == 0), stop=(idx == 31)),
                                 reads=[BY, Brt[r]], writes=[pb[bank]])
                            idx += 1
                    P.op("dve", lambda e, j=j, cc=cc: e.tensor_scalar(out=vt[j], in0=vt[j], scalar1=pvs("hyb", ci * 8 + cc), scalar2=None, op0=ALU.mult), reads=[Bvt[j], Bc], writes=[Bvt[j]])
                    P.op("dve", lambda e, j=j, cc=cc, bank=bank: e.scalar_tensor_tensor(out=zt[j], in0=psum[:, bank, :], scalar=rl1s[:, ci * 8 + cc:ci * 8 + cc + 1], in1=vt[j], op0=ALU.mult, op1=ALU.add),
                         reads=[pb[bank], Brl1, Bvt[j]], writes=[Bzt[j]])
                    if ci == 0:
                        P.op("dve", lambda e, j=j: e.tensor_tensor(out=zt[j], in0=zt[j], in1=xg[j], op=ALU.mult), reads=[Bzt[j], Bxg[j]], writes=[Bzt[j]])
                        if tb < 2:
                            P.dma("sp", lambda e, j=j, cc=cc, ts_=ts_: e.dma_start(out=z_s[cc][:, ts_], in_=zt[j]), reads=[Bzt[j]], writes=[Bzs], key="zt_st%d" % j)
                        P.op("pool", lambda e, j=j: e.tensor_copy(out=zb[j], in_=zt[j]), reads=[Bzt[j]], writes=[Bzb[j]])
                        psT = psum[:, 6 + j, :].bitcast(BF16)[:, 0:512].rearrange("p (a b) -> p a b", b=128)
                        for i in range(4):
                            P.op("pe", lambda e, i=i, j=j, psT=psT: e.transpose(out=psT[:, i, :], in_=zb[j][:, i * 128:(i + 1) * 128], identity=identb[:]),
                                 reads=[Bzb[j], Bc], writes=[pb[6 + j]])
                        P.op("act", lambda e, psT=psT, tb=tb, cc=cc: e.copy(out=uT[:, tb * 4:(tb + 1) * 4, cc * 128:(cc + 1) * 128], in_=psT), reads=[pb[6 + j]], writes=[BuT[hh]])
                    else:
                        P.op("dve", lambda e, j=j: e.tensor_tensor(out=zb[j], in0=zt[j], in1=xg[j], op=ALU.mult), reads=[Bzt[j], Bxg[j]], writes=[Bzb[j]])
                        P.dma("sp", lambda e, j=j, cc=cc, ts_=ts_: e.dma_start(out=yaT_s[cc][:, ts_], in_=zb[j]), reads=[Bzb[j]], writes=[Byas], key="zb_st%d" % j)
        P.barrier()

    def phase_M():
        A = Arena()
        h2T = A.take(16 * 1024, BF16).rearrange("p (a b) -> p a b", b=1024)
        mT = A.take(16 * 1024, BF16).rearrange("p (a b) -> p a b", b=1024)
        wsl = [A.take(16 * 512, BF16).rearrange("p (a b) -> p a b", b=512) for _ in range(2)]
        mark = A.off
        yaT = A.take(8 * 1024, BF16).rearrange("p (a b) -> p a b", b=1024)
        ybT = A.take(8 * 1024, BF16).rearrange("p (a b) -> p a b", b=1024)
        ga = [A.take(512, F32) for _ in range(2)]
        gb = [A.take(512, F32) for _ in range(2)]
        m1 = [A.take(512, F32) for _ in range(2)]
        m2_ = [A.take(512, F32) for _ in range(2)]
        A.off = mark
        oT = A.take(16 * 512, F32).rearrange("p (a b) -> p a b", b=512)
        sq = A.take(16 * 512, BF16).rearrange("p (a b) -> p a b", b=512)
        rstd = A.take(512, F32)
        rstd2 = A.take(512, F32)
        xt = [A.take(512, F32) for _ in range(2)]
        tt = [A.take(512, F32) for _ in range(2)]
        pb = PB()
        Bya, Byb, BmT, Bh2 = Buf("yaT"), Buf("ybT"), Buf("mT"), Buf("h2T")
        Bws = [Buf("wslM0"), Buf("wslM1")]
        Bga = [Buf("ga0"), Buf("ga1")]
        Bgb = [Buf("gb0"), Buf("gb1")]
        Bm1 = [Buf("m10"), Buf("m11")]
        Bm2 = [Buf("m20"), Buf("m21")]
        BoT, Bsq, Brs, Brs2 = Buf("oT"), Buf("sqM"), Buf("rstdM"), Buf("rstdM2")
        Bxt = [Buf("xtM0"), Buf("xtM1")]
        Btt = [Buf("ttM0"), Buf("ttM1")]
        Br1s = Buf("r1T_s")
        P.dma("sp", lambda e: e.dma_start(out=yaT, in_=yaT_s.rearrange("c p t -> p c t")), writes=[Bya])
        P.dma("sp", lambda e: e.dma_start(out=ybT, in_=ybT_s.rearrange("c p t -> p c t")), writes=[Byb])
        wc = 0
        bc = 0
        for nb in range(4):
            s = wc % 2
            wc += 1

            def ldw(e, s=s, nb=nb):
                r = []
                for k in range(8):
                    r.append(e.dma_start(out=wsl[s][:, k, :], in_=hy_proj_d[k * 128:(k + 1) * 128, nb * 512:(nb + 1) * 512]))
                    r.append(e.dma_start(out=wsl[s][:, 8 + k, :], in_=cf_proj_d[k * 128:(k + 1) * 128, nb * 512:(nb + 1) * 512]))
                return r
            P.dma("pool", ldw, writes=[Bws[s]], n=16)
            for q in range(4):
                dch = nb * 4 + q
                for tb in range(2):
                    ts_ = slice(tb * 512, (tb + 1) * 512)
                    j = bc % 2
                    bc += 1
                    bA, bB = 2 * j, 2 * j + 1
                    P.dma("sp", lambda e, j=j, dch=dch, ts_=ts_: e.dma_start(out=ga[j], in_=sg_s[0, dch][:, ts_]), writes=[Bga[j]])
                    P.dma("sp", lambda e, j=j, dch=dch, ts_=ts_: e.dma_start(out=gb[j], in_=sg_s[1, dch][:, ts_]), writes=[Bgb[j]])
                    for k in range(8):
                        P.op("pe", lambda e, k=k, s=s, q=q, ts_=ts_, bA=bA: e.matmul(psum[:, bA, :], lhsT=wsl[s][:, k, q * 128:(q + 1) * 128], rhs=yaT[:, k, ts_], start=(k == 0), stop=(k == 7)),
                             reads=[Bws[s], Bya], writes=[pb[bA]])
                    for k in range(8):
                        P.op("pe", lambda e, k=k, s=s, q=q, ts_=ts_, bB=bB: e.matmul(psum[:, bB, :], lhsT=wsl[s][:, 8 + k, q * 128:(q + 1) * 128], rhs=ybT[:, k, ts_], start=(k == 0), stop=(k == 7)),
                             reads=[Bws[s], Byb], writes=[pb[bB]])
                    P.op("dve", lambda e, j=j, bA=bA: e.tensor_tensor(out=m1[j], in0=psum[:, bA, :], in1=ga[j], op=ALU.mult), reads=[pb[bA], Bga[j]], writes=[Bm1[j]])
                    P.op("dve", lambda e, j=j, bB=bB: e.tensor_tensor(out=m2_[j], in0=psum[:, bB, :], in1=gb[j], op=ALU.mult), reads=[pb[bB], Bgb[j]], writes=[Bm2[j]])
                    P.op("dve", lambda e, j=j, dch=dch, ts_=ts_: e.tensor_tensor(out=mT[:, dch, ts_], in0=m1[j], in1=m2_[j], op=ALU.add), reads=[Bm1[j], Bm2[j]], writes=[BmT])
        P.barrier()
        for tb in range(2):
            ts_ = slice(tb * 512, (tb + 1) * 512)
            for nb in range(4):
                s = wc % 2
                wc += 1
                P.dma("pool", lambda e, s=s, nb=nb: e.dma_start(out=wsl[s], in_=w_out_d[:, nb * 512:(nb + 1) * 512].rearrange("(k p) n -> p k n", p=128)), writes=[Bws[s]])
                for q in range(4):
                    nch = nb * 4 + q
                    bank = 4 + bc % 2
                    bc += 1
                    for k in range(16):
                        P.op("pe", lambda e, k=k, s=s, q=q, ts_=ts_, bank=bank: e.matmul(psum[:, bank, :], lhsT=wsl[s][:, k, q * 128:(q + 1) * 128], rhs=mT[:, k, ts_], start=(k == 0), stop=(k == 15)),
                             reads=[Bws[s], BmT], writes=[pb[bank]])
                    evac(oT[:, nch, :], psum[:, bank, :], [pb[bank]], [BoT])
                P.op("act", lambda e, nb=nb: e.activation(out=sq[:, nb * 4:(nb + 1) * 4, :], in_=oT[:, nb * 4:(nb + 1) * 4, :], func=AF.Square), reads=[BoT], writes=[Bsq])
            rms_stats(sq, 16, 512, 6, pb[6], Bsq, rstd, Brs)
            for nch in range(16):
                j = nch % 2
                P.dma("sp", lambda e, j=j, nch=nch, ts_=ts_: e.dma_start(out=xt[j], in_=xT_s[nch][:, ts_]), writes=[Bxt[j]])
                P.op("dve", lambda e, j=j, nch=nch: e.scalar_tensor_tensor(out=tt[j], in0=oT[:, nch, :], scalar=pvs("g2", nch), in1=rstd, op0=ALU.mult, op1=ALU.mult),
                     reads=[BoT, Brs, Bc], writes=[Btt[j]])
                P.op("dve", lambda e, j=j, nch=nch: e.tensor_tensor(out=oT[:, nch, :], in0=tt[j], in1=xt[j], op=ALU.add), reads=[Btt[j], Bxt[j], BoT], writes=[BoT])
            P.dma("sp", lambda e, ts_=ts_: e.dma_start(out=r1T_s[:, :, ts_].rearrange("c p t -> p c t"), in_=oT), reads=[BoT], writes=[Br1s], key="oT_st")
            for g4 in range(4):
                P.op("act", lambda e, g4=g4: e.activation(out=sq[:, g4 * 4:(g4 + 1) * 4, :], in_=oT[:, g4 * 4:(g4 + 1) * 4, :], func=AF.Square), reads=[BoT], writes=[Bsq])
            rms_stats(sq, 16, 512, 7, pb[7], Bsq, rstd2, Brs2)
            for nch in range(16):
                P.op("dve", lambda e, nch=nch, ts_=ts_: e.scalar_tensor_tensor(out=h2T[:, nch, ts_], in0=oT[:, nch, :], scalar=pvs("g3", nch), in1=rstd2, op0=ALU.mult, op1=ALU.mult),
                     reads=[BoT, Brs2, Bc], writes=[Bh2])
        P.barrier()

    def phase_G():
        A = Arena()
        h2T = A.take(16 * 1024, BF16).rearrange("p (a b) -> p a b", b=1024)
        wsl = [A.take(16 * 512, BF16).rearrange("p (a b) -> p a b", b=512) for _ in range(3)]
        st = [A.take(512, F32) for _ in range(2)]
        ab = [A.take(512, BF16) for _ in range(2)]
        pb = PB()
        Bh2 = Buf("h2T")
        Bws = [Buf("wslG%d" % i) for i in range(3)]
        Bst = [Buf("st0"), Buf("st1")]
        Bab = [Buf("ab0"), Buf("ab1")]
        Bas = Buf("act_s")
        bc = 0
        for blk in range(22):
            s = blk % 3

            def ldw(e, s=s, blk=blk):
                return [e.dma_start(out=wsl[s][:, :, 0:256], in_=w_gu_d[:, blk * 256:(blk + 1) * 256].rearrange("(k p) n -> p k n", p=128)),
                        e.dma_start(out=wsl[s][:, :, 256:512], in_=w_gu_d[:, FF + blk * 256:FF + (blk + 1) * 256].rearrange("(k p) n -> p k n", p=128))]
            P.dma("pool", ldw, writes=[Bws[s]], n=2)
            for i in range(2):
                kch = blk * 2 + i
                for tb in range(2):
                    ts_ = slice(tb * 512, (tb + 1) * 512)
                    j = bc % 2
                    bc += 1
                    bG, bU = 2 * j, 2 * j + 1
                    for k in range(16):
                        P.op("pe", lambda e, k=k, s=s, i=i, ts_=ts_, bG=bG: e.matmul(psum[:, bG, :], lhsT=wsl[s][:, k, i * 128:(i + 1) * 128], rhs=h2T[:, k, ts_], start=(k == 0), stop=(k == 15)),
                             reads=[Bws[s], Bh2], writes=[pb[bG]])
                    for k in range(16):
                        P.op("pe", lambda e, k=k, s=s, i=i, ts_=ts_, bU=bU: e.matmul(psum[:, bU, :], lhsT=wsl[s][:, k, 256 + i * 128:256 + (i + 1) * 128], rhs=h2T[:, k, ts_], start=(k == 0), stop=(k == 15)),
                             reads=[Bws[s], Bh2], writes=[pb[bU]])
                    P.op("act", lambda e, j=j, bG=bG: e.activation(out=st[j], in_=psum[:, bG, :], func=AF.Silu), reads=[pb[bG]], writes=[Bst[j]])
                    P.op("dve", lambda e, j=j, bU=bU: e.tensor_tensor(out=ab[j], in0=psum[:, bU, :], in1=st[j], op=ALU.mult), reads=[pb[bU], Bst[j]], writes=[Bab[j]])
                    P.dma("sp", lambda e, j=j, kch=kch, ts_=ts_: e.dma_start(out=act_s[kch][:, ts_], in_=ab[j]), reads=[Bab[j]], writes=[Bas], key="ab_st%d" % j)
        P.barrier()

    def phase_Dn():
        A = Arena()
        oT = A.take(16 * 1024, F32).rearrange("p (a b) -> p a b", b=1024)
        wd = [A.take(4 * 512, BF16).rearrange("p (a b) -> p a b", b=512) for _ in range(3)]
        ab = [A.take(4 * 1024, BF16).rearrange("p (a b) -> p a b", b=1024) for _ in range(3)]
        sq = A.take(16 * 512, BF16).rearrange("p (a b) -> p a b", b=512)
        rstd = A.take(512, F32)
        xt = [A.take(512, F32) for _ in range(2)]
        tt = [A.take(512, F32) for _ in range(2)]
        fo = A.take(16 * 512, F32).rearrange("p (a b) -> p a b", b=512)
        ot = [A.take(2048, F32) for _ in range(2)]
        pb = PB()
        BoT, Bsq, Brs, Bfo = Buf("oTD"), Buf("sqD"), Buf("rstdD"), Buf("fo")
        Bwd = [Buf("wd%d" % i) for i in range(3)]
        Bab = [Buf("abD%d" % i) for i in range(3)]
        Bxt = [Buf("xtD0"), Buf("xtD1")]
        Btt = [Buf("ttD0"), Buf("ttD1")]
        Bot = [Buf("ot0"), Buf("ot1")]
        Byy = Buf("yout")
        c3 = 0
        for ps_ in range(4):
            for kg in range(11):
                s = c3 % 3
                c3 += 1
                def ldwd(e, s=s, kg=kg, ps_=ps_):
                    return [e.dma_start(out=wd[s][:, k4, :], in_=w_down_d[(kg * 4 + k4) * 128:(kg * 4 + k4 + 1) * 128, ps_ * 512:(ps_ + 1) * 512]) for k4 in range(4)]
                P.dma("pool", ldwd, writes=[Bwd[s]], n=4)
                P.dma("sp", lambda e, s=s, kg=kg: e.dma_start(out=ab[s], in_=act_s[kg * 4:(kg + 1) * 4].rearrange("k p t -> p k t")), writes=[Bab[s]])
                for k4 in range(4):
                    for q in range(4):
                        for tb in range(2):
                            bank = q * 2 + tb
                            P.op("pe", lambda e, s=s, k4=k4, q=q, tb=tb, bank=bank, kg=kg: e.matmul(psum[:, bank, :], lhsT=wd[s][:, k4, q * 128:(q + 1) * 128], rhs=ab[s][:, k4, tb * 512:(tb + 1) * 512], start=(kg == 0 and k4 == 0), stop=(kg == 10 and k4 == 3)),
                                 reads=[Bwd[s], Bab[s]], writes=[pb[bank]])
            for q in range(4):
                for tb in range(2):
                    bank = q * 2 + tb
                    evac(oT[:, ps_ * 4 + q, tb * 512:(tb + 1) * 512], psum[:, bank, :], [pb[bank]], [BoT])
        ocnt = 0
        for tb in range(2):
            ts_ = slice(tb * 512, (tb + 1) * 512)
            for g4 in range(4):
                P.op("act", lambda e, g4=g4, ts_=ts_: e.activation(out=sq[:, g4 * 4:(g4 + 1) * 4, :], in_=oT[:, g4 * 4:(g4 + 1) * 4, ts_], func=AF.Square), reads=[BoT], writes=[Bsq])
            rms_stats(sq, 16, 512, tb, pb[tb], Bsq, rstd, Brs)
            for nch in range(16):
                j = nch % 2
                P.dma("sp", lambda e, j=j, nch=nch, ts_=ts_: e.dma_start(out=xt[j], in_=r1T_s[nch][:, ts_]), writes=[Bxt[j]])
                P.op("dve", lambda e, j=j, nch=nch, ts_=ts_: e.scalar_tensor_tensor(out=tt[j], in0=oT[:, nch, ts_], scalar=pvs("g4", nch), in1=rstd, op0=ALU.mult, op1=ALU.mult),
                     reads=[BoT, Brs, Bc], writes=[Btt[j]])
                P.op("dve", lambda e, j=j, nch=nch: e.tensor_tensor(out=fo[:, nch, :], in0=tt[j], in1=xt[j], op=ALU.add), reads=[Btt[j], Bxt[j]], writes=[Bfo])
            for t4 in range(4):
                o2 = ocnt % 2
                ocnt += 1
                for g4 in range(4):
                    bank = 2 + (t4 * 4 + g4) % 4
                    for q in range(4):
                        nch = g4 * 4 + q
                        P.op("pe", lambda e, bank=bank, q=q, nch=nch, t4=t4: e.transpose(out=psum[:, bank, q * 128:(q + 1) * 128], in_=fo[:, nch, t4 * 128:(t4 + 1) * 128], identity=identf[:]),
                             reads=[Bfo, Bc], writes=[pb[bank]])
                    evac(ot[o2][:, g4 * 512:(g4 + 1) * 512], psum[:, bank, :], [pb[bank]], [Bot[o2]])
                row = tb * 512 + t4 * 128
                P.dma("sp", lambda e, o2=o2, row=row: e.dma_start(out=y_d[row:row + 128, :], in_=ot[o2]), reads=[Bot[o2]], writes=[Byy], key="ot_st%d" % o2)
        P.op("sp", None, reads=[Byy])
        P.barrier()

    if "N1" in stages:
        phase_N1()
    if "P" in stages:
        phase_P()
    if "C" in stages:
        phase_C(0)
        phase_C(1)
    if "M" in stages:
        phase_M()
    if "G" in stages:
        phase_G()
    if "Dn" in stages:
        phase_Dn()

    if "Dn" not in stages and "F" in stages:
        By = Buf("y")
        Bfin = Buf("fin")
        fin = nc.alloc_sbuf_tensor("fin", [128, 16], F32)
        P.op("dve", lambda e: e.tensor_copy(out=fin[:], in_=rl1s[:]), reads=[Brl1], writes=[Bfin])
        P.dma("sp", lambda e: e.dma_start(out=y_d[0:128, 0:16], in_=fin[:]), reads=[Bfin], writes=[By], key="ystore")
        P.op("sp", None, reads=[By])
    if dbg is not None:
        srcs = {"KT": lambda: KT_s[0, 1], "KT2": lambda: KT_s[1, 0],
                "v": lambda: v_s.rearrange("c p t -> (c p) t"), "x1": lambda: x1_s.rearrange("c p t -> (c p) t"),
                "x2": lambda: x2_s.rearrange("c p t -> (c p) t"), "z": lambda: z_s.rearrange("c p t -> (c p) t"),
                "xT": lambda: xT_s.rearrange("c p t -> (c p) t"), "r1": lambda: r1T_s.rearrange("c p t -> (c p) t"),
                "sg": lambda: sg_s[0].rearrange("c p t -> (c p) t")}
        Bdbg = Buf("dbg")
        P.dma("sp", lambda e: e.dma_start(out=dbg_d, in_=srcs[dbg_name]()), writes=[Bdbg], key="dbgst")
        P.op("sp", None, reads=[Bdbg])
    P.emit()
    return nc


def core_inputs(inp, b, rev):
    C = consts()
    d = {}
    xb = inp["x"][b]
    d["x"] = np.ascontiguousarray(xb[::-1] if rev else xb)
    d["pvec"] = make_pvec(inp, rev)
    d["w_in"] = inp["w_in"][0]
    d["hy_proj"] = inp["hy_proj"][0]
    d["cf_proj"] = inp["cf_proj"][0]
    d["w_out"] = inp["w_out"][0]
    d["w_gu"] = inp["ffn_w_gu"][0]
    d["w_down"] = inp["ffn_w_down"][0]
    d["fw1"] = inp["hy_filt_w1"][0]
    fm = np.zeros((64, 68), np.float32)
    fm[:, 0:64] = inp["hy_filt_w2"][0]
    fm[:, 64] = inp["hy_filt_b1"][0]
    fm[:, 65] = inp["hy_filt_fr1"][0]
    fm[:, 66] = inp["hy_filt_b2"][0]
    fm[:, 67] = inp["hy_filt_fr2"][0]
    d["fmlp"] = fm
    w3 = inp["hy_filt_w3"][0]
    if rev:
        w3 = w3.reshape(64, 2, 2, 1024)[:, :, ::-1].reshape(64, 4096)
    d["fw3"] = np.ascontiguousarray(w3)
    for k in ["zT", "tabF_c", "tabF_s", "tabD_c", "tabD_s", "tabR", "ident_bf", "ident_f", "ones_bf", "tau_row", "adel_row"]:
        d[k] = C[k]
    return d


_NC = {}


def kernel(**inputs):
    inp = {k: np.asarray(v, dtype=np.float32) for k, v in inputs.items()}
    if "nc" not in _NC:
        _NC["nc"] = build()
    nc = _NC["nc"]
    in_maps = []
    for c in range(8):
        b, j = c // 2, c % 2
        in_maps.append(core_inputs(inp, b, j == 1))
    res = run_bass_kernel_spmd(nc, in_maps, core_ids=list(range(8)))
    out = np.zeros((4, 2048, 2048), np.float32)
    for c in range(8):
        b, j = c // 2, c % 2
        y = np.asarray(res.results[c]["y"], dtype=np.float32)
        if j == 0:
            out[b, 0:1024] = y
        else:
            out[b, 1024:2048] = y[::-1]
    return out
```

```python
import math
import numpy as np
import ml_dtypes
import concourse.bass as bass
import concourse.mybir as mybir
from concourse.bass_utils import run_bass_kernel_spmd

F32 = mybir.dt.float32
BF16 = mybir.dt.bfloat16
AF = mybir.ActivationFunctionType
ALU = mybir.AluOpType
AX = mybir.AxisListType

D = 2048
L = 2048
T1 = 1024
HWID = 1024
NIN = 9216
FF = 5632
EPS = 1e-6
NPV = 480
HY_MIN_DECAY = math.log(1e-2) / 1.5
HY_MAX_DECAY = math.log(1e-2) / 0.3


class Buf:
    __slots__ = ("name", "last_w", "readers")

    def __init__(self, name=""):
        self.name = name
        self.last_w = None
        self.readers = []


class Op:
    __slots__ = ("eng", "fn", "deps", "signal", "sig", "is_dma", "sem", "semval", "idx", "key")


class Prog:
    ENG_BLOCK = {"pe": "tensor", "act": "scalar", "dve": "vector", "pool": "gpsimd", "sp": "sync"}

    def __init__(self, nc):
        self.nc = nc
        self.ops = {k: [] for k in self.ENG_BLOCK}
        self.prog_sem = {k: nc.alloc_semaphore(name="prog_" + k) for k in self.ENG_BLOCK}
        self.dma_sems = {}
        self.nops = 0
        self.last_compute = {}
        self.phase_dmas = {}

    def _deps(self, o, reads, writes):
        deps = []

        def add(d):
            if d is None or d is o:
                return
            if (not d.is_dma) and (not o.is_dma) and d.eng == "pe" and o.eng == "pe":
                return
            for x in deps:
                if x is d:
                    return
            deps.append(d)

        for b in reads:
            add(b.last_w)
        for b in writes:
            add(b.last_w)
            for r in b.readers:
                add(r)
        return deps

    def _commit(self, o, reads, writes):
        for b in reads:
            if not o.is_dma:
                b.readers = [r for r in b.readers if r.is_dma or r.eng != o.eng]
            b.readers.append(o)
        for b in writes:
            b.last_w = o
            b.readers = []
        for d in o.deps:
            d.signal = True
        self.ops[o.eng].append(o)

    def op(self, eng, fn, reads=(), writes=()):
        o = Op()
        o.eng, o.fn, o.signal, o.sig, o.is_dma, o.sem, o.semval = eng, fn, False, 0, False, None, 0
        o.idx = self.nops
        self.nops += 1
        o.deps = self._deps(o, reads, writes)
        self._commit(o, reads, writes)
        if fn is not None:
            self.last_compute[eng] = o
        return o

    def dma(self, queue, fn, reads=(), writes=(), key=None, n=1):
        o = Op()
        o.eng, o.fn, o.signal, o.sig, o.is_dma = queue, fn, False, 0, True
        o.idx = self.nops
        self.nops += 1
        o.deps = self._deps(o, reads, writes)
        if key is None:
            key = writes[0].name
        if key not in self.dma_sems:
            self.dma_sems[key] = [self.nc.alloc_semaphore(name="d_" + key), 0]
        ent = self.dma_sems[key]
        ent[1] += 16 * n
        o.sem, o.semval, o.key = ent[0], ent[1], key
        self._commit(o, reads, writes)
        self.phase_dmas[key] = o
        return o

    def barrier(self):
        lasts = dict(self.last_compute)
        dmas = list(self.phase_dmas.values())
        for e in self.ENG_BLOCK:
            o = Op()
            o.eng, o.fn, o.signal, o.sig, o.is_dma, o.sem, o.semval = e, None, False, 0, False, None, 0
            o.idx = self.nops
            self.nops += 1
            o.deps = [v for k, v in lasts.items() if k != e] + dmas
            for d in o.deps:
                d.signal = True
            self.ops[e].append(o)
        self.phase_dmas = {}

    def emit(self):
        nc = self.nc
        for e, lst in self.ops.items():
            c = 0
            for o in lst:
                if o.is_dma:
                    continue
                if o.signal:
                    c += 1
                    o.sig = c
        with nc.Block() as block:
            for ename, bname in self.ENG_BLOCK.items():
                if not self.ops[ename]:
                    continue
                deco = getattr(block, bname)

                def body(eng, ename=ename):
                    waited = {}
                    mysem = self.prog_sem[ename]
                    for o in self.ops[ename]:
                        for d in o.deps:
                            if d.is_dma:
                                sem, val = d.sem, d.semval
                            else:
                                sem, val = self.prog_sem[d.eng], d.sig
                            k = id(sem)
                            if waited.get(k, 0) < val:
                                eng.wait_ge(sem, val)
                                waited[k] = val
                        if o.fn is None:
                            continue
                        r = o.fn(eng)
                        if o.is_dma:
                            if not isinstance(r, (list, tuple)):
                                r = [r]
                            for ins in r:
                                ins.then_inc(o.sem, 16)
                        elif o.signal:
                            r.then_inc(mysem, 1)

                deco(body)


_CONST = {}


def _lhsT_layout(M):
    A = M.reshape(16, 128, 16, 128)
    return np.ascontiguousarray(A.transpose(2, 1, 0, 3))


def consts():
    if _CONST:
        return _CONST
    bf = ml_dtypes.bfloat16
    n = np.arange(2048, dtype=np.float64)
    phi = math.pi / 4096.0
    th = phi * np.outer(n, 2 * n + 1)
    _CONST["tabF_c"] = _lhsT_layout(np.cos(th)).astype(bf)
    _CONST["tabF_s"] = _lhsT_layout(-np.sin(th)).astype(bf)
    ps = (phi / 2) * np.outer(2 * n + 1, 2 * n + 1)
    Ct = np.cos(ps)
    St = -np.sin(ps)
    _CONST["tabD_c"] = _lhsT_layout(Ct).astype(bf)
    _CONST["tabD_s"] = _lhsT_layout(St).astype(bf)
    _CONST["tabR"] = np.ascontiguousarray(np.stack([Ct, St], 0)).astype(bf)
    _CONST["ident_bf"] = np.eye(128).astype(bf)
    _CONST["ident_f"] = np.eye(128).astype(np.float32)
    _CONST["ones_bf"] = np.ones((128, 128)).astype(bf)
    _CONST["tau_row"] = n.astype(np.float32)[None, :]
    deltas = np.linspace(HY_MIN_DECAY, HY_MAX_DECAY, HWID, dtype=np.float32)
    _CONST["adel_row"] = np.abs(deltas)[None, :].astype(np.float32)
    f32 = np.float32
    t = np.linspace(0.0, 1.0, L, dtype=f32)[:, None]
    w = (2.0 * math.pi * np.arange(L, dtype=f32)[:, None] / L).astype(f32)
    f = np.linspace(1e-4, 15, 16, dtype=f32)[None, :]
    z = np.concatenate([t, np.cos(f * w), -np.sin(f * w)], axis=-1).astype(f32)
    _CONST["zT"] = np.ascontiguousarray(z.T)
    negt = -(np.arange(2048, dtype=np.float64) / (L - 1))
    _CONST["negt"] = negt.reshape(16, 128).T.astype(np.float32)
    nd = -(np.abs(deltas).astype(np.float64) / (L - 1))
    _CONST["negdel"] = nd.reshape(8, 128).T.astype(np.float32)
    return _CONST


def pm(v, nch):
    return np.ascontiguousarray(np.asarray(v).reshape(nch, 128).T)


def make_pvec(inp, rev):
    C = consts()
    pv = np.zeros((128, NPV), np.float32)
    pv[:, 0:16] = pm(inp["mix_pre_g"][0], 16)
    pv[:, 16:32] = pm(inp["mix_post_g"][0], 16)
    pv[:, 32:48] = pm(inp["ffn_pre_g"][0], 16)
    pv[:, 48:64] = pm(inp["ffn_post_g"][0], 16)
    hcw = inp["hy_conv_w"][0]
    if rev:
        hcw = hcw[::-1]
    pv[:, 64:136] = hcw.T.reshape(24, 128, 3).transpose(1, 0, 2).reshape(128, 72)
    pv[:, 136:160] = pm(inp["hy_conv_b"][0], 24)
    cdw = inp["cf_dw_w"][0]
    if rev:
        cdw = cdw[::-1]
    pv[:, 160:408] = cdw.T.reshape(8, 128, 31).transpose(1, 0, 2).reshape(128, 248)
    pv[:, 408:416] = pm(inp["cf_dw_b"][0], 8)
    pv[:, 416:424] = pm(inp["cf_ln_g"][0], 8)
    pv[:, 424:432] = pm(inp["cf_ln_b"][0], 8)
    hb = inp["hy_bias"][0]
    pv[:, 432:448] = hb.reshape(2, 8, 128).transpose(2, 0, 1).reshape(128, 16)
    pv[:, 448] = 0.0 if rev else 1.0
    pv[:, 449] = 1.0 if rev else 0.0
    pv[:, 450:466] = C["negt"]
    pv[:, 466:474] = C["negdel"]
    return pv


PV = dict(g1=0, g2=16, g3=32, g4=48, hcw=64, hcb=136, cdw=160, cdb=408, lng=416, lnb=424, hyb=432,
          flags=448, negt=450, negdel=466)


def build(stages=("F", "N1", "P", "C", "M", "G", "Dn"), dbg=None):
    nc = bass.Bass("TRN2", target_bir_lowering=False)
    P = Prog(nc)

    def din(n, s, dt=F32):
        return nc.dram_tensor(n, list(s), dt, kind="ExternalInput").ap()

    def dscr(n, s, dt=F32):
        return nc.dram_tensor(n, list(s), dt).ap()

    x_d = din("x", [2048, 2048])
    pvec_d = din("pvec", [128, NPV])
    w_in_d = din("w_in", [D, NIN])
    hy_proj_d = din("hy_proj", [HWID, D])
    cf_proj_d = din("cf_proj", [HWID, D])
    w_out_d = din("w_out", [D, D])
    w_gu_d = din("w_gu", [D, 2 * FF])
    w_down_d = din("w_down", [FF, D])
    w1_d = din("fw1", [33, 64])
    fm_d = din("fmlp", [64, 68])
    w3_d = din("fw3", [64, 4096])
    zT_d = din("zT", [33, 2048])
    tabF_d = [din("tabF_c", [16, 128, 16, 128], BF16), din("tabF_s", [16, 128, 16, 128], BF16)]
    tabD_d = [din("tabD_c", [16, 128, 16, 128], BF16), din("tabD_s", [16, 128, 16, 128], BF16)]
    tabR_d = din("tabR", [2, 2048, 2048], BF16)
    identb_d = din("ident_bf", [128, 128], BF16)
    identf_d = din("ident_f", [128, 128])
    onesb_d = din("ones_bf", [128, 128], BF16)
    tau_d = din("tau_row", [1, 2048])
    adel_d = din("adel_row", [1, 1024])
    y_d = nc.dram_tensor("y", [T1, D], F32, kind="ExternalOutput").ap()
    dbg_d = None
    dbg_name = None
    if dbg is not None:
        dbg_name, dshape = dbg
        dbg_d = nc.dram_tensor("dbg", list(dshape), F32, kind="ExternalOutput").ap()

    KT_s = dscr("KT_s", [2, 2, 2048, 1024])
    xT_s = dscr("xT_s", [16, 128, T1])
    v_s = dscr("v_s", [8, 128, 2048])
    x1_s = dscr("x1_s", [8, 128, 2048])
    x2_s = dscr("x2_s", [8, 128, T1])
    yaT_s = dscr("yaT_s", [8, 128, T1], BF16)
    ybT_s = dscr("ybT_s", [8, 128, T1], BF16)
    sg_s = dscr("sg_s", [2, 16, 128, T1])
    r1T_s = dscr("r1T_s", [16, 128, T1])
    act_s = dscr("act_s", [44, 128, T1], BF16)

    pv = nc.alloc_sbuf_tensor("pv", [128, NPV], F32)
    identb = nc.alloc_sbuf_tensor("identb", [128, 128], BF16)
    identf = nc.alloc_sbuf_tensor("identf", [128, 128], F32)
    onesb = nc.alloc_sbuf_tensor("onesb", [128, 128], BF16)
    rl1s = nc.alloc_sbuf_tensor("rl1s", [128, 16], F32)
    epsc = nc.alloc_sbuf_tensor("epsc", [128, 1], F32)
    zeroc = nc.alloc_sbuf_tensor("zeroc", [128, 1], F32)
    ARENA_BYTES = 200 * 1024
    arena = nc.alloc_sbuf_tensor("arena", [128, ARENA_BYTES // 2], BF16)
    psum = nc.alloc_psum_tensor("psum", [128, 8, 512], F32)
    Bc = Buf("const")
    Brl1 = Buf("rl1s")

    class Arena:
        def __init__(self):
            self.off = 0

        def take(self, nelem, dt):
            sz = 2 if dt == BF16 else 4
            nb = (nelem * sz + 63) // 64 * 64
            assert self.off + nb <= ARENA_BYTES, (self.off, nb)
            v = arena[:, self.off // 2:(self.off + nb) // 2]
            self.off += nb
            if dt == F32:
                v = v.bitcast(F32)
            return v[:, 0:nelem]

    def pvs(name, i=0, n=1):
        o = PV[name] + i
        return pv[:, o:o + n]

    def PB():
        return [Buf("pb%d" % i) for i in range(8)]

    P.dma("sp", lambda e: e.dma_start(out=pv[:], in_=pvec_d), writes=[Bc], key="c0")
    P.dma("sp", lambda e: e.dma_start(out=identb[:], in_=identb_d), writes=[Bc], key="c1")
    P.dma("sp", lambda e: e.dma_start(out=identf[:], in_=identf_d), writes=[Bc], key="c2")
    P.dma("sp", lambda e: e.dma_start(out=onesb[:], in_=onesb_d), writes=[Bc], key="c3")
    P.op("dve", lambda e: e.memset(epsc[:], EPS), writes=[Bc])
    P.op("dve", lambda e: e.memset(zeroc[:], 0.0), writes=[Bc])
    P.barrier()
    Bc = Buf("const")

    def phase_F():
        A = Arena()
        zT = A.take(2048, F32)
        h1T = A.take(2048, F32)
        h2T = A.take(2048, BF16)
        w3f = A.take(4096, F32)
        w3 = A.take(4096, BF16)
        w3e = A.take(2048, BF16)
        w3d = A.take(2048, BF16)
        w1 = A.take(64, F32)
        fm = A.take(68, F32)
        cc12 = A.take(2, F32)
        tau_bc = A.take(2048, F32)
        adel_bc = A.take(1024, F32)
        arg = [A.take(512, F32) for _ in range(2)]
        l1p = A.take(128, F32)
        l1q = A.take(32, F32)
        l1t = A.take(16, F32)
        junk = A.take(512, F32)
        wincm = A.take(2048, F32)
        wintm = [A.take(1024, F32) for _ in range(2)]
        tmpf = [A.take(512, F32) for _ in range(2)]
        tmpb = [A.take(512, F32) for _ in range(2)]
        tmpa = [A.take(512, F32) for _ in range(2)]
        ke = A.take(16 * 1024, BF16).rearrange("p (a b) -> p a b", b=1024)
        kd = A.take(16 * 1024, BF16).rearrange("p (a b) -> p a b", b=1024)
        tsl = [A.take(16 * 128, BF16).rearrange("p (a b) -> p a b", b=128) for _ in range(4)]
        ksb = [A.take(1024, F32) for _ in range(2)]
        pb = PB()
        Bz, Bw = Buf("Fz"), Buf("Fw")
        Bh1, Bh2 = Buf("h1"), Buf("h2")
        Barg = [Buf("arg0"), Buf("arg1")]
        Bjk = Buf("jk")
        P.dma("sp", lambda e: e.dma_start(out=zT[0:33, :], in_=zT_d), writes=[Bz], key="f0")
        P.dma("sp", lambda e: e.dma_start(out=w1[0:33, :], in_=w1_d), writes=[Bw], key="f1")
        P.dma("sp", lambda e: e.dma_start(out=fm[0:64, :], in_=fm_d), writes=[Bw], key="f2")
        Bw3f = Buf("w3f")
        P.dma("sp", lambda e: e.dma_start(out=w3f[0:64, :], in_=w3_d), writes=[Bw3f], key="f3")
        P.op("dve", lambda e: e.memset(w3, 0.0), writes=[Bw])
        P.op("dve", lambda e: e.tensor_copy(out=w3[0:64, :], in_=w3f[0:64, :]), reads=[Bw3f, Bw], writes=[Bw])
        P.op("dve", lambda e: e.memset(h2T, 0.0), writes=[Bh2])
        Bw3ed = Buf("w3ed")
        P.op("dve", lambda e: e.memset(w3e, 0.0), writes=[Bw3ed])
        P.op("dve", lambda e: e.memset(w3d, 0.0), writes=[Bw3ed])
        for o_ in range(2):
            cf_, cb_2 = (2 * o_) * 1024, (2 * o_ + 1) * 1024
            P.op("dve", lambda e, o_=o_, cf_=cf_, cb_2=cb_2: e.tensor_tensor(out=w3e[0:64, o_ * 1024:(o_ + 1) * 1024], in0=w3f[0:64, cf_:cf_ + 1024], in1=w3f[0:64, cb_2:cb_2 + 1024], op=ALU.add),
                 reads=[Bw3f, Bw3ed], writes=[Bw3ed])
            P.op("dve", lambda e, o_=o_, cf_=cf_, cb_2=cb_2: e.tensor_tensor(out=w3d[0:64, o_ * 1024:(o_ + 1) * 1024], in0=w3f[0:64, cf_:cf_ + 1024], in1=w3f[0:64, cb_2:cb_2 + 1024], op=ALU.subtract),
                 reads=[Bw3f, Bw3ed], writes=[Bw3ed])
        P.dma("sp", lambda e: e.dma_start(out=tau_bc, in_=tau_d.partition_broadcast(128)), writes=[Bw], key="f4")
        P.dma("sp", lambda e: e.dma_start(out=adel_bc, in_=adel_d.partition_broadcast(128)), writes=[Bw], key="f5")
        Bcc = Buf("cc")
        P.op("dve", lambda e: e.tensor_tensor(out=cc12[0:64, 0:1], in0=fm[0:64, 64:65], in1=fm[0:64, 65:66], op=ALU.mult),
             reads=[Bw], writes=[Bcc])
        P.op("dve", lambda e: e.tensor_tensor(out=cc12[0:64, 1:2], in0=fm[0:64, 66:67], in1=fm[0:64, 67:68], op=ALU.mult),
             reads=[Bw, Bcc], writes=[Bcc])
        cnt = 0
        for layer in range(2):
            src = zT if layer == 0 else h1T
            dst = h1T if layer == 0 else h2T
            Bs = Bz if layer == 0 else Bh1
            Bd = Bh1 if layer == 0 else Bh2
            kk = 33 if layer == 0 else 64
            wl = w1[0:33, 0:64] if layer == 0 else fm[0:64, 0:64]
            frc = fm[0:64, 65:66] if layer == 0 else fm[0:64, 67:68]
            cb_ = cc12[0:64, layer:layer + 1]
            for tb in range(4):
                bk = cnt % 2
                cnt += 1
                ts = slice(tb * 512, (tb + 1) * 512)
                P.op("pe", lambda e, bk=bk, ts=ts, src=src, kk=kk, wl=wl: e.matmul(psum[0:64, bk, :], lhsT=wl, rhs=src[0:kk, ts], start=True, stop=True),
                     reads=[Bs, Bw], writes=[pb[bk]])
                P.op("act", lambda e, bk=bk, frc=frc, cb_=cb_: e.activation(out=arg[bk][0:64, :], in_=psum[0:64, bk, :], func=AF.Identity, scale=frc, bias=cb_),
                     reads=[pb[bk], Bw, Bcc], writes=[Barg[bk]])
                for _ in range(2):
                    for (thr, cmp_, per) in ((math.pi, ALU.is_gt, -2 * math.pi), (-math.pi, ALU.is_lt, 2 * math.pi)):
                        P.op("dve", lambda e, bk=bk, thr=thr, cmp_=cmp_, per=per: e.tensor_scalar(out=junk[0:64, :], in0=arg[bk][0:64, :], scalar1=thr, scalar2=per, op0=cmp_, op1=ALU.mult),
                             reads=[Barg[bk]], writes=[Bjk])
                        P.op("dve", lambda e, bk=bk: e.tensor_tensor(out=arg[bk][0:64, :], in0=arg[bk][0:64, :], in1=junk[0:64, :], op=ALU.add),
                             reads=[Barg[bk], Bjk], writes=[Barg[bk]])
                P.op("act", lambda e, bk=bk, dst=dst, ts=ts: e.activation(out=dst[0:64, ts], in_=arg[bk][0:64, :], func=AF.Sin),
                     reads=[Barg[bk]], writes=[Bd])
        Bwin, Bl1p = Buf("wincm"), Buf("l1p")
        Btmp = [Buf("tmpa0"), Buf("tmpa1")]
        Bjunk = Buf("junk")
        cnt = 0
        for cc in range(8):
            P.op("act", lambda e, cc=cc: e.activation(out=wincm, in_=tau_bc, func=AF.Exp, scale=pvs("negdel", cc)),
                 reads=[Bw, Bc], writes=[Bwin])
            for o in range(2):
                for dr in range(2):
                    q = o * 16 + dr * 8 + cc
                    for tb in range(4):
                        bk = 2 + cnt % 2
                        tm = cnt % 2
                        cnt += 1
                        ts = slice(tb * 512, (tb + 1) * 512)
                        P.op("pe", lambda e, bk=bk, q=q, ts=ts: e.matmul(psum[:, bk, :], lhsT=w3[:, q * 128:(q + 1) * 128], rhs=h2T[:, ts], start=True, stop=True),
                             reads=[Bw, Bh2], writes=[pb[bk]])
                        P.op("dve", lambda e, bk=bk, tm=tm, ts=ts: e.tensor_tensor(out=tmpa[tm], in0=psum[:, bk, :], in1=wincm[:, ts], op=ALU.mult),
                             reads=[pb[bk], Bwin], writes=[Btmp[tm]])
                        if tb == 0:
                            P.op("dve", lambda e, tm=tm, dr=dr: e.tensor_scalar(out=tmpa[tm][:, 0:1], in0=tmpa[tm][:, 0:1], scalar1=pvs("flags", dr), scalar2=None, op0=ALU.mult),
                                 reads=[Btmp[tm], Bc], writes=[Btmp[tm]])
                        P.op("act", lambda e, tm=tm, q=q, tb=tb: e.activation(out=junk, in_=tmpa[tm], func=AF.Abs, accum_out=l1p[:, q * 4 + tb:q * 4 + tb + 1]),
                             reads=[Btmp[tm]], writes=[Bjunk, Bl1p])
        P.op("dve", lambda e: e.tensor_reduce(out=l1q, in_=l1p.rearrange("p (a b) -> p a b", b=4), axis=AX.X, op=ALU.add),
             reads=[Bl1p], writes=[Bl1p])
        for o in range(2):
            P.op("dve", lambda e, o=o: e.tensor_tensor(out=l1t[:, o * 8:(o + 1) * 8], in0=l1q[:, o * 16:o * 16 + 8], in1=l1q[:, o * 16 + 8:o * 16 + 16], op=ALU.add),
                 reads=[Bl1p], writes=[Bl1p])
        P.op("dve", lambda e: e.reciprocal(out=l1t, in_=l1t), reads=[Bl1p], writes=[Bl1p])
        P.op("dve", lambda e: e.tensor_scalar(out=rl1s[:], in0=l1t, scalar1=1.0 / 2048.0, scalar2=None, op0=ALU.mult),
             reads=[Bl1p], writes=[Brl1])
        Bwt = [Buf("wintm0"), Buf("wintm1")]
        Btf = [Buf("tmpf0"), Buf("tmpf1")]
        Btb = [Buf("tmpb0"), Buf("tmpb1")]
        Bke, Bkd = Buf("ke"), Buf("kd")
        Bts = [Buf("tsl%d" % i) for i in range(4)]
        Bks = [Buf("ksb0"), Buf("ksb1")]
        BKT = Buf("KT")
        cnt = 0
        tcnt = 0
        kcnt = 0
        iters = [(part, fc) for part in range(2) for fc in range(16)]
        PF = 3
        tbase = 0
        for o in range(2):
            def issue_tsl(i, tbase=tbase):
                part_, fc_ = iters[i]
                s3_ = (tbase + i) % 4
                P.dma("sp", lambda e, s3_=s3_, part_=part_, fc_=fc_: e.dma_start(out=tsl[s3_], in_=tabF_d[part_][fc_]), writes=[Bts[s3_]])
            for i in range(PF):
                issue_tsl(i)
            for tc in range(16):
                wi = tc % 2
                P.op("act", lambda e, wi=wi, tc=tc: e.activation(out=wintm[wi], in_=adel_bc, func=AF.Exp, scale=pvs("negt", tc)),
                     reads=[Bw, Bc], writes=[Bwt[wi]])
                for half in range(2):
                    colf = (o * 2 + 0) * 1024 + half * 512
                    colb = (o * 2 + 1) * 1024 + half * 512
                    i2 = cnt % 2
                    cnt += 1
                    bF, bB = 4 + 2 * i2, 5 + 2 * i2
                    tsl_ = slice(tc * 128, (tc + 1) * 128)
                    hs = slice(half * 512, (half + 1) * 512)
                    if tc >= 1:
                        colE = o * 1024 + half * 512
                        P.op("pe", lambda e, bF=bF, tsl_=tsl_, colE=colE: e.matmul(psum[:, bF, :], lhsT=h2T[:, tsl_], rhs=w3e[:, colE:colE + 512], start=True, stop=True),
                             reads=[Bw3ed, Bh2], writes=[pb[bF]])
                        P.op("pe", lambda e, bB=bB, tsl_=tsl_, colE=colE: e.matmul(psum[:, bB, :], lhsT=h2T[:, tsl_], rhs=w3d[:, colE:colE + 512], start=True, stop=True),
                             reads=[Bw3ed, Bh2], writes=[pb[bB]])
                        P.op("dve", lambda e, bF=bF, tc=tc, wi=wi, hs=hs: e.tensor_tensor(out=ke[:, tc, hs], in0=psum[:, bF, :], in1=wintm[wi][:, hs], op=ALU.mult),
                             reads=[pb[bF], Bwt[wi]], writes=[Bke])
                        P.op("dve", lambda e, bB=bB, tc=tc, wi=wi, hs=hs: e.tensor_tensor(out=kd[:, tc, hs], in0=psum[:, bB, :], in1=wintm[wi][:, hs], op=ALU.mult),
                             reads=[pb[bB], Bwt[wi]], writes=[Bkd])
                        continue
                    P.op("pe", lambda e, bF=bF, tsl_=tsl_, colf=colf: e.matmul(psum[:, bF, :], lhsT=h2T[:, tsl_], rhs=w3[:, colf:colf + 512], start=True, stop=True),
                         reads=[Bw, Bh2], writes=[pb[bF]])
                    P.op("pe", lambda e, bB=bB, tsl_=tsl_, colb=colb: e.matmul(psum[:, bB, :], lhsT=h2T[:, tsl_], rhs=w3[:, colb:colb + 512], start=True, stop=True),
                         reads=[Bw, Bh2], writes=[pb[bB]])
                    P.op("dve", lambda e, bF=bF, i2=i2, wi=wi, hs=hs: e.tensor_tensor(out=tmpf[i2], in0=psum[:, bF, :], in1=wintm[wi][:, hs], op=ALU.mult),
                         reads=[pb[bF], Bwt[wi]], writes=[Btf[i2]])
                    P.op("dve", lambda e, bB=bB, i2=i2, wi=wi, hs=hs: e.tensor_tensor(out=tmpb[i2], in0=psum[:, bB, :], in1=wintm[wi][:, hs], op=ALU.mult),
                         reads=[pb[bB], Bwt[wi]], writes=[Btb[i2]])
                    if tc == 0:
                        P.op("dve", lambda e, i2=i2: e.tensor_scalar(out=tmpf[i2][0:1, :], in0=tmpf[i2][0:1, :], scalar1=pv[0:1, PV["flags"]:PV["flags"] + 1], scalar2=None, op0=ALU.mult),
                             reads=[Btf[i2], Bc], writes=[Btf[i2]])
                        P.op("dve", lambda e, i2=i2: e.tensor_scalar(out=tmpb[i2][0:1, :], in0=tmpb[i2][0:1, :], scalar1=pv[0:1, PV["flags"] + 1:PV["flags"] + 2], scalar2=None, op0=ALU.mult),
                             reads=[Btb[i2], Bc], writes=[Btb[i2]])
                    P.op("pool", lambda e, i2=i2, tc=tc, hs=hs: e.tensor_tensor(out=ke[:, tc, hs], in0=tmpf[i2], in1=tmpb[i2], op=ALU.add),
                         reads=[Btf[i2], Btb[i2]], writes=[Bke])
                    P.op("pool", lambda e, i2=i2, tc=tc, hs=hs: e.tensor_tensor(out=kd[:, tc, hs], in0=tmpf[i2], in1=tmpb[i2], op=ALU.subtract),
                         reads=[Btf[i2], Btb[i2]], writes=[Bkd])
            for i, (part, fc) in enumerate(iters):
                src, Bsrc = (ke, Bke) if part == 0 else (kd, Bkd)
                if i + PF < len(iters):
                    issue_tsl(i + PF)
                s3 = (tbase + i) % 4
                ks = kcnt % 2
                kcnt += 1
                for cb in range(2):
                    bk = (kcnt * 2 + cb) % 4
                    for tc in range(16):
                        P.op("pe", lambda e, bk=bk, s3=s3, tc=tc, cb=cb, src=src: e.matmul(psum[:, bk, :], lhsT=tsl[s3][:, tc, :], rhs=src[:, tc, cb * 512:(cb + 1) * 512], start=(tc == 0), stop=(tc == 15)),
                             reads=[Bts[s3], Bsrc], writes=[pb[bk]])
                    if cb == 0:
                        P.op("act", lambda e, bk=bk, ks=ks, cb=cb: e.copy(out=ksb[ks][:, cb * 512:(cb + 1) * 512], in_=psum[:, bk, :]),
                             reads=[pb[bk]], writes=[Bks[ks]])
                    else:
                        P.op("dve", lambda e, bk=bk, ks=ks, cb=cb: e.tensor_copy(out=ksb[ks][:, cb * 512:(cb + 1) * 512], in_=psum[:, bk, :]),
                             reads=[pb[bk]], writes=[Bks[ks]])
                P.dma("sp", lambda e, ks=ks, o=o, part=part, fc=fc: e.dma_start(out=KT_s[o, part, fc * 128:(fc + 1) * 128, :], in_=ksb[ks]),
                      reads=[Bks[ks]], writes=[BKT], key="ksb_st%d" % ks)
            tbase += len(iters)
        P.barrier()

    if "F" in stages:
        phase_F()

    TAIL_OFF = 168 * 1024
    uT = arena[:, TAIL_OFF // 2:(TAIL_OFF + 32768) // 2].rearrange("p (a b) -> p a b", b=1024)
    BuT = [Buf("uT0"), Buf("uT1")]
    z_s = dscr("z_s", [8, 128, T1])
    cpy_cnt = [0]

    def evac(out, in_, Bin, Bout):
        cpy_cnt[0] += 1
        if cpy_cnt[0] % 2:
            P.op("act", lambda e: e.copy(out=out, in_=in_), reads=Bin, writes=Bout)
        else:
            P.op("dve", lambda e: e.tensor_copy(out=out, in_=in_), reads=Bin, writes=Bout)

    def rms_stats(sq3, nch, ncol, bank, pbk, Bsq, rstd, Brstd):
        for c in range(nch):
            P.op("pe", lambda e, c=c: e.matmul(psum[:, bank, 0:ncol], lhsT=onesb[:], rhs=sq3[:, c, :], start=(c == 0), stop=(c == nch - 1)),
                 reads=[Bsq, Bc], writes=[pbk])
        P.op("act", lambda e: e.activation(out=rstd, in_=psum[:, bank, 0:ncol], func=AF.Sqrt, scale=1.0 / (nch * 128), bias=epsc[:, 0:1]),
             reads=[pbk, Bc], writes=[Brstd])
        P.op("dve", lambda e: e.reciprocal(out=rstd, in_=rstd), reads=[Brstd], writes=[Brstd])

    def phase_N1():
        A = Arena()
        hT = A.take(16 * 2048, BF16).rearrange("p (a b) -> p a b", b=2048)
        xt = [A.take(2048, F32) for _ in range(2)]
        xTb_ = [A.take(16 * 512, F32).rearrange("p (a b) -> p a b", b=512) for _ in range(2)]
        sq_ = [A.take(16 * 512, BF16).rearrange("p (a b) -> p a b", b=512) for _ in range(2)]
        rstd_ = [A.take(512, F32) for _ in range(2)]
        pb = PB()
        Bxt = [Buf("xt0"), Buf("xt1")]
        BhT, BxTs = Buf("hT"), Buf("xTs")
        BxTb_ = [Buf("xTb0"), Buf("xTb1")]
        Bsq_ = [Buf("sqN0"), Buf("sqN1")]
        Brs_ = [Buf("rstdN0"), Buf("rstdN1")]
        bcnt = 0
        for tb in range(4):
            xTb, sq, rstd = xTb_[tb % 2], sq_[tb % 2], rstd_[tb % 2]
            BxTb, Bsq, Brs = BxTb_[tb % 2], Bsq_[tb % 2], Brs_[tb % 2]
            for i in range(4):
                tile = tb * 4 + i
                s = tile % 2
                P.dma("sp", lambda e, s=s, tile=tile: e.dma_start(out=xt[s], in_=x_d[tile * 128:(tile + 1) * 128, :]), writes=[Bxt[s]])
                for g4 in range(4):
                    bank = bcnt % 4
                    bcnt += 1
                    for q in range(4):
                        dc = g4 * 4 + q
                        P.op("pe", lambda e, s=s, bank=bank, q=q, dc=dc: e.transpose(out=psum[:, bank, q * 128:(q + 1) * 128], in_=xt[s][:, dc * 128:(dc + 1) * 128], identity=identf[:]),
                             reads=[Bxt[s], Bc], writes=[pb[bank]])
                    evac(xTb[:, g4 * 4:(g4 + 1) * 4, i * 128:(i + 1) * 128], psum[:, bank, :].rearrange("p (a b) -> p a b", b=128), [pb[bank]], [BxTb])
            for g4 in range(4):
                P.op("act", lambda e, g4=g4, sq=sq, xTb=xTb: e.activation(out=sq[:, g4 * 4:(g4 + 1) * 4, :], in_=xTb[:, g4 * 4:(g4 + 1) * 4, :], func=AF.Square),
                     reads=[BxTb], writes=[Bsq])
            rms_stats(sq, 16, 512, 4 + tb % 2, pb[4 + tb % 2], Bsq, rstd, Brs)
            for dc in range(16):
                P.op("dve", lambda e, dc=dc, tb=tb, xTb=xTb, rstd=rstd: e.scalar_tensor_tensor(out=hT[:, dc, tb * 512:(tb + 1) * 512], in0=xTb[:, dc, :], scalar=pvs("g1", dc), in1=rstd, op0=ALU.mult, op1=ALU.mult),
                     reads=[BxTb, Brs, Bc], writes=[BhT])
            if tb < 2:
                P.dma("sp", lambda e, tb=tb, xTb=xTb: e.dma_start(out=xT_s[:, :, tb * 512:(tb + 1) * 512].rearrange("c p t -> p c t"), in_=xTb),
                      reads=[BxTb], writes=[BxTs], key="xTb_st%d" % (tb % 2))
        P.barrier()

    def phase_P():
        A = Arena()
        hT = A.take(16 * 2048, BF16).rearrange("p (a b) -> p a b", b=2048)
        wsl = [A.take(16 * 512, BF16).rearrange("p (a b) -> p a b", b=512) for _ in range(2)]
        sgt = [A.take(512, F32) for _ in range(2)]
        mark = A.off
        p_sb = [A.take(2050, F32) for _ in range(2)]
        acc = [A.take(2048, F32) for _ in range(2)]
        accb = A.take(2048, BF16)
        assert A.off <= TAIL_OFF, A.off
        A.off = mark
        upad = A.take(15 + 1056 + 1, BF16)
        dg = A.take(31 * 128, BF16).rearrange("p (a b) -> p a b", b=128)
        ucv = A.take(8 * 1024, F32).rearrange("p (a b) -> p a b", b=1024)
        ub = A.take(8 * 512, BF16).rearrange("p (a b) -> p a b", b=512)
        usq = A.take(8 * 512, BF16).rearrange("p (a b) -> p a b", b=512)
        mu = A.take(512, F32)
        m2 = A.take(512, F32)
        rstd = A.take(512, F32)
        tq = sgt
        ybt = [A.take(512, BF16) for _ in range(2)]
        assert A.off <= TAIL_OFF, A.off
        pb = PB()
        BhT = Buf("hT")
        Bws = [Buf("wsl0"), Buf("wsl1")]
        Bp = [Buf("p_sb0"), Buf("p_sb1")]
        Bacc = [Buf("acc0"), Buf("acc1")]
        Baccb, Bup, Bdg = Buf("accb"), Buf("upad"), Buf("dg")
        Bsg = [Buf("sgt0"), Buf("sgt1")]
        Bucv, Bub, Busq, Bmu, Bm2, Brs = Buf("ucv"), Buf("ub"), Buf("usq"), Buf("mu"), Buf("m2"), Buf("rstdP")
        Btq = Bsg
        Bybt = [Buf("ybt0"), Buf("ybt1")]
        Bvs, Bx1s, Bx2s, Bybs, Bsgs = Buf("v_s"), Buf("x1_s"), Buf("x2_s"), Buf("ybT_s"), Buf("sg_s")
        for i in range(2):
            P.op("dve", lambda e, i=i: e.memset(p_sb[i], 0.0), writes=[Bp[i]])
        wcnt = [0]
        bcnt = [0]

        def loadw(cols):
            s = wcnt[0] % 2
            wcnt[0] += 1

            def fn(e, s=s, cols=cols):
                r = []
                o = 0
                for (c0, n) in cols:
                    r.append(e.dma_start(out=wsl[s][:, :, o:o + n], in_=w_in_d[:, c0:c0 + n].rearrange("(k p) n -> p k n", p=128)))
                    o += n
                return r
            P.dma("pool", fn, writes=[Bws[s]], n=len(cols))
            return s

        def mm16(bank, n, s, col, t0):
            for k in range(16):
                P.op("pe", lambda e, k=k: e.matmul(psum[:, bank, 0:n], lhsT=wsl[s][:, k, col:col + 128], rhs=hT[:, k, t0:t0 + n], start=(k == 0), stop=(k == 15)),
                     reads=[Bws[s], BhT], writes=[pb[bank]])

        def nextbank(lo=0, n=4):
            b = lo + bcnt[0] % n
            bcnt[0] += 1
            return b

        for blk in range(6):
            s = loadw([(blk * 512, 512)])
            for q in range(4):
                ch = blk * 4 + q
                kind = ch // 8
                i2 = ch % 2
                groups = [(0, 512), (512, 512), (1024, 512), (1536, 512)] if kind < 2 else [(0, 512), (512, 512), (1024, 32)]
                for (t0, n) in groups:
                    bank = nextbank()
                    mm16(bank, n, s, q * 128, t0)
                    evac(p_sb[i2][:, 1 + t0:1 + t0 + n], psum[:, bank, 0:n], [pb[bank]], [Bp[i2]])
                nt = 2048 if kind < 2 else 1024
                a = acc[i2][:, 0:nt]
                P.op("act", lambda e, a=a, i2=i2, nt=nt, ch=ch: e.activation(out=a, in_=p_sb[i2][:, 1:1 + nt], func=AF.Identity, scale=pvs("hcw", ch * 3 + 1), bias=pvs("hcb", ch)),
                     reads=[Bp[i2], Bc], writes=[Bacc[i2]])
                P.op("dve", lambda e, a=a, i2=i2, nt=nt, ch=ch: e.scalar_tensor_tensor(out=a, in0=p_sb[i2][:, 0:nt], scalar=pvs("hcw", ch * 3 + 0), in1=a, op0=ALU.mult, op1=ALU.add),
                     reads=[Bp[i2], Bc, Bacc[i2]], writes=[Bacc[i2]])
                P.op("dve", lambda e, a=a, i2=i2, nt=nt, ch=ch: e.scalar_tensor_tensor(out=a, in0=p_sb[i2][:, 2:2 + nt], scalar=pvs("hcw", ch * 3 + 2), in1=a, op0=ALU.mult, op1=ALU.add),
                     reads=[Bp[i2], Bc, Bacc[i2]], writes=[Bacc[i2]])
                dst = (v_s, x1_s, x2_s)[kind][ch % 8]
                Bd = (Bvs, Bx1s, Bx2s)[kind]
                P.dma("sp", lambda e, a=a, dst=dst: e.dma_start(out=dst, in_=a), reads=[Bacc[i2]], writes=[Bd], key="acc_st%d" % i2)
                if kind == 0:
                    P.op("pool", lambda e, i2=i2: e.tensor_copy(out=accb, in_=acc[i2]), reads=[Bacc[i2]], writes=[Baccb])
                    psT = psum[:, 4:6, :].rearrange("p a b -> p (a b)").bitcast(BF16).rearrange("p (a b) -> p a b", b=128)
                    for tc in range(16):
                        P.op("pe", lambda e, tc=tc: e.transpose(out=psT[:, tc, :], in_=accb[:, tc * 128:(tc + 1) * 128], identity=identb[:]),
                             reads=[Baccb, Bc], writes=[pb[4], pb[5]])
                    P.op("act", lambda e, ch=ch: e.copy(out=uT[:, :, ch * 128:(ch + 1) * 128], in_=psT), reads=[pb[4], pb[5]], writes=[BuT[ch // 4]])
        P.barrier()
        P.op("dve", lambda e: e.memset(upad, 0.0), writes=[Bup])
        for pg in range(4):
            s = loadw([(3072 + pg * 256, 256), (4096 + pg * 256, 256)])
            for i in range(2):
                cc = pg * 2 + i
                for (t0, n) in [(0, 512), (512, 512), (1024, 32)]:
                    bA = nextbank()
                    mm16(bA, n, s, i * 128, t0)
                    bB = nextbank()
                    mm16(bB, n, s, 256 + i * 128, t0)
                    g2 = bB % 2
                    P.op("act", lambda e, bB=bB, n=n, g2=g2: e.activation(out=sgt[g2][:, 0:n], in_=psum[:, bB, 0:n], func=AF.Sigmoid),
                         reads=[pb[bB]], writes=[Bsg[g2]])
                    P.op("dve", lambda e, bA=bA, n=n, g2=g2, t0=t0: e.tensor_tensor(out=upad[:, 15 + t0:15 + t0 + n], in0=psum[:, bA, 0:n], in1=sgt[g2][:, 0:n], op=ALU.mult),
                         reads=[pb[bA], Bsg[g2]], writes=[Bup])
                for k in range(31):
                    P.op("act", lambda e, k=k, cc=cc: e.activation(out=dg[:, k, :], in_=identb[:], func=AF.Identity, scale=pvs("cdw", cc * 31 + k), bias=zeroc[:, 0:1]),
                         reads=[Bc], writes=[Bdg])
                for tb in range(2):
                    bank = nextbank()
                    for k in range(31):
                        P.op("pe", lambda e, k=k, tb=tb, bank=bank: e.matmul(psum[:, bank, :], lhsT=dg[:, k, :], rhs=upad[:, tb * 512 + k:tb * 512 + k + 512], start=(k == 0), stop=(k == 30)),
                             reads=[Bdg, Bup], writes=[pb[bank]])
                    P.op("act", lambda e, tb=tb, bank=bank, cc=cc: e.activation(out=ucv[:, cc, tb * 512:(tb + 1) * 512], in_=psum[:, bank, :], func=AF.Identity, scale=1.0, bias=pvs("cdb", cc)),
                         reads=[pb[bank], Bc], writes=[Bucv])
        for tb in range(2):
            tsl_ = slice(tb * 512, (tb + 1) * 512)
            P.op("act", lambda e, tsl_=tsl_: e.activation(out=usq, in_=ucv[:, :, tsl_], func=AF.Square), reads=[Bucv], writes=[Busq])
            P.op("pool", lambda e, tsl_=tsl_: e.tensor_copy(out=ub, in_=ucv[:, :, tsl_]), reads=[Bucv], writes=[Bub])
            for c in range(8):
                P.op("pe", lambda e, c=c: e.matmul(psum[:, 6, :], lhsT=onesb[:], rhs=ub[:, c, :], start=(c == 0), stop=(c == 7)), reads=[Bub, Bc], writes=[pb[6]])
            for c in range(8):
                P.op("pe", lambda e, c=c: e.matmul(psum[:, 7, :], lhsT=onesb[:], rhs=usq[:, c, :], start=(c == 0), stop=(c == 7)), reads=[Busq, Bc], writes=[pb[7]])
            P.op("dve", lambda e: e.tensor_scalar(out=mu, in0=psum[:, 6, :], scalar1=1.0 / 1024, scalar2=None, op0=ALU.mult), reads=[pb[6]], writes=[Bmu])
            P.op("dve", lambda e: e.tensor_tensor(out=m2, in0=mu, in1=mu, op=ALU.mult), reads=[Bmu], writes=[Bm2])
            P.op("dve", lambda e: e.scalar_tensor_tensor(out=m2, in0=psum[:, 7, :], scalar=1.0 / 1024, in1=m2, op0=ALU.mult, op1=ALU.subtract), reads=[pb[7], Bm2], writes=[Bm2])
            P.op("act", lambda e: e.activation(out=rstd, in_=m2, func=AF.Sqrt, scale=1.0, bias=epsc[:, 0:1]), reads=[Bm2, Bc], writes=[Brs])
            P.op("dve", lambda e: e.reciprocal(out=rstd, in_=rstd), reads=[Brs], writes=[Brs])
            for c in range(8):
                j = c % 2
                P.op("dve", lambda e, c=c, j=j, tsl_=tsl_: e.tensor_tensor(out=tq[j], in0=ucv[:, c, tsl_], in1=mu, op=ALU.subtract), reads=[Bucv, Bmu], writes=[Btq[j]])
                P.op("dve", lambda e, j=j: e.tensor_tensor(out=tq[j], in0=tq[j], in1=rstd, op=ALU.mult), reads=[Btq[j], Brs], writes=[Btq[j]])
                P.op("act", lambda e, c=c, j=j: e.activation(out=ybt[j], in_=tq[j], func=AF.Silu, scale=pvs("lng", c), bias=pvs("lnb", c)), reads=[Btq[j], Bc], writes=[Bybt[j]])
                P.dma("sp", lambda e, c=c, j=j, tsl_=tsl_: e.dma_start(out=ybT_s[c][:, tsl_], in_=ybt[j]), reads=[Bybt[j]], writes=[Bybs], key="ybt_st%d" % j)
        for blk in range(8):
            s = loadw([(5120 + blk * 512, 512)])
            for q in range(4):
                g = blk * 4 + q
                for tb in range(2):
                    bank = nextbank()
                    mm16(bank, 512, s, q * 128, tb * 512)
                    g2 = bank % 2
                    P.op("act", lambda e, bank=bank, g2=g2: e.activation(out=sgt[g2], in_=psum[:, bank, :], func=AF.Sigmoid), reads=[pb[bank]], writes=[Bsg[g2]])
                    P.dma("sp", lambda e, g=g, tb=tb, g2=g2: e.dma_start(out=sg_s[g // 16, g % 16][:, tb * 512:(tb + 1) * 512], in_=sgt[g2]), reads=[Bsg[g2]], writes=[Bsgs], key="sgt_st%d" % g2)
        P.barrier()

    def phase_C(ci):
        A = Arena()
        Y = A.take(2 * 16 * 512, BF16).rearrange("p (a b c) -> p a b c", a=2, b=16)
        tsc = [A.take(16 * 128, BF16).rearrange("p (a b) -> p a b", b=128) for _ in range(2)]
        tss = [A.take(16 * 128, BF16).rearrange("p (a b) -> p a b", b=128) for _ in range(2)]
        kr = [A.take(512, F32) for _ in range(2)]
        ki = [A.take(512, F32) for _ in range(2)]
        t1, t2, t3, t4 = [A.take(512, F32) for _ in range(4)]
        rt = [A.take(32 * 512, BF16).rearrange("p (a b) -> p a b", b=512) for _ in range(2)]
        vt = [A.take(512, F32) for _ in range(2)]
        xg = [A.take(512, F32) for _ in range(2)]
        zt = [A.take(512, F32) for _ in range(2)]
        zb = [A.take(512, BF16) for _ in range(2)]
        assert A.off <= TAIL_OFF, A.off
        pb = PB()
        BY = Buf("Y")
        Btc = [Buf("tsc0"), Buf("tsc1")]
        Bts = [Buf("tss0"), Buf("tss1")]
        Bkr = [Buf("kr0"), Buf("kr1")]
        Bki = [Buf("ki0"), Buf("ki1")]
        Bt = [Buf("t1"), Buf("t2"), Buf("t3"), Buf("t4")]
        Brt = [Buf("rt0"), Buf("rt1")]
        Bvt = [Buf("vt0"), Buf("vt1")]
        Bxg = [Buf("xg0"), Buf("xg1")]
        Bzt = [Buf("zt0"), Buf("zt1")]
        Bzb = [Buf("zb0"), Buf("zb1")]
        Bzs, Byas = Buf("z_s"), Buf("yaT_s")
        fcnt = 0
        rcnt = 0
        ecnt = 0
        ntb = 4 if ci == 0 else 2

        def load_rt(r, tb):
            ts2 = slice(tb * 512, (tb + 1) * 512)

            def ld(e, r=r, ts2=ts2):
                return [e.dma_start(out=rt[r][:, part * 16:(part + 1) * 16, :], in_=tabR_d[part].rearrange("(fc p) t -> p fc t", p=128)[:, :, ts2]) for part in range(2)]
            P.dma("sp", ld, writes=[Brt[r]], n=2)

        for hh in range(2):
            hs = slice(hh * 512, (hh + 1) * 512)
            rslots = [(rcnt + t) % 2 for t in range(ntb)]
            rcnt += ntb
            load_rt(rslots[0], 0)
            for fc in range(16):
                s = fcnt % 2
                fcnt += 1
                P.dma("sp", lambda e, s=s, fc=fc: e.dma_start(out=tsc[s], in_=tabD_d[0][fc]), writes=[Btc[s]])
                P.dma("sp", lambda e, s=s, fc=fc: e.dma_start(out=tss[s], in_=tabD_d[1][fc]), writes=[Bts[s]])
                P.dma("sp", lambda e, s=s, fc=fc, hs=hs: e.dma_start(out=kr[s], in_=KT_s[ci, 0, fc * 128:(fc + 1) * 128, hs]), writes=[Bkr[s]])
                P.dma("sp", lambda e, s=s, fc=fc, hs=hs: e.dma_start(out=ki[s], in_=KT_s[ci, 1, fc * 128:(fc + 1) * 128, hs]), writes=[Bki[s]])
                bR, bI = 2 * s, 2 * s + 1
                for tc in range(16):
                    P.op("pe", lambda e, tc=tc, s=s, bR=bR, hs=hs: e.matmul(psum[:, bR, :], lhsT=tsc[s][:, tc, :], rhs=uT[:, tc, hs], start=(tc == 0), stop=(tc == 15)),
                         reads=[Btc[s], BuT[hh]], writes=[pb[bR]])
                for tc in range(16):
                    P.op("pe", lambda e, tc=tc, s=s, bI=bI, hs=hs: e.matmul(psum[:, bI, :], lhsT=tss[s][:, tc, :], rhs=uT[:, tc, hs], start=(tc == 0), stop=(tc == 15)),
                         reads=[Bts[s], BuT[hh]], writes=[pb[bI]])
                P.op("dve", lambda e, s=s, bR=bR: e.tensor_tensor(out=t1, in0=psum[:, bR, :], in1=kr[s], op=ALU.mult), reads=[pb[bR], Bkr[s]], writes=[Bt[0]])
                P.op("dve", lambda e, s=s, bI=bI: e.tensor_tensor(out=t2, in0=psum[:, bI, :], in1=ki[s], op=ALU.mult), reads=[pb[bI], Bki[s]], writes=[Bt[1]])
                P.op("dve", lambda e, s=s, bR=bR: e.tensor_tensor(out=t3, in0=psum[:, bR, :], in1=ki[s], op=ALU.mult), reads=[pb[bR], Bki[s]], writes=[Bt[2]])
                P.op("dve", lambda e, s=s, bI=bI: e.tensor_tensor(out=t4, in0=psum[:, bI, :], in1=kr[s], op=ALU.mult), reads=[pb[bI], Bkr[s]], writes=[Bt[3]])
                P.op("pool", lambda e, fc=fc: e.tensor_tensor(out=Y[:, 0, fc, :], in0=t1, in1=t2, op=ALU.subtract), reads=[Bt[0], Bt[1]], writes=[BY])
                P.op("pool", lambda e, fc=fc: e.tensor_tensor(out=Y[:, 1, fc, :], in0=t3, in1=t4, op=ALU.add), reads=[Bt[2], Bt[3]], writes=[BY])
            for tb in range(ntb):
                ts_ = slice(tb * 512, (tb + 1) * 512)
                if tb + 1 < ntb:
                    load_rt(rslots[tb + 1], tb + 1)
                r = rslots[tb]
                for c4 in range(4):
                    cc = hh * 4 + c4
                    j = ecnt % 2
                    ecnt += 1
                    bank = 4 + j
                    src_v = (v_s if ci == 0 else z_s)[cc][:, ts_]
                    src_x = (x1_s if ci == 0 else x2_s)[cc][:, ts_]
                    P.dma("sp", lambda e, j=j, src_v=src_v: e.dma_start(out=vt[j], in_=src_v), writes=[Bvt[j]])
                    P.dma("sp", lambda e, j=j, src_x=src_x: e.dma_start(out=xg[j], in_=src_x), writes=[Bxg[j]])
                    idx = 0
                    for part in range(2):
                        for fc in range(16):
                            P.op("pe", lambda e, part=part, fc=fc, c4=c4, r=r, bank=bank, idx=idx: e.matmul(psum[:, bank, :], lhsT=Y[:, part, fc, c4 * 128:(c4 + 1) * 128], rhs=rt[r][:, part * 16 + fc, :], start=(idx == 0), stop=(idx == 31)),
                                 reads=[BY, Brt[r]], writes=[pb[bank]])
                            idx += 1
                    P.op("dve", lambda e, j=j, cc=cc: e.tensor_scalar(out=vt[j], in0=vt[j], scalar1=pvs("hyb", ci * 8 + cc), scalar2=None, op0=ALU.mult), reads=[Bvt[j], Bc], writes=[Bvt[j]])
                    P.op("dve", lambda e, j=j, cc=cc, bank=bank: e.scalar_tensor_tensor(out=zt[j], in0=psum[:, bank, :], scalar=rl1s[:, ci * 8 + cc:ci * 8 + cc + 1], in1=vt[j], op0=ALU.mult, op1=ALU.add),
                         reads=[pb[bank], Brl1, Bvt[j]], writes=[Bzt[j]])
                    if ci == 0:
                        P.op("dve", lambda e, j=j: e.tensor_tensor(out=zt[j], in0=zt[j], in1=xg[j], op=ALU.mult), reads=[Bzt[j], Bxg[j]], writes=[Bzt[j]])
                        if tb < 2:
                            P.dma("sp", lambda e, j=j, cc=cc, ts_=ts_: e.dma_start(out=z_s[cc][:, ts_], in_=zt[j]), reads=[Bzt[j]], writes=[Bzs], key="zt_st%d" % j)
                        P.op("pool", lambda e, j=j: e.tensor_copy(out=zb[j], in_=zt[j]), reads=[Bzt[j]], writes=[Bzb[j]])
                        psT = psum[:, 6 + j, :].bitcast(BF16)[:, 0:512].rearrange("p (a b) -> p a b", b=128)
                        for i in range(4):
                            P.op("pe", lambda e, i=i, j=j, psT=psT: e.transpose(out=psT[:, i, :], in_=zb[j][:, i * 128:(i + 1) * 128], identity=identb[:]),
                                 reads=[Bzb[j], Bc], writes=[pb[6 + j]])
                        P.op("act", lambda e, psT=psT, tb=tb, cc=cc: e.copy(out=uT[:, tb * 4:(tb + 1) * 4, cc * 128:(cc + 1) * 128], in_=psT), reads=[pb[6 + j]], writes=[BuT[hh]])
                    else:
                        P.op("dve", lambda e, j=j: e.tensor_tensor(out=zb[j], in0=zt[j], in1=xg[j], op=ALU.mult), reads=[Bzt[j], Bxg[j]], writes=[Bzb[j]])
                        P.dma("sp", lambda e, j=j, cc=cc, ts_=ts_: e.dma_start(out=yaT_s[cc][:, ts_], in_=zb[j]), reads=[Bzb[j]], writes=[Byas], key="zb_st%d" % j)
        P.barrier()

    def phase_M():
        A = Arena()
        h2T = A.take(16 * 1024, BF16).rearrange("p (a b) -> p a b", b=1024)
        mT = A.take(16 * 1024, BF16).rearrange("p (a b) -> p a b", b=1024)
        wsl = [A.take(16 * 512, BF16).rearrange("p (a b) -> p a b", b=512) for _ in range(2)]
        mark = A.off
        yaT = A.take(8 * 1024, BF16).rearrange("p (a b) -> p a b", b=1024)
        ybT = A.take(8 * 1024, BF16).rearrange("p (a b) -> p a b", b=1024)
        ga = [A.take(512, F32) for _ in range(2)]
        gb = [A.take(512, F32) for _ in range(2)]
        m1 = [A.take(512, F32) for _ in range(2)]
        m2_ = [A.take(512, F32) for _ in range(2)]
        A.off = mark
        oT = A.take(16 * 512, F32).rearrange("p (a b) -> p a b", b=512)
        sq = A.take(16 * 512, BF16).rearrange("p (a b) -> p a b", b=512)
        rstd = A.take(512, F32)
        rstd2 = A.take(512, F32)
        xt = [A.take(512, F32) for _ in range(2)]
        tt = [A.take(512, F32) for _ in range(2)]
        pb = PB()
        Bya, Byb, BmT, Bh2 = Buf("yaT"), Buf("ybT"), Buf("mT"), Buf("h2T")
        Bws = [Buf("wslM0"), Buf("wslM1")]
        Bga = [Buf("ga0"), Buf("ga1")]
        Bgb = [Buf("gb0"), Buf("gb1")]
        Bm1 = [Buf("m10"), Buf("m11")]
        Bm2 = [Buf("m20"), Buf("m21")]
        BoT, Bsq, Brs, Brs2 = Buf("oT"), Buf("sqM"), Buf("rstdM"), Buf("rstdM2")
        Bxt = [Buf("xtM0"), Buf("xtM1")]
        Btt = [Buf("ttM0"), Buf("ttM1")]
        Br1s = Buf("r1T_s")
        P.dma("sp", lambda e: e.dma_start(out=yaT, in_=yaT_s.rearrange("c p t -> p c t")), writes=[Bya])
        P.dma("sp", lambda e: e.dma_start(out=ybT, in_=ybT_s.rearrange("c p t -> p c t")), writes=[Byb])
        wc = 0
        bc = 0
        for nb in range(4):
            s = wc % 2
            wc += 1

            def ldw(e, s=s, nb=nb):
                r = []
                for k in range(8):
                    r.append(e.dma_start(out=wsl[s][:, k, :], in_=hy_proj_d[k * 128:(k + 1) * 128, nb * 512:(nb + 1) * 512]))
                    r.append(e.dma_start(out=wsl[s][:, 8 + k, :], in_=cf_proj_d[k * 128:(k + 1) * 128, nb * 512:(nb + 1) * 512]))
                return r
            P.dma("pool", ldw, writes=[Bws[s]], n=16)
            for q in range(4):
                dch = nb * 4 + q
                for tb in range(2):
                    ts_ = slice(tb * 512, (tb + 1) * 512)
                    j = bc % 2
                    bc += 1
                    bA, bB = 2 * j, 2 * j + 1
                    P.dma("sp", lambda e, j=j, dch=dch, ts_=ts_: e.dma_start(out=ga[j], in_=sg_s[0, dch][:, ts_]), writes=[Bga[j]])
                    P.dma("sp", lambda e, j=j, dch=dch, ts_=ts_: e.dma_start(out=gb[j], in_=sg_s[1, dch][:, ts_]), writes=[Bgb[j]])
                    for k in range(8):
                        P.op("pe", lambda e, k=k, s=s, q=q, ts_=ts_, bA=bA: e.matmul(psum[:, bA, :], lhsT=wsl[s][:, k, q * 128:(q + 1) * 128], rhs=yaT[:, k, ts_], start=(k == 0), stop=(k == 7)),
                             reads=[Bws[s], Bya], writes=[pb[bA]])
                    for k in range(8):
                        P.op("pe", lambda e, k=k, s=s, q=q, ts_=ts_, bB=bB: e.matmul(psum[:, bB, :], lhsT=wsl[s][:, 8 + k, q * 128:(q + 1) * 128], rhs=ybT[:, k, ts_], start=(k == 0), stop=(k == 7)),
                             reads=[Bws[s], Byb], writes=[pb[bB]])
                    P.op("dve", lambda e, j=j, bA=bA: e.tensor_tensor(out=m1[j], in0=psum[:, bA, :], in1=ga[j], op=ALU.mult), reads=[pb[bA], Bga[j]], writes=[Bm1[j]])
                    P.op("dve", lambda e, j=j, bB=bB: e.tensor_tensor(out=m2_[j], in0=psum[:, bB, :], in1=gb[j], op=ALU.mult), reads=[pb[bB], Bgb[j]], writes=[Bm2[j]])
                    P.op("dve", lambda e, j=j, dch=dch, ts_=ts_: e.tensor_tensor(out=mT[:, dch, ts_], in0=m1[j], in1=m2_[j], op=ALU.add), reads=[Bm1[j], Bm2[j]], writes=[BmT])
        P.barrier()
        for tb in range(2):
            ts_ = slice(tb * 512, (tb + 1) * 512)
            for nb in range(4):
                s = wc % 2
                wc += 1
                P.dma("pool", lambda e, s=s, nb=nb: e.dma_start(out=wsl[s], in_=w_out_d[:, nb * 512:(nb + 1) * 512].rearrange("(k p) n -> p k n", p=128)), writes=[Bws[s]])
                for q in range(4):
                    nch = nb * 4 + q
                    bank = 4 + bc % 2
                    bc += 1
                    for k in range(16):
                        P.op("pe", lambda e, k=k, s=s, q=q, ts_=ts_, bank=bank: e.matmul(psum[:, bank, :], lhsT=wsl[s][:, k, q * 128:(q + 1) * 128], rhs=mT[:, k, ts_], start=(k == 0), stop=(k == 15)),
                             reads=[Bws[s], BmT], writes=[pb[bank]])
                    evac(oT[:, nch, :], psum[:, bank, :], [pb[bank]], [BoT])
                P.op("act", lambda e, nb=nb: e.activation(out=sq[:, nb * 4:(nb + 1) * 4, :], in_=oT[:, nb * 4:(nb + 1) * 4, :], func=AF.Square), reads=[BoT], writes=[Bsq])
            rms_stats(sq, 16, 512, 6, pb[6], Bsq, rstd, Brs)
            for nch in range(16):
                j = nch % 2
                P.dma("sp", lambda e, j=j, nch=nch, ts_=ts_: e.dma_start(out=xt[j], in_=xT_s[nch][:, ts_]), writes=[Bxt[j]])
                P.op("dve", lambda e, j=j, nch=nch: e.scalar_tensor_tensor(out=tt[j], in0=oT[:, nch, :], scalar=pvs("g2", nch), in1=rstd, op0=ALU.mult, op1=ALU.mult),
                     reads=[BoT, Brs, Bc], writes=[Btt[j]])
                P.op("dve", lambda e, j=j, nch=nch: e.tensor_tensor(out=oT[:, nch, :], in0=tt[j], in1=xt[j], op=ALU.add), reads=[Btt[j], Bxt[j], BoT], writes=[BoT])
            P.dma("sp", lambda e, ts_=ts_: e.dma_start(out=r1T_s[:, :, ts_].rearrange("c p t -> p c t"), in_=oT), reads=[BoT], writes=[Br1s], key="oT_st")
            for g4 in range(4):
                P.op("act", lambda e, g4=g4: e.activation(out=sq[:, g4 * 4:(g4 + 1) * 4, :], in_=oT[:, g4 * 4:(g4 + 1) * 4, :], func=AF.Square), reads=[BoT], writes=[Bsq])
            rms_stats(sq, 16, 512, 7, pb[7], Bsq, rstd2, Brs2)
            for nch in range(16):
                P.op("dve", lambda e, nch=nch, ts_=ts_: e.scalar_tensor_tensor(out=h2T[:, nch, ts_], in0=oT[:, nch, :], scalar=pvs("g3", nch), in1=rstd2, op0=ALU.mult, op1=ALU.mult),
                     reads=[BoT, Brs2, Bc], writes=[Bh2])
        P.barrier()

    def phase_G():
        A = Arena()
        h2T = A.take(16 * 1024, BF16).rearrange("p (a b) -> p a b", b=1024)
        wsl = [A.take(16 * 512, BF16).rearrange("p (a b) -> p a b", b=512) for _ in range(3)]
        st = [A.take(512, F32) for _ in range(2)]
        ab = [A.take(512, BF16) for _ in range(2)]
        pb = PB()
        Bh2 = Buf("h2T")
        Bws = [Buf("wslG%d" % i) for i in range(3)]
        Bst = [Buf("st0"), Buf("st1")]
        Bab = [Buf("ab0"), Buf("ab1")]
        Bas = Buf("act_s")
        bc = 0
        for blk in range(22):
            s = blk % 3

            def ldw(e, s=s, blk=blk):
                return [e.dma_start(out=wsl[s][:, :, 0:256], in_=w_gu_d[:, blk * 256:(blk + 1) * 256].rearrange("(k p) n -> p k n", p=128)),
                        e.dma_start(out=wsl[s][:, :, 256:512], in_=w_gu_d[:, FF + blk * 256:FF + (blk + 1) * 256].rearrange("(k p) n -> p k n", p=128))]
            P.dma("pool", ldw, writes=[Bws[s]], n=2)
            for i in range(2):
                kch = blk * 2 + i
                for tb in range(2):
                    ts_ = slice(tb * 512, (tb + 1) * 512)
                    j = bc % 2
                    bc += 1
                    bG, bU = 2 * j, 2 * j + 1
                    for k in range(16):
                        P.op("pe", lambda e, k=k, s=s, i=i, ts_=ts_, bG=bG: e.matmul(psum[:, bG, :], lhsT=wsl[s][:, k, i * 128:(i + 1) * 128], rhs=h2T[:, k, ts_], start=(k == 0), stop=(k == 15)),
                             reads=[Bws[s], Bh2], writes=[pb[bG]])
                    for k in range(16):
                        P.op("pe", lambda e, k=k, s=s, i=i, ts_=ts_, bU=bU: e.matmul(psum[:, bU, :], lhsT=wsl[s][:, k, 256 + i * 128:256 + (i + 1) * 128], rhs=h2T[:, k, ts_], start=(k == 0), stop=(k == 15)),
                             reads=[Bws[s], Bh2], writes=[pb[bU]])
                    P.op("act", lambda e, j=j, bG=bG: e.activation(out=st[j], in_=psum[:, bG, :], func=AF.Silu), reads=[pb[bG]], writes=[Bst[j]])
                    P.op("dve", lambda e, j=j, bU=bU: e.tensor_tensor(out=ab[j], in0=psum[:, bU, :], in1=st[j], op=ALU.mult), reads=[pb[bU], Bst[j]], writes=[Bab[j]])
                    P.dma("sp", lambda e, j=j, kch=kch, ts_=ts_: e.dma_start(out=act_s[kch][:, ts_], in_=ab[j]), reads=[Bab[j]], writes=[Bas], key="ab_st%d" % j)
        P.barrier()

    def phase_Dn():
        A = Arena()
        oT = A.take(16 * 1024, F32).rearrange("p (a b) -> p a b", b=1024)
        wd = [A.take(4 * 512, BF16).rearrange("p (a b) -> p a b", b=512) for _ in range(3)]
        ab = [A.take(4 * 1024, BF16).rearrange("p (a b) -> p a b", b=1024) for _ in range(3)]
        sq = A.take(16 * 512, BF16).rearrange("p (a b) -> p a b", b=512)
        rstd = A.take(512, F32)
        xt = [A.take(512, F32) for _ in range(2)]
        tt = [A.take(512, F32) for _ in range(2)]
        fo = A.take(16 * 512, F32).rearrange("p (a b) -> p a b", b=512)
        ot = [A.take(2048, F32) for _ in range(2)]
        pb = PB()
        BoT, Bsq, Brs, Bfo = Buf("oTD"), Buf("sqD"), Buf("rstdD"), Buf("fo")
        Bwd = [Buf("wd%d" % i) for i in range(3)]
        Bab = [Buf("abD%d" % i) for i in range(3)]
        Bxt = [Buf("xtD0"), Buf("xtD1")]
        Btt = [Buf("ttD0"), Buf("ttD1")]
        Bot = [Buf("ot0"), Buf("ot1")]
        Byy = Buf("yout")
        c3 = 0
        for ps_ in range(4):
            for kg in range(11):
                s = c3 % 3
                c3 += 1
                def ldwd(e, s=s, kg=kg, ps_=ps_):
                    return [e.dma_start(out=wd[s][:, k4, :], in_=w_down_d[(kg * 4 + k4) * 128:(kg * 4 + k4 + 1) * 128, ps_ * 512:(ps_ + 1) * 512]) for k4 in range(4)]
                P.dma("pool", ldwd, writes=[Bwd[s]], n=4)
                P.dma("sp", lambda e, s=s, kg=kg: e.dma_start(out=ab[s], in_=act_s[kg * 4:(kg + 1) * 4].rearrange("k p t -> p k t")), writes=[Bab[s]])
                for k4 in range(4):
                    for q in range(4):
                        for tb in range(2):
                            bank = q * 2 + tb
                            P.op("pe", lambda e, s=s, k4=k4, q=q, tb=tb, bank=bank, kg=kg: e.matmul(psum[:, bank, :], lhsT=wd[s][:, k4, q * 128:(q + 1) * 128], rhs=ab[s][:, k4, tb * 512:(tb + 1) * 512], start=(kg == 0 and k4 == 0), stop=(kg == 10 and k4 == 3)),
                                 reads=[Bwd[s], Bab[s]], writes=[pb[bank]])
            for q in range(4):
                for tb in range(2):
                    bank = q * 2 + tb
                    evac(oT[:, ps_ * 4 + q, tb * 512:(tb + 1) * 512], psum[:, bank, :], [pb[bank]], [BoT])
        ocnt = 0
        for tb in range(2):
            ts_ = slice(tb * 512, (tb + 1) * 512)
            for g4 in range(4):
                P.op("act", lambda e, g4=g4, ts_=ts_: e.activation(out=sq[:, g4 * 4:(g4 + 1) * 4, :], in_=oT[:, g4 * 4:(g4 + 1) * 4, ts_], func=AF.Square), reads=[BoT], writes=[Bsq])
            rms_stats(sq, 16, 512, tb, pb[tb], Bsq, rstd, Brs)
            for nch in range(16):
                j = nch % 2
                P.dma("sp", lambda e, j=j, nch=nch, ts_=ts_: e.dma_start(out=xt[j], in_=r1T_s[nch][:, ts_]), writes=[Bxt[j]])
                P.op("dve", lambda e, j=j, nch=nch, ts_=ts_: e.scalar_tensor_tensor(out=tt[j], in0=oT[:, nch, ts_], scalar=pvs("g4", nch), in1=rstd, op0=ALU.mult, op1=ALU.mult),
                     reads=[BoT, Brs, Bc], writes=[Btt[j]])
                P.op("dve", lambda e, j=j, nch=nch: e.tensor_tensor(out=fo[:, nch, :], in0=tt[j], in1=xt[j], op=ALU.add), reads=[Btt[j], Bxt[j]], writes=[Bfo])
            for t4 in range(4):
                o2 = ocnt % 2
                ocnt += 1
                for g4 in range(4):
                    bank = 2 + (t4 * 4 + g4) % 4
                    for q in range(4):
                        nch = g4 * 4 + q
                        P.op("pe", lambda e, bank=bank, q=q, nch=nch, t4=t4: e.transpose(out=psum[:, bank, q * 128:(q + 1) * 128], in_=fo[:, nch, t4 * 128:(t4 + 1) * 128], identity=identf[:]),
                             reads=[Bfo, Bc], writes=[pb[bank]])
                    evac(ot[o2][:, g4 * 512:(g4 + 1) * 512], psum[:, bank, :], [pb[bank]], [Bot[o2]])
                row = tb * 512 + t4 * 128
                P.dma("sp", lambda e, o2=o2, row=row: e.dma_start(out=y_d[row:row + 128, :], in_=ot[o2]), reads=[Bot[o2]], writes=[Byy], key="ot_st%d" % o2)
        P.op("sp", None, reads=[Byy])
        P.barrier()

    if "N1" in stages:
        phase_N1()
    if "P" in stages:
        phase_P()
    if "C" in stages:
        phase_C(0)
        phase_C(1)
    if "M" in stages:
        phase_M()
    if "G" in stages:
        phase_G()
    if "Dn" in stages:
        phase_Dn()

    if "Dn" not in stages and "F" in stages:
        By = Buf("y")
        Bfin = Buf("fin")
        fin = nc.alloc_sbuf_tensor("fin", [128, 16], F32)
        P.op("dve", lambda e: e.tensor_copy(out=fin[:], in_=rl1s[:]), reads=[Brl1], writes=[Bfin])
        P.dma("sp", lambda e: e.dma_start(out=y_d[0:128, 0:16], in_=fin[:]), reads=[Bfin], writes=[By], key="ystore")
        P.op("sp", None, reads=[By])
    if dbg is not None:
        srcs = {"KT": lambda: KT_s[0, 1], "KT2": lambda: KT_s[1, 0],
                "v": lambda: v_s.rearrange("c p t -> (c p) t"), "x1": lambda: x1_s.rearrange("c p t -> (c p) t"),
                "x2": lambda: x2_s.rearrange("c p t -> (c p) t"), "z": lambda: z_s.rearrange("c p t -> (c p) t"),
                "xT": lambda: xT_s.rearrange("c p t -> (c p) t"), "r1": lambda: r1T_s.rearrange("c p t -> (c p) t"),
                "sg": lambda: sg_s[0].rearrange("c p t -> (c p) t")}
        Bdbg = Buf("dbg")
        P.dma("sp", lambda e: e.dma_start(out=dbg_d, in_=srcs[dbg_name]()), writes=[Bdbg], key="dbgst")
        P.op("sp", None, reads=[Bdbg])
    P.emit()
    return nc


def core_inputs(inp, b, rev):
    C = consts()
    d = {}
    xb = inp["x"][b]
    d["x"] = np.ascontiguousarray(xb[::-1] if rev else xb)
    d["pvec"] = make_pvec(inp, rev)
    d["w_in"] = inp["w_in"][0]
    d["hy_proj"] = inp["hy_proj"][0]
    d["cf_proj"] = inp["cf_proj"][0]
    d["w_out"] = inp["w_out"][0]
    d["w_gu"] = inp["ffn_w_gu"][0]
    d["w_down"] = inp["ffn_w_down"][0]
    d["fw1"] = inp["hy_filt_w1"][0]
    fm = np.zeros((64, 68), np.float32)
    fm[:, 0:64] = inp["hy_filt_w2"][0]
    fm[:, 64] = inp["hy_filt_b1"][0]
    fm[:, 65] = inp["hy_filt_fr1"][0]
    fm[:, 66] = inp["hy_filt_b2"][0]
    fm[:, 67] = inp["hy_filt_fr2"][0]
    d["fmlp"] = fm
    w3 = inp["hy_filt_w3"][0]
    if rev:
        w3 = w3.reshape(64, 2, 2, 1024)[:, :, ::-1].reshape(64, 4096)
    d["fw3"] = np.ascontiguousarray(w3)
    for k in ["zT", "tabF_c", "tabF_s", "tabD_c", "tabD_s", "tabR", "ident_bf", "ident_f", "ones_bf", "tau_row", "adel_row"]:
        d[k] = C[k]
    return d


_NC = {}


def kernel(**inputs):
    inp = {k: np.asarray(v, dtype=np.float32) for k, v in inputs.items()}
    if "nc" not in _NC:
        _NC["nc"] = build()
    nc = _NC["nc"]
    in_maps = []
    for c in range(8):
        b, j = c // 2, c % 2
        in_maps.append(core_inputs(inp, b, j == 1))
    res = run_bass_kernel_spmd(nc, in_maps, core_ids=list(range(8)))
    out = np.zeros((4, 2048, 2048), np.float32)
    for c in range(8):
        b, j = c // 2, c % 2
        y = np.asarray(res.results[c]["y"], dtype=np.float32)
        if j == 0:
            out[b, 0:1024] = y
        else:
            out[b, 1024:2048] = y[::-1]
    return out
```

```python
import math
import numpy as np
import ml_dtypes
import concourse.bass as bass
import concourse.mybir as mybir
from concourse.bass_utils import run_bass_kernel_spmd

F32 = mybir.dt.float32
BF16 = mybir.dt.bfloat16
AF = mybir.ActivationFunctionType
ALU = mybir.AluOpType
AX = mybir.AxisListType

D = 2048
L = 2048
T1 = 1024
HWID = 1024
NIN = 9216
FF = 5632
EPS = 1e-6
NPV = 480
HY_MIN_DECAY = math.log(1e-2) / 1.5
HY_MAX_DECAY = math.log(1e-2) / 0.3


class Buf:
    __slots__ = ("name", "last_w", "readers")

    def __init__(self, name=""):
        self.name = name
        self.last_w = None
        self.readers = []


class Op:
    __slots__ = ("eng", "fn", "deps", "signal", "sig", "is_dma", "sem", "semval", "idx", "key")


class Prog:
    ENG_BLOCK = {"pe": "tensor", "act": "scalar", "dve": "vector", "pool": "gpsimd", "sp": "sync"}

    def __init__(self, nc):
        self.nc = nc
        self.ops = {k: [] for k in self.ENG_BLOCK}
        self.prog_sem = {k: nc.alloc_semaphore(name="prog_" + k) for k in self.ENG_BLOCK}
        self.dma_sems = {}
        self.nops = 0
        self.last_compute = {}
        self.phase_dmas = {}

    def _deps(self, o, reads, writes):
        deps = []

        def add(d):
            if d is None or d is o:
                return
            if (not d.is_dma) and (not o.is_dma) and d.eng == "pe" and o.eng == "pe":
                return
            for x in deps:
                if x is d:
                    return
            deps.append(d)

        for b in reads:
            add(b.last_w)
        for b in writes:
            add(b.last_w)
            for r in b.readers:
                add(r)
        return deps

    def _commit(self, o, reads, writes):
        for b in reads:
            if not o.is_dma:
                b.readers = [r for r in b.readers if r.is_dma or r.eng != o.eng]
            b.readers.append(o)
        for b in writes:
            b.last_w = o
            b.readers = []
        for d in o.deps:
            d.signal = True
        self.ops[o.eng].append(o)

    def op(self, eng, fn, reads=(), writes=()):
        o = Op()
        o.eng, o.fn, o.signal, o.sig, o.is_dma, o.sem, o.semval = eng, fn, False, 0, False, None, 0
        o.idx = self.nops
        self.nops += 1
        o.deps = self._deps(o, reads, writes)
        self._commit(o, reads, writes)
        if fn is not None:
            self.last_compute[eng] = o
        return o

    def dma(self, queue, fn, reads=(), writes=(), key=None, n=1):
        o = Op()
        o.eng, o.fn, o.signal, o.sig, o.is_dma = queue, fn, False, 0, True
        o.idx = self.nops
        self.nops += 1
        o.deps = self._deps(o, reads, writes)
        if key is None:
            key = writes[0].name
        if key not in self.dma_sems:
            self.dma_sems[key] = [self.nc.alloc_semaphore(name="d_" + key), 0]
        ent = self.dma_sems[key]
        ent[1] += 16 * n
        o.sem, o.semval, o.key = ent[0], ent[1], key
        self._commit(o, reads, writes)
        self.phase_dmas[key] = o
        return o

    def barrier(self):
        lasts = dict(self.last_compute)
        dmas = list(self.phase_dmas.values())
        for e in self.ENG_BLOCK:
            o = Op()
            o.eng, o.fn, o.signal, o.sig, o.is_dma, o.sem, o.semval = e, None, False, 0, False, None, 0
            o.idx = self.nops
            self.nops += 1
            o.deps = [v for k, v in lasts.items() if k != e] + dmas
            for d in o.deps:
                d.signal = True
            self.ops[e].append(o)
        self.phase_dmas = {}

    def emit(self):
        nc = self.nc
        for e, lst in self.ops.items():
            c = 0
            for o in lst:
                if o.is_dma:
                    continue
                if o.signal:
                    c += 1
                    o.sig = c
        with nc.Block() as block:
            for ename, bname in self.ENG_BLOCK.items():
                if not self.ops[ename]:
                    continue
                deco = getattr(block, bname)

                def body(eng, ename=ename):
                    waited = {}
                    mysem = self.prog_sem[ename]
                    for o in self.ops[ename]:
                        for d in o.deps:
                            if d.is_dma:
                                sem, val = d.sem, d.semval
                            else:
                                sem, val = self.prog_sem[d.eng], d.sig
                            k = id(sem)
                            if waited.get(k, 0) < val:
                                eng.wait_ge(sem, val)
                                waited[k] = val
                        if o.fn is None:
                            continue
                        r = o.fn(eng)
                        if o.is_dma:
                            if not isinstance(r, (list, tuple)):
                                r = [r]
                            for ins in r:
                                ins.then_inc(o.sem, 16)
                        elif o.signal:
                            r.then_inc(mysem, 1)

                deco(body)


_CONST = {}


def _lhsT_layout(M):
    A = M.reshape(16, 128, 16, 128)
    return np.ascontiguousarray(A.transpose(2, 1, 0, 3))


def consts():
    if _CONST:
        return _CONST
    bf = ml_dtypes.bfloat16
    n = np.arange(2048, dtype=np.float64)
    phi = math.pi / 4096.0
    th = phi * np.outer(n, 2 * n + 1)
    _CONST["tabF_c"] = _lhsT_layout(np.cos(th)).astype(bf)
    _CONST["tabF_s"] = _lhsT_layout(-np.sin(th)).astype(bf)
    ps = (phi / 2) * np.outer(2 * n + 1, 2 * n + 1)
    Ct = np.cos(ps)
    St = -np.sin(ps)
    _CONST["tabD_c"] = _lhsT_layout(Ct).astype(bf)
    _CONST["tabD_s"] = _lhsT_layout(St).astype(bf)
    _CONST["tabR"] = np.ascontiguousarray(np.stack([Ct, St], 0)).astype(bf)
    _CONST["ident_bf"] = np.eye(128).astype(bf)
    _CONST["ident_f"] = np.eye(128).astype(np.float32)
    _CONST["ones_bf"] = np.ones((128, 128)).astype(bf)
    _CONST["tau_row"] = n.astype(np.float32)[None, :]
    deltas = np.linspace(HY_MIN_DECAY, HY_MAX_DECAY, HWID, dtype=np.float32)
    _CONST["adel_row"] = np.abs(deltas)[None, :].astype(np.float32)
    f32 = np.float32
    t = np.linspace(0.0, 1.0, L, dtype=f32)[:, None]
    w = (2.0 * math.pi * np.arange(L, dtype=f32)[:, None] / L).astype(f32)
    f = np.linspace(1e-4, 15, 16, dtype=f32)[None, :]
    z = np.concatenate([t, np.cos(f * w), -np.sin(f * w)], axis=-1).astype(f32)
    _CONST["zT"] = np.ascontiguousarray(z.T)
    negt = -(np.arange(2048, dtype=np.float64) / (L - 1))
    _CONST["negt"] = negt.reshape(16, 128).T.astype(np.float32)
    nd = -(np.abs(deltas).astype(np.float64) / (L - 1))
    _CONST["negdel"] = nd.reshape(8, 128).T.astype(np.float32)
    return _CONST


def pm(v, nch):
    return np.ascontiguousarray(np.asarray(v).reshape(nch, 128).T)


def make_pvec(inp, rev):
    C = consts()
    pv = np.zeros((128, NPV), np.float32)
    pv[:, 0:16] = pm(inp["mix_pre_g"][0], 16)
    pv[:, 16:32] = pm(inp["mix_post_g"][0], 16)
    pv[:, 32:48] = pm(inp["ffn_pre_g"][0], 16)
    pv[:, 48:64] = pm(inp["ffn_post_g"][0], 16)
    hcw = inp["hy_conv_w"][0]
    if rev:
        hcw = hcw[::-1]
    pv[:, 64:136] = hcw.T.reshape(24, 128, 3).transpose(1, 0, 2).reshape(128, 72)
    pv[:, 136:160] = pm(inp["hy_conv_b"][0], 24)
    cdw = inp["cf_dw_w"][0]
    if rev:
        cdw = cdw[::-1]
    pv[:, 160:408] = cdw.T.reshape(8, 128, 31).transpose(1, 0, 2).reshape(128, 248)
    pv[:, 408:416] = pm(inp["cf_dw_b"][0], 8)
    pv[:, 416:424] = pm(inp["cf_ln_g"][0], 8)
    pv[:, 424:432] = pm(inp["cf_ln_b"][0], 8)
    hb = inp["hy_bias"][0]
    pv[:, 432:448] = hb.reshape(2, 8, 128).transpose(2, 0, 1).reshape(128, 16)
    pv[:, 448] = 0.0 if rev else 1.0
    pv[:, 449] = 1.0 if rev else 0.0
    pv[:, 450:466] = C["negt"]
    pv[:, 466:474] = C["negdel"]
    return pv


PV = dict(g1=0, g2=16, g3=32, g4=48, hcw=64, hcb=136, cdw=160, cdb=408, lng=416, lnb=424, hyb=432,
          flags=448, negt=450, negdel=466)


def build(stages=("F", "N1", "P", "C", "M", "G", "Dn"), dbg=None):
    nc = bass.Bass("TRN2", target_bir_lowering=False)
    P = Prog(nc)

    def din(n, s, dt=F32):
        return nc.dram_tensor(n, list(s), dt, kind="ExternalInput").ap()

    def dscr(n, s, dt=F32):
        return nc.dram_tensor(n, list(s), dt).ap()

    x_d = din("x", [2048, 2048])
    pvec_d = din("pvec", [128, NPV])
    w_in_d = din("w_in", [D, NIN])
    hy_proj_d = din("hy_proj", [HWID, D])
    cf_proj_d = din("cf_proj", [HWID, D])
    w_out_d = din("w_out", [D, D])
    w_gu_d = din("w_gu", [D, 2 * FF])
    w_down_d = din("w_down", [FF, D])
    w1_d = din("fw1", [33, 64])
    fm_d = din("fmlp", [64, 68])
    w3_d = din("fw3", [64, 4096])
    zT_d = din("zT", [33, 2048])
    tabF_d = [din("tabF_c", [16, 128, 16, 128], BF16), din("tabF_s", [16, 128, 16, 128], BF16)]
    tabD_d = [din("tabD_c", [16, 128, 16, 128], BF16), din("tabD_s", [16, 128, 16, 128], BF16)]
    tabR_d = din("tabR", [2, 2048, 2048], BF16)
    identb_d = din("ident_bf", [128, 128], BF16)
    identf_d = din("ident_f", [128, 128])
    onesb_d = din("ones_bf", [128, 128], BF16)
    tau_d = din("tau_row", [1, 2048])
    adel_d = din("adel_row", [1, 1024])
    y_d = nc.dram_tensor("y", [T1, D], F32, kind="ExternalOutput").ap()
    dbg_d = None
    dbg_name = None
    if dbg is not None:
        dbg_name, dshape = dbg
        dbg_d = nc.dram_tensor("dbg", list(dshape), F32, kind="ExternalOutput").ap()

    KT_s = dscr("KT_s", [2, 2, 2048, 1024])
    xT_s = dscr("xT_s", [16, 128, T1])
    v_s = dscr("v_s", [8, 128, 2048])
    x1_s = dscr("x1_s", [8, 128, 2048])
    x2_s = dscr("x2_s", [8, 128, T1])
    yaT_s = dscr("yaT_s", [8, 128, T1], BF16)
    ybT_s = dscr("ybT_s", [8, 128, T1], BF16)
    sg_s = dscr("sg_s", [2, 16, 128, T1])
    r1T_s = dscr("r1T_s", [16, 128, T1])
    act_s = dscr("act_s", [44, 128, T1], BF16)

    pv = nc.alloc_sbuf_tensor("pv", [128, NPV], F32)
    identb = nc.alloc_sbuf_tensor("identb", [128, 128], BF16)
    identf = nc.alloc_sbuf_tensor("identf", [128, 128], F32)
    onesb = nc.alloc_sbuf_tensor("onesb", [128, 128], BF16)
    rl1s = nc.alloc_sbuf_tensor("rl1s", [128, 16], F32)
    epsc = nc.alloc_sbuf_tensor("epsc", [128, 1], F32)
    zeroc = nc.alloc_sbuf_tensor("zeroc", [128, 1], F32)
    ARENA_BYTES = 200 * 1024
    arena = nc.alloc_sbuf_tensor("arena", [128, ARENA_BYTES // 2], BF16)
    psum = nc.alloc_psum_tensor("psum", [128, 8, 512], F32)
    Bc = Buf("const")
    Brl1 = Buf("rl1s")

    class Arena:
        def __init__(self):
            self.off = 0

        def take(self, nelem, dt):
            sz = 2 if dt == BF16 else 4
            nb = (nelem * sz + 63) // 64 * 64
            assert self.off + nb <= ARENA_BYTES, (self.off, nb)
            v = arena[:, self.off // 2:(self.off + nb) // 2]
            self.off += nb
            if dt == F32:
                v = v.bitcast(F32)
            return v[:, 0:nelem]

    def pvs(name, i=0, n=1):
        o = PV[name] + i
        return pv[:, o:o + n]

    def PB():
        return [Buf("pb%d" % i) for i in range(8)]

    P.dma("sp", lambda e: e.dma_start(out=pv[:], in_=pvec_d), writes=[Bc], key="c0")
    P.dma("sp", lambda e: e.dma_start(out=identb[:], in_=identb_d), writes=[Bc], key="c1")
    P.dma("sp", lambda e: e.dma_start(out=identf[:], in_=identf_d), writes=[Bc], key="c2")
    P.dma("sp", lambda e: e.dma_start(out=onesb[:], in_=onesb_d), writes=[Bc], key="c3")
    P.op("dve", lambda e: e.memset(epsc[:], EPS), writes=[Bc])
    P.op("dve", lambda e: e.memset(zeroc[:], 0.0), writes=[Bc])
    P.barrier()
    Bc = Buf("const")

    def phase_F():
        A = Arena()
        zT = A.take(2048, F32)
        h1T = A.take(2048, F32)
        h2T = A.take(2048, BF16)
        w3f = A.take(4096, F32)
        w3 = A.take(4096, BF16)
        w3e = A.take(2048, BF16)
        w3d = A.take(2048, BF16)
        w1 = A.take(64, F32)
        fm = A.take(68, F32)
        cc12 = A.take(2, F32)
        tau_bc = A.take(2048, F32)
        adel_bc = A.take(1024, F32)
        arg = [A.take(512, F32) for _ in range(2)]
        l1p = A.take(128, F32)
        l1q = A.take(32, F32)
        l1t = A.take(16, F32)
        junk = A.take(512, F32)
        wincm = A.take(2048, F32)
        wintm = [A.take(1024, F32) for _ in range(2)]
        tmpf = [A.take(512, F32) for _ in range(2)]
        tmpb = [A.take(512, F32) for _ in range(2)]
        tmpa = [A.take(512, F32) for _ in range(2)]
        ke = A.take(16 * 1024, BF16).rearrange("p (a b) -> p a b", b=1024)
        kd = A.take(16 * 1024, BF16).rearrange("p (a b) -> p a b", b=1024)
        tsl = [A.take(16 * 128, BF16).rearrange("p (a b) -> p a b", b=128) for _ in range(4)]
        ksb = [A.take(1024, F32) for _ in range(2)]
        pb = PB()
        Bz, Bw = Buf("Fz"), Buf("Fw")
        Bh1, Bh2 = Buf("h1"), Buf("h2")
        Barg = [Buf("arg0"), Buf("arg1")]
        Bjk = Buf("jk")
        P.dma("sp", lambda e: e.dma_start(out=zT[0:33, :], in_=zT_d), writes=[Bz], key="f0")
        P.dma("sp", lambda e: e.dma_start(out=w1[0:33, :], in_=w1_d), writes=[Bw], key="f1")
        P.dma("sp", lambda e: e.dma_start(out=fm[0:64, :], in_=fm_d), writes=[Bw], key="f2")
        Bw3f = Buf("w3f")
        P.dma("sp", lambda e: e.dma_start(out=w3f[0:64, :], in_=w3_d), writes=[Bw3f], key="f3")
        P.op("dve", lambda e: e.memset(w3, 0.0), writes=[Bw])
        P.op("dve", lambda e: e.tensor_copy(out=w3[0:64, :], in_=w3f[0:64, :]), reads=[Bw3f, Bw], writes=[Bw])
        P.op("dve", lambda e: e.memset(h2T, 0.0), writes=[Bh2])
        Bw3ed = Buf("w3ed")
        P.op("dve", lambda e: e.memset(w3e, 0.0), writes=[Bw3ed])
        P.op("dve", lambda e: e.memset(w3d, 0.0), writes=[Bw3ed])
        for o_ in range(2):
            cf_, cb_2 = (2 * o_) * 1024, (2 * o_ + 1) * 1024
            P.op("dve", lambda e, o_=o_, cf_=cf_, cb_2=cb_2: e.tensor_tensor(out=w3e[0:64, o_ * 1024:(o_ + 1) * 1024], in0=w3f[0:64, cf_:cf_ + 1024], in1=w3f[0:64, cb_2:cb_2 + 1024], op=ALU.add),
                 reads=[Bw3f, Bw3ed], writes=[Bw3ed])
            P.op("dve", lambda e, o_=o_, cf_=cf_, cb_2=cb_2: e.tensor_tensor(out=w3d[0:64, o_ * 1024:(o_ + 1) * 1024], in0=w3f[0:64, cf_:cf_ + 1024], in1=w3f[0:64, cb_2:cb_2 + 1024], op=ALU.subtract),
                 reads=[Bw3f, Bw3ed], writes=[Bw3ed])
        P.dma("sp", lambda e: e.dma_start(out=tau_bc, in_=tau_d.partition_broadcast(128)), writes=[Bw], key="f4")
        P.dma("sp", lambda e: e.dma_start(out=adel_bc, in_=adel_d.partition_broadcast(128)), writes=[Bw], key="f5")
        Bcc = Buf("cc")
        P.op("dve", lambda e: e.tensor_tensor(out=cc12[0:64, 0:1], in0=fm[0:64, 64:65], in1=fm[0:64, 65:66], op=ALU.mult),
             reads=[Bw], writes=[Bcc])
        P.op("dve", lambda e: e.tensor_tensor(out=cc12[0:64, 1:2], in0=fm[0:64, 66:67], in1=fm[0:64, 67:68], op=ALU.mult),
             reads=[Bw, Bcc], writes=[Bcc])
        cnt = 0
        for layer in range(2):
            src = zT if layer == 0 else h1T
            dst = h1T if layer == 0 else h2T
            Bs = Bz if layer == 0 else Bh1
            Bd = Bh1 if layer == 0 else Bh2
            kk = 33 if layer == 0 else 64
            wl = w1[0:33, 0:64] if layer == 0 else fm[0:64, 0:64]
            frc = fm[0:64, 65:66] if layer == 0 else fm[0:64, 67:68]
            cb_ = cc12[0:64, layer:layer + 1]
            for tb in range(4):
                bk = cnt % 2
                cnt += 1
                ts = slice(tb * 512, (tb + 1) * 512)
                P.op("pe", lambda e, bk=bk, ts=ts, src=src, kk=kk, wl=wl: e.matmul(psum[0:64, bk, :], lhsT=wl, rhs=src[0:kk, ts], start=True, stop=True),
                     reads=[Bs, Bw], writes=[pb[bk]])
                P.op("act", lambda e, bk=bk, frc=frc, cb_=cb_: e.activation(out=arg[bk][0:64, :], in_=psum[0:64, bk, :], func=AF.Identity, scale=frc, bias=cb_),
                     reads=[pb[bk], Bw, Bcc], writes=[Barg[bk]])
                for _ in range(2):
                    for (thr, cmp_, per) in ((math.pi, ALU.is_gt, -2 * math.pi), (-math.pi, ALU.is_lt, 2 * math.pi)):
                        P.op("dve", lambda e, bk=bk, thr=thr, cmp_=cmp_, per=per: e.tensor_scalar(out=junk[0:64, :], in0=arg[bk][0:64, :], scalar1=thr, scalar2=per, op0=cmp_, op1=ALU.mult),
                             reads=[Barg[bk]], writes=[Bjk])
                        P.op("dve", lambda e, bk=bk: e.tensor_tensor(out=arg[bk][0:64, :], in0=arg[bk][0:64, :], in1=junk[0:64, :], op=ALU.add),
                             reads=[Barg[bk], Bjk], writes=[Barg[bk]])
                P.op("act", lambda e, bk=bk, dst=dst, ts=ts: e.activation(out=dst[0:64, ts], in_=arg[bk][0:64, :], func=AF.Sin),
                     reads=[Barg[bk]], writes=[Bd])
        Bwin, Bl1p = Buf("wincm"), Buf("l1p")
        Btmp = [Buf("tmpa0"), Buf("tmpa1")]
        Bjunk = Buf("junk")
        cnt = 0
        for cc in range(8):
            P.op("act", lambda e, cc=cc: e.activation(out=wincm, in_=tau_bc, func=AF.Exp, scale=pvs("negdel", cc)),
                 reads=[Bw, Bc], writes=[Bwin])
            for o in range(2):
                for dr in range(2):
                    q = o * 16 + dr * 8 + cc
                    for tb in range(4):
                        bk = 2 + cnt % 2
                        tm = cnt % 2
                        cnt += 1
                        ts = slice(tb * 512, (tb + 1) * 512)
                        P.op("pe", lambda e, bk=bk, q=q, ts=ts: e.matmul(psum[:, bk, :], lhsT=w3[:, q * 128:(q + 1) * 128], rhs=h2T[:, ts], start=True, stop=True),
                             reads=[Bw, Bh2], writes=[pb[bk]])
                        P.op("dve", lambda e, bk=bk, tm=tm, ts=ts: e.tensor_tensor(out=tmpa[tm], in0=psum[:, bk, :], in1=wincm[:, ts], op=ALU.mult),
                             reads=[pb[bk], Bwin], writes=[Btmp[tm]])
                        if tb == 0:
                            P.op("dve", lambda e, tm=tm, dr=dr: e.tensor_scalar(out=tmpa[tm][:, 0:1], in0=tmpa[tm][:, 0:1], scalar1=pvs("flags", dr), scalar2=None, op0=ALU.mult),
                                 reads=[Btmp[tm], Bc], writes=[Btmp[tm]])
                        P.op("act", lambda e, tm=tm, q=q, tb=tb: e.activation(out=junk, in_=tmpa[tm], func=AF.Abs, accum_out=l1p[:, q * 4 + tb:q * 4 + tb + 1]),
                             reads=[Btmp[tm]], writes=[Bjunk, Bl1p])
        P.op("dve", lambda e: e.tensor_reduce(out=l1q, in_=l1p.rearrange("p (a b) -> p a b", b=4), axis=AX.X, op=ALU.add),
             reads=[Bl1p], writes=[Bl1p])
        for o in range(2):
            P.op("dve", lambda e, o=o: e.tensor_tensor(out=l1t[:, o * 8:(o + 1) * 8], in0=l1q[:, o * 16:o * 16 + 8], in1=l1q[:, o * 16 + 8:o * 16 + 16], op=ALU.add),
                 reads=[Bl1p], writes=[Bl1p])
        P.op("dve", lambda e: e.reciprocal(out=l1t, in_=l1t), reads=[Bl1p], writes=[Bl1p])
        P.op("dve", lambda e: e.tensor_scalar(out=rl1s[:], in0=l1t, scalar1=1.0 / 2048.0, scalar2=None, op0=ALU.mult),
             reads=[Bl1p], writes=[Brl1])
        Bwt = [Buf("wintm0"), Buf("wintm1")]
        Btf = [Buf("tmpf0"), Buf("tmpf1")]
        Btb = [Buf("tmpb0"), Buf("tmpb1")]
        Bke, Bkd = Buf("ke"), Buf("kd")
        Bts = [Buf("tsl%d" % i) for i in range(4)]
        Bks = [Buf("ksb0"), Buf("ksb1")]
        BKT = Buf("KT")
        cnt = 0
        tcnt = 0
        kcnt = 0
        iters = [(part, fc) for part in range(2) for fc in range(16)]
        PF = 3
        tbase = 0
        for o in range(2):
            def issue_tsl(i, tbase=tbase):
                part_, fc_ = iters[i]
                s3_ = (tbase + i) % 4
                P.dma("sp", lambda e, s3_=s3_, part_=part_, fc_=fc_: e.dma_start(out=tsl[s3_], in_=tabF_d[part_][fc_]), writes=[Bts[s3_]])
            for i in range(PF):
                issue_tsl(i)
            for tc in range(16):
                wi = tc % 2
                P.op("act", lambda e, wi=wi, tc=tc: e.activation(out=wintm[wi], in_=adel_bc, func=AF.Exp, scale=pvs("negt", tc)),
                     reads=[Bw, Bc], writes=[Bwt[wi]])
                for half in range(2):
                    colf = (o * 2 + 0) * 1024 + half * 512
                    colb = (o * 2 + 1) * 1024 + half * 512
                    i2 = cnt % 2
                    cnt += 1
                    bF, bB = 4 + 2 * i2, 5 + 2 * i2
                    tsl_ = slice(tc * 128, (tc + 1) * 128)
                    hs = slice(half * 512, (half + 1) * 512)
                    if tc >= 1:
                        colE = o * 1024 + half * 512
                        P.op("pe", lambda e, bF=bF, tsl_=tsl_, colE=colE: e.matmul(psum[:, bF, :], lhsT=h2T[:, tsl_], rhs=w3e[:, colE:colE + 512], start=True, stop=True),
                             reads=[Bw3ed, Bh2], writes=[pb[bF]])
                        P.op("pe", lambda e, bB=bB, tsl_=tsl_, colE=colE: e.matmul(psum[:, bB, :], lhsT=h2T[:, tsl_], rhs=w3d[:, colE:colE + 512], start=True, stop=True),
                             reads=[Bw3ed, Bh2], writes=[pb[bB]])
                        P.op("dve", lambda e, bF=bF, tc=tc, wi=wi, hs=hs: e.tensor_tensor(out=ke[:, tc, hs], in0=psum[:, bF, :], in1=wintm[wi][:, hs], op=ALU.mult),
                             reads=[pb[bF], Bwt[wi]], writes=[Bke])
                        P.op("dve", lambda e, bB=bB, tc=tc, wi=wi, hs=hs: e.tensor_tensor(out=kd[:, tc, hs], in0=psum[:, bB, :], in1=wintm[wi][:, hs], op=ALU.mult),
                             reads=[pb[bB], Bwt[wi]], writes=[Bkd])
                        continue
                    P.op("pe", lambda e, bF=bF, tsl_=tsl_, colf=colf: e.matmul(psum[:, bF, :], lhsT=h2T[:, tsl_], rhs=w3[:, colf:colf + 512], start=True, stop=True),
                         reads=[Bw, Bh2], writes=[pb[bF]])
                    P.op("pe", lambda e, bB=bB, tsl_=tsl_, colb=colb: e.matmul(psum[:, bB, :], lhsT=h2T[:, tsl_], rhs=w3[:, colb:colb + 512], start=True, stop=True),
                         reads=[Bw, Bh2], writes=[pb[bB]])
                    P.op("dve", lambda e, bF=bF, i2=i2, wi=wi, hs=hs: e.tensor_tensor(out=tmpf[i2], in0=psum[:, bF, :], in1=wintm[wi][:, hs], op=ALU.mult),
                         reads=[pb[bF], Bwt[wi]], writes=[Btf[i2]])
                    P.op("dve", lambda e, bB=bB, i2=i2, wi=wi, hs=hs: e.tensor_tensor(out=tmpb[i2], in0=psum[:, bB, :], in1=wintm[wi][:, hs], op=ALU.mult),
                         reads=[pb[bB], Bwt[wi]], writes=[Btb[i2]])
                    if tc == 0:
                        P.op("dve", lambda e, i2=i2: e.tensor_scalar(out=tmpf[i2][0:1, :], in0=tmpf[i2][0:1, :], scalar1=pv[0:1, PV["flags"]:PV["flags"] + 1], scalar2=None, op0=ALU.mult),
                             reads=[Btf[i2], Bc], writes=[Btf[i2]])
                        P.op("dve", lambda e, i2=i2: e.tensor_scalar(out=tmpb[i2][0:1, :], in0=tmpb[i2][0:1, :], scalar1=pv[0:1, PV["flags"] + 1:PV["flags"] + 2], scalar2=None, op0=ALU.mult),
                             reads=[Btb[i2], Bc], writes=[Btb[i2]])
                    P.op("pool", lambda e, i2=i2, tc=tc, hs=hs: e.tensor_tensor(out=ke[:, tc, hs], in0=tmpf[i2], in1=tmpb[i2], op=ALU.add),
                         reads=[Btf[i2], Btb[i2]], writes=[Bke])
                    P.op("pool", lambda e, i2=i2, tc=tc, hs=hs: e.tensor_tensor(out=kd[:, tc, hs], in0=tmpf[i2], in1=tmpb[i2], op=ALU.subtract),
                         reads=[Btf[i2], Btb[i2]], writes=[Bkd])
            for i, (part, fc) in enumerate(iters):
                src, Bsrc = (ke, Bke) if part == 0 else (kd, Bkd)
                if i + PF < len(iters):
                    issue_tsl(i + PF)
                s3 = (tbase + i) % 4
                ks = kcnt % 2
                kcnt += 1
                for cb in range(2):
                    bk = (kcnt * 2 + cb) % 4
                    for tc in range(16):
                        P.op("pe", lambda e, bk=bk, s3=s3, tc=tc, cb=cb, src=src: e.matmul(psum[:, bk, :], lhsT=tsl[s3][:, tc, :], rhs=src[:, tc, cb * 512:(cb + 1) * 512], start=(tc == 0), stop=(tc == 15)),
                             reads=[Bts[s3], Bsrc], writes=[pb[bk]])
                    if cb == 0:
                        P.op("act", lambda e, bk=bk, ks=ks, cb=cb: e.copy(out=ksb[ks][:, cb * 512:(cb + 1) * 512], in_=psum[:, bk, :]),
                             reads=[pb[bk]], writes=[Bks[ks]])
                    else:
                        P.op("dve", lambda e, bk=bk, ks=ks, cb=cb: e.tensor_copy(out=ksb[ks][:, cb * 512:(cb + 1) * 512], in_=psum[:, bk, :]),
                             reads=[pb[bk]], writes=[Bks[ks]])
                P.dma("sp", lambda e, ks=ks, o=o, part=part, fc=fc: e.dma_start(out=KT_s[o, part, fc * 128:(fc + 1) * 128, :], in_=ksb[ks]),
                      reads=[Bks[ks]], writes=[BKT], key="ksb_st%d" % ks)
            tbase += len(iters)
        P.barrier()

    if "F" in stages:
        phase_F()

    TAIL_OFF = 168 * 1024
    uT = arena[:, TAIL_OFF // 2:(TAIL_OFF + 32768) // 2].rearrange("p (a b) -> p a b", b=1024)
    BuT = [Buf("uT0"), Buf("uT1")]
    z_s = dscr("z_s", [8, 128, T1])
    cpy_cnt = [0]

    def evac(out, in_, Bin, Bout):
        cpy_cnt[0] += 1
        if cpy_cnt[0] % 2:
            P.op("act", lambda e: e.copy(out=out, in_=in_), reads=Bin, writes=Bout)
        else:
            P.op("dve", lambda e: e.tensor_copy(out=out, in_=in_), reads=Bin, writes=Bout)

    def rms_stats(sq3, nch, ncol, bank, pbk, Bsq, rstd, Brstd):
        for c in range(nch):
            P.op("pe", lambda e, c=c: e.matmul(psum[:, bank, 0:ncol], lhsT=onesb[:], rhs=sq3[:, c, :], start=(c == 0), stop=(c == nch - 1)),
                 reads=[Bsq, Bc], writes=[pbk])
        P.op("act", lambda e: e.activation(out=rstd, in_=psum[:, bank, 0:ncol], func=AF.Sqrt, scale=1.0 / (nch * 128), bias=epsc[:, 0:1]),
             reads=[pbk, Bc], writes=[Brstd])
        P.op("dve", lambda e: e.reciprocal(out=rstd, in_=rstd), reads=[Brstd], writes=[Brstd])

    def phase_N1():
        A = Arena()
        hT = A.take(16 * 2048, BF16).rearrange("p (a b) -> p a b", b=2048)
        xt = [A.take(2048, F32) for _ in range(4)]
        xTb_ = [A.take(16 * 512, F32).rearrange("p (a b) -> p a b", b=512) for _ in range(2)]
        sq_ = [A.take(16 * 512, BF16).rearrange("p (a b) -> p a b", b=512) for _ in range(2)]
        rstd_ = [A.take(512, F32) for _ in range(2)]
        pb = PB()
        Bxt = [Buf("xt%d" % i) for i in range(4)]
        BhT, BxTs = Buf("hT"), Buf("xTs")
        BxTb_ = [Buf("xTb0"), Buf("xTb1")]
        Bsq_ = [Buf("sqN0"), Buf("sqN1")]
        Brs_ = [Buf("rstdN0"), Buf("rstdN1")]
        bcnt = 0
        for tb in range(4):
            xTb, sq, rstd = xTb_[tb % 2], sq_[tb % 2], rstd_[tb % 2]
            BxTb, Bsq, Brs = BxTb_[tb % 2], Bsq_[tb % 2], Brs_[tb % 2]
            for i in range(4):
                tile = tb * 4 + i
                s = tile % 4
                P.dma("sp", lambda e, s=s, tile=tile: e.dma_start(out=xt[s], in_=x_d[tile * 128:(tile + 1) * 128, :]), writes=[Bxt[s]])
                for g4 in range(4):
                    bank = bcnt % 4
                    bcnt += 1
                    for q in range(4):
                        dc = g4 * 4 + q
                        P.op("pe", lambda e, s=s, bank=bank, q=q, dc=dc: e.transpose(out=psum[:, bank, q * 128:(q + 1) * 128], in_=xt[s][:, dc * 128:(dc + 1) * 128], identity=identf[:]),
                             reads=[Bxt[s], Bc], writes=[pb[bank]])
                    evac(xTb[:, g4 * 4:(g4 + 1) * 4, i * 128:(i + 1) * 128], psum[:, bank, :].rearrange("p (a b) -> p a b", b=128), [pb[bank]], [BxTb])
            for g4 in range(4):
                P.op("act", lambda e, g4=g4, sq=sq, xTb=xTb: e.activation(out=sq[:, g4 * 4:(g4 + 1) * 4, :], in_=xTb[:, g4 * 4:(g4 + 1) * 4, :], func=AF.Square),
                     reads=[BxTb], writes=[Bsq])
            rms_stats(sq, 16, 512, 4 + tb % 2, pb[4 + tb % 2], Bsq, rstd, Brs)
            for dc in range(16):
                P.op("dve", lambda e, dc=dc, tb=tb, xTb=xTb, rstd=rstd: e.scalar_tensor_tensor(out=hT[:, dc, tb * 512:(tb + 1) * 512], in0=xTb[:, dc, :], scalar=pvs("g1", dc), in1=rstd, op0=ALU.mult, op1=ALU.mult),
                     reads=[BxTb, Brs, Bc], writes=[BhT])
            if tb < 2:
                P.dma("sp", lambda e, tb=tb, xTb=xTb: e.dma_start(out=xT_s[:, :, tb * 512:(tb + 1) * 512].rearrange("c p t -> p c t"), in_=xTb),
                      reads=[BxTb], writes=[BxTs], key="xTb_st%d" % (tb % 2))
        P.barrier()

    def phase_P():
        A = Arena()
        hT = A.take(16 * 2048, BF16).rearrange("p (a b) -> p a b", b=2048)
        wsl = [A.take(16 * 512, BF16).rearrange("p (a b) -> p a b", b=512) for _ in range(2)]
        sgt = [A.take(512, F32) for _ in range(2)]
        mark = A.off
        p_sb = [A.take(2050, F32) for _ in range(2)]
        acc = [A.take(2048, F32) for _ in range(2)]
        accb = A.take(2048, BF16)
        assert A.off <= TAIL_OFF, A.off
        A.off = mark
        upad = A.take(15 + 1056 + 1, BF16)
        dg = A.take(31 * 128, BF16).rearrange("p (a b) -> p a b", b=128)
        ucv = A.take(8 * 1024, F32).rearrange("p (a b) -> p a b", b=1024)
        ub = A.take(8 * 512, BF16).rearrange("p (a b) -> p a b", b=512)
        usq = A.take(8 * 512, BF16).rearrange("p (a b) -> p a b", b=512)
        mu = A.take(512, F32)
        m2 = A.take(512, F32)
        rstd = A.take(512, F32)
        tq = sgt
        ybt = [A.take(512, BF16) for _ in range(2)]
        assert A.off <= TAIL_OFF, A.off
        pb = PB()
        BhT = Buf("hT")
        Bws = [Buf("wsl0"), Buf("wsl1")]
        Bp = [Buf("p_sb0"), Buf("p_sb1")]
        Bacc = [Buf("acc0"), Buf("acc1")]
        Baccb, Bup, Bdg = Buf("accb"), Buf("upad"), Buf("dg")
        Bsg = [Buf("sgt0"), Buf("sgt1")]
        Bucv, Bub, Busq, Bmu, Bm2, Brs = Buf("ucv"), Buf("ub"), Buf("usq"), Buf("mu"), Buf("m2"), Buf("rstdP")
        Btq = Bsg
        Bybt = [Buf("ybt0"), Buf("ybt1")]
        Bvs, Bx1s, Bx2s, Bybs, Bsgs = Buf("v_s"), Buf("x1_s"), Buf("x2_s"), Buf("ybT_s"), Buf("sg_s")
        for i in range(2):
            P.op("dve", lambda e, i=i: e.memset(p_sb[i], 0.0), writes=[Bp[i]])
        wcnt = [0]
        bcnt = [0]

        def loadw(cols):
            s = wcnt[0] % 2
            wcnt[0] += 1

            def fn(e, s=s, cols=cols):
                r = []
                o = 0
                for (c0, n) in cols:
                    r.append(e.dma_start(out=wsl[s][:, :, o:o + n], in_=w_in_d[:, c0:c0 + n].rearrange("(k p) n -> p k n", p=128)))
                    o += n
                return r
            P.dma("pool", fn, writes=[Bws[s]], n=len(cols))
            return s

        def mm16(bank, n, s, col, t0):
            for k in range(16):
                P.op("pe", lambda e, k=k: e.matmul(psum[:, bank, 0:n], lhsT=wsl[s][:, k, col:col + 128], rhs=hT[:, k, t0:t0 + n], start=(k == 0), stop=(k == 15)),
                     reads=[Bws[s], BhT], writes=[pb[bank]])

        def nextbank(lo=0, n=4):
            b = lo + bcnt[0] % n
            bcnt[0] += 1
            return b

        pending = []
        for blk in range(6):
            s = loadw([(blk * 512, 512)])
            for q in range(4):
                ch = blk * 4 + q
                kind = ch // 8
                i2 = ch % 2
                groups = [(0, 512), (512, 512), (1024, 512), (1536, 512)] if kind < 2 else [(0, 512), (512, 512), (1024, 32)]
                for (t0, n) in groups:
                    bank = nextbank()
                    mm16(bank, n, s, q * 128, t0)
                    evac(p_sb[i2][:, 1 + t0:1 + t0 + n], psum[:, bank, 0:n], [pb[bank]], [Bp[i2]])
                while pending:
                    pending.pop(0)()
                nt = 2048 if kind < 2 else 1024
                a = acc[i2][:, 0:nt]
                P.op("act", lambda e, a=a, i2=i2, nt=nt, ch=ch: e.activation(out=a, in_=p_sb[i2][:, 1:1 + nt], func=AF.Identity, scale=pvs("hcw", ch * 3 + 1), bias=pvs("hcb", ch)),
                     reads=[Bp[i2], Bc], writes=[Bacc[i2]])
                P.op("dve", lambda e, a=a, i2=i2, nt=nt, ch=ch: e.scalar_tensor_tensor(out=a, in0=p_sb[i2][:, 0:nt], scalar=pvs("hcw", ch * 3 + 0), in1=a, op0=ALU.mult, op1=ALU.add),
                     reads=[Bp[i2], Bc, Bacc[i2]], writes=[Bacc[i2]])
                P.op("dve", lambda e, a=a, i2=i2, nt=nt, ch=ch: e.scalar_tensor_tensor(out=a, in0=p_sb[i2][:, 2:2 + nt], scalar=pvs("hcw", ch * 3 + 2), in1=a, op0=ALU.mult, op1=ALU.add),
                     reads=[Bp[i2], Bc, Bacc[i2]], writes=[Bacc[i2]])
                dst = (v_s, x1_s, x2_s)[kind][ch % 8]
                Bd = (Bvs, Bx1s, Bx2s)[kind]
                P.dma("sp", lambda e, a=a, dst=dst: e.dma_start(out=dst, in_=a), reads=[Bacc[i2]], writes=[Bd], key="acc_st%d" % i2)
                if kind == 0:
                    P.op("pool", lambda e, i2=i2: e.tensor_copy(out=accb, in_=acc[i2]), reads=[Bacc[i2]], writes=[Baccb])

                    def do_tr(ch=ch):
                        psT = psum[:, 4:6, :].rearrange("p a b -> p (a b)").bitcast(BF16).rearrange("p (a b) -> p a b", b=128)
                        for tc in range(16):
                            P.op("pe", lambda e, tc=tc: e.transpose(out=psT[:, tc, :], in_=accb[:, tc * 128:(tc + 1) * 128], identity=identb[:]),
                                 reads=[Baccb, Bc], writes=[pb[4], pb[5]])
                        P.op("act", lambda e, ch=ch: e.copy(out=uT[:, :, ch * 128:(ch + 1) * 128], in_=psT), reads=[pb[4], pb[5]], writes=[BuT[ch // 4]])
                    pending.append(do_tr)
        while pending:
            pending.pop(0)()
        P.barrier()
        P.op("dve", lambda e: e.memset(upad, 0.0), writes=[Bup])
        for pg in range(4):
            s = loadw([(3072 + pg * 256, 256), (4096 + pg * 256, 256)])
            for i in range(2):
                cc = pg * 2 + i
                for (t0, n) in [(0, 512), (512, 512), (1024, 32)]:
                    bA = nextbank()
                    mm16(bA, n, s, i * 128, t0)
                    bB = nextbank()
                    mm16(bB, n, s, 256 + i * 128, t0)
                    g2 = bB % 2
                    P.op("act", lambda e, bB=bB, n=n, g2=g2: e.activation(out=sgt[g2][:, 0:n], in_=psum[:, bB, 0:n], func=AF.Sigmoid),
                         reads=[pb[bB]], writes=[Bsg[g2]])
                    P.op("dve", lambda e, bA=bA, n=n, g2=g2, t0=t0: e.tensor_tensor(out=upad[:, 15 + t0:15 + t0 + n], in0=psum[:, bA, 0:n], in1=sgt[g2][:, 0:n], op=ALU.mult),
                         reads=[pb[bA], Bsg[g2]], writes=[Bup])
                for k in range(31):
                    P.op("act", lambda e, k=k, cc=cc: e.activation(out=dg[:, k, :], in_=identb[:], func=AF.Identity, scale=pvs("cdw", cc * 31 + k), bias=zeroc[:, 0:1]),
                         reads=[Bc], writes=[Bdg])
                for tb in range(2):
                    bank = nextbank()
                    for k in range(31):
                        P.op("pe", lambda e, k=k, tb=tb, bank=bank: e.matmul(psum[:, bank, :], lhsT=dg[:, k, :], rhs=upad[:, tb * 512 + k:tb * 512 + k + 512], start=(k == 0), stop=(k == 30)),
                             reads=[Bdg, Bup], writes=[pb[bank]])
                    P.op("act", lambda e, tb=tb, bank=bank, cc=cc: e.activation(out=ucv[:, cc, tb * 512:(tb + 1) * 512], in_=psum[:, bank, :], func=AF.Identity, scale=1.0, bias=pvs("cdb", cc)),
                         reads=[pb[bank], Bc], writes=[Bucv])
        for tb in range(2):
            tsl_ = slice(tb * 512, (tb + 1) * 512)
            P.op("act", lambda e, tsl_=tsl_: e.activation(out=usq, in_=ucv[:, :, tsl_], func=AF.Square), reads=[Bucv], writes=[Busq])
            P.op("pool", lambda e, tsl_=tsl_: e.tensor_copy(out=ub, in_=ucv[:, :, tsl_]), reads=[Bucv], writes=[Bub])
            for c in range(8):
                P.op("pe", lambda e, c=c: e.matmul(psum[:, 6, :], lhsT=onesb[:], rhs=ub[:, c, :], start=(c == 0), stop=(c == 7)), reads=[Bub, Bc], writes=[pb[6]])
            for c in range(8):
                P.op("pe", lambda e, c=c: e.matmul(psum[:, 7, :], lhsT=onesb[:], rhs=usq[:, c, :], start=(c == 0), stop=(c == 7)), reads=[Busq, Bc], writes=[pb[7]])
            P.op("dve", lambda e: e.tensor_scalar(out=mu, in0=psum[:, 6, :], scalar1=1.0 / 1024, scalar2=None, op0=ALU.mult), reads=[pb[6]], writes=[Bmu])
            P.op("dve", lambda e: e.tensor_tensor(out=m2, in0=mu, in1=mu, op=ALU.mult), reads=[Bmu], writes=[Bm2])
            P.op("dve", lambda e: e.scalar_tensor_tensor(out=m2, in0=psum[:, 7, :], scalar=1.0 / 1024, in1=m2, op0=ALU.mult, op1=ALU.subtract), reads=[pb[7], Bm2], writes=[Bm2])
            P.op("act", lambda e: e.activation(out=rstd, in_=m2, func=AF.Sqrt, scale=1.0, bias=epsc[:, 0:1]), reads=[Bm2, Bc], writes=[Brs])
            P.op("dve", lambda e: e.reciprocal(out=rstd, in_=rstd), reads=[Brs], writes=[Brs])
            for c in range(8):
                j = c % 2
                P.op("dve", lambda e, c=c, j=j, tsl_=tsl_: e.tensor_tensor(out=tq[j], in0=ucv[:, c, tsl_], in1=mu, op=ALU.subtract), reads=[Bucv, Bmu], writes=[Btq[j]])
                P.op("dve", lambda e, j=j: e.tensor_tensor(out=tq[j], in0=tq[j], in1=rstd, op=ALU.mult), reads=[Btq[j], Brs], writes=[Btq[j]])
                P.op("act", lambda e, c=c, j=j: e.activation(out=ybt[j], in_=tq[j], func=AF.Silu, scale=pvs("lng", c), bias=pvs("lnb", c)), reads=[Btq[j], Bc], writes=[Bybt[j]])
                P.dma("sp", lambda e, c=c, j=j, tsl_=tsl_: e.dma_start(out=ybT_s[c][:, tsl_], in_=ybt[j]), reads=[Bybt[j]], writes=[Bybs], key="ybt_st%d" % j)
        for blk in range(8):
            s = loadw([(5120 + blk * 512, 512)])
            for q in range(4):
                g = blk * 4 + q
                for tb in range(2):
                    bank = nextbank()
                    mm16(bank, 512, s, q * 128, tb * 512)
                    g2 = bank % 2
                    P.op("act", lambda e, bank=bank, g2=g2: e.activation(out=sgt[g2], in_=psum[:, bank, :], func=AF.Sigmoid), reads=[pb[bank]], writes=[Bsg[g2]])
                    P.dma("sp", lambda e, g=g, tb=tb, g2=g2: e.dma_start(out=sg_s[g // 16, g % 16][:, tb * 512:(tb + 1) * 512], in_=sgt[g2]), reads=[Bsg[g2]], writes=[Bsgs], key="sgt_st%d" % g2)
        P.barrier()

    def phase_C(ci):
        A = Arena()
        Y = A.take(2 * 16 * 512, BF16).rearrange("p (a b c) -> p a b c", a=2, b=16)
        tsc = [A.take(16 * 128, BF16).rearrange("p (a b) -> p a b", b=128) for _ in range(2)]
        tss = [A.take(16 * 128, BF16).rearrange("p (a b) -> p a b", b=128) for _ in range(2)]
        kr = [A.take(512, F32) for _ in range(2)]
        ki = [A.take(512, F32) for _ in range(2)]
        t1, t2, t3, t4 = [A.take(512, F32) for _ in range(4)]
        rt = [A.take(32 * 512, BF16).rearrange("p (a b) -> p a b", b=512) for _ in range(2)]
        vt = [A.take(512, F32) for _ in range(2)]
        xg = [A.take(512, F32) for _ in range(2)]
        zt = [A.take(512, F32) for _ in range(2)]
        zb = [A.take(512, BF16) for _ in range(2)]
        assert A.off <= TAIL_OFF, A.off
        pb = PB()
        BY = Buf("Y")
        Btc = [Buf("tsc0"), Buf("tsc1")]
        Bts = [Buf("tss0"), Buf("tss1")]
        Bkr = [Buf("kr0"), Buf("kr1")]
        Bki = [Buf("ki0"), Buf("ki1")]
        Bt = [Buf("t1"), Buf("t2"), Buf("t3"), Buf("t4")]
        Brt = [Buf("rt0"), Buf("rt1")]
        Bvt = [Buf("vt0"), Buf("vt1")]
        Bxg = [Buf("xg0"), Buf("xg1")]
        Bzt = [Buf("zt0"), Buf("zt1")]
        Bzb = [Buf("zb0"), Buf("zb1")]
        Bzs, Byas = Buf("z_s"), Buf("yaT_s")
        fcnt = 0
        rcnt = 0
        ecnt = 0
        ntb = 4 if ci == 0 else 2

        def load_rt(r, tb):
            ts2 = slice(tb * 512, (tb + 1) * 512)

            def ld(e, r=r, ts2=ts2):
                return [e.dma_start(out=rt[r][:, part * 16:(part + 1) * 16, :], in_=tabR_d[part].rearrange("(fc p) t -> p fc t", p=128)[:, :, ts2]) for part in range(2)]
            P.dma("sp", ld, writes=[Brt[r]], n=2)

        for hh in range(2):
            hs = slice(hh * 512, (hh + 1) * 512)
            rslots = [(rcnt + t) % 2 for t in range(ntb)]
            rcnt += ntb
            load_rt(rslots[0], 0)
            for fc in range(16):
                s = fcnt % 2
                fcnt += 1
                P.dma("sp", lambda e, s=s, fc=fc: e.dma_start(out=tsc[s], in_=tabD_d[0][fc]), writes=[Btc[s]])
                P.dma("sp", lambda e, s=s, fc=fc: e.dma_start(out=tss[s], in_=tabD_d[1][fc]), writes=[Bts[s]])
                P.dma("sp", lambda e, s=s, fc=fc, hs=hs: e.dma_start(out=kr[s], in_=KT_s[ci, 0, fc * 128:(fc + 1) * 128, hs]), writes=[Bkr[s]])
                P.dma("sp", lambda e, s=s, fc=fc, hs=hs: e.dma_start(out=ki[s], in_=KT_s[ci, 1, fc * 128:(fc + 1) * 128, hs]), writes=[Bki[s]])
                bR, bI = 2 * s, 2 * s + 1
                for tc in range(16):
                    P.op("pe", lambda e, tc=tc, s=s, bR=bR, hs=hs: e.matmul(psum[:, bR, :], lhsT=tsc[s][:, tc, :], rhs=uT[:, tc, hs], start=(tc == 0), stop=(tc == 15)),
                         reads=[Btc[s], BuT[hh]], writes=[pb[bR]])
                for tc in range(16):
                    P.op("pe", lambda e, tc=tc, s=s, bI=bI, hs=hs: e.matmul(psum[:, bI, :], lhsT=tss[s][:, tc, :], rhs=uT[:, tc, hs], start=(tc == 0), stop=(tc == 15)),
                         reads=[Bts[s], BuT[hh]], writes=[pb[bI]])
                P.op("dve", lambda e, s=s, bR=bR: e.tensor_tensor(out=t1, in0=psum[:, bR, :], in1=kr[s], op=ALU.mult), reads=[pb[bR], Bkr[s]], writes=[Bt[0]])
                P.op("dve", lambda e, s=s, bI=bI: e.tensor_tensor(out=t2, in0=psum[:, bI, :], in1=ki[s], op=ALU.mult), reads=[pb[bI], Bki[s]], writes=[Bt[1]])
                P.op("dve", lambda e, s=s, bR=bR: e.tensor_tensor(out=t3, in0=psum[:, bR, :], in1=ki[s], op=ALU.mult), reads=[pb[bR], Bki[s]], writes=[Bt[2]])
                P.op("dve", lambda e, s=s, bI=bI: e.tensor_tensor(out=t4, in0=psum[:, bI, :], in1=kr[s], op=ALU.mult), reads=[pb[bI], Bkr[s]], writes=[Bt[3]])
                P.op("pool", lambda e, fc=fc: e.tensor_tensor(out=Y[:, 0, fc, :], in0=t1, in1=t2, op=ALU.subtract), reads=[Bt[0], Bt[1]], writes=[BY])
                P.op("pool", lambda e, fc=fc: e.tensor_tensor(out=Y[:, 1, fc, :], in0=t3, in1=t4, op=ALU.add), reads=[Bt[2], Bt[3]], writes=[BY])
            for tb in range(ntb):
                ts_ = slice(tb * 512, (tb + 1) * 512)
                if tb + 1 < ntb:
                    load_rt(rslots[tb + 1], tb + 1)
                r = rslots[tb]
                for c4 in range(4):
                    cc = hh * 4 + c4
                    j = ecnt % 2
                    ecnt += 1
                    bank = 4 + j
                    src_v = (v_s if ci == 0 else z_s)[cc][:, ts_]
                    src_x = (x1_s if ci == 0 else x2_s)[cc][:, ts_]
                    P.dma("sp", lambda e, j=j, src_v=src_v: e.dma_start(out=vt[j], in_=src_v), writes=[Bvt[j]])
                    P.dma("sp", lambda e, j=j, src_x=src_x: e.dma_start(out=xg[j], in_=src_x), writes=[Bxg[j]])
                    idx = 0
                    for part in range(2):
                        for fc in range(16):
                            P.op("pe", lambda e, part=part, fc=fc, c4=c4, r=r, bank=bank, idx=idx: e.matmul(psum[:, bank, :], lhsT=Y[:, part, fc, c4 * 128:(c4 + 1) * 128], rhs=rt[r][:, part * 16 + fc, :], start=(idx == 0), stop=(idx == 31)),
                                 reads=[BY, Brt[r]], writes=[pb[bank]])
                            idx += 1
                    P.op("dve", lambda e, j=j, cc=cc: e.tensor_scalar(out=vt[j], in0=vt[j], scalar1=pvs("hyb", ci * 8 + cc), scalar2=None, op0=ALU.mult), reads=[Bvt[j], Bc], writes=[Bvt[j]])
                    P.op("dve", lambda e, j=j, cc=cc, bank=bank: e.scalar_tensor_tensor(out=zt[j], in0=psum[:, bank, :], scalar=rl1s[:, ci * 8 + cc:ci * 8 + cc + 1], in1=vt[j], op0=ALU.mult, op1=ALU.add),
                         reads=[pb[bank], Brl1, Bvt[j]], writes=[Bzt[j]])
                    if ci == 0:
                        P.op("dve", lambda e, j=j: e.tensor_tensor(out=zt[j], in0=zt[j], in1=xg[j], op=ALU.mult), reads=[Bzt[j], Bxg[j]], writes=[Bzt[j]])
                        if tb < 2:
                            P.dma("sp", lambda e, j=j, cc=cc, ts_=ts_: e.dma_start(out=z_s[cc][:, ts_], in_=zt[j]), reads=[Bzt[j]], writes=[Bzs], key="zt_st%d" % j)
                        P.op("pool", lambda e, j=j: e.tensor_copy(out=zb[j], in_=zt[j]), reads=[Bzt[j]], writes=[Bzb[j]])
                        psT = psum[:, 6 + j, :].bitcast(BF16)[:, 0:512].rearrange("p (a b) -> p a b", b=128)
                        for i in range(4):
                            P.op("pe", lambda e, i=i, j=j, psT=psT: e.transpose(out=psT[:, i, :], in_=zb[j][:, i * 128:(i + 1) * 128], identity=identb[:]),
                                 reads=[Bzb[j], Bc], writes=[pb[6 + j]])
                        P.op("act", lambda e, psT=psT, tb=tb, cc=cc: e.copy(out=uT[:, tb * 4:(tb + 1) * 4, cc * 128:(cc + 1) * 128], in_=psT), reads=[pb[6 + j]], writes=[BuT[hh]])
                    else:
                        P.op("dve", lambda e, j=j: e.tensor_tensor(out=zb[j], in0=zt[j], in1=xg[j], op=ALU.mult), reads=[Bzt[j], Bxg[j]], writes=[Bzb[j]])
                        P.dma("sp", lambda e, j=j, cc=cc, ts_=ts_: e.dma_start(out=yaT_s[cc][:, ts_], in_=zb[j]), reads=[Bzb[j]], writes=[Byas], key="zb_st%d" % j)
        P.barrier()

    def phase_M():
        A = Arena()
        h2T = A.take(16 * 1024, BF16).rearrange("p (a b) -> p a b", b=1024)
        mT = A.take(16 * 1024, BF16).rearrange("p (a b) -> p a b", b=1024)
        wsl = [A.take(16 * 512, BF16).rearrange("p (a b) -> p a b", b=512) for _ in range(2)]
        mark = A.off
        yaT = A.take(8 * 1024, BF16).rearrange("p (a b) -> p a b", b=1024)
        ybT = A.take(8 * 1024, BF16).rearrange("p (a b) -> p a b", b=1024)
        ga = [A.take(512, F32) for _ in range(2)]
        gb = [A.take(512, F32) for _ in range(2)]
        m1 = [A.take(512, F32) for _ in range(2)]
        m2_ = [A.take(512, F32) for _ in range(2)]
        A.off = mark
        oT = A.take(16 * 512, F32).rearrange("p (a b) -> p a b", b=512)
        sq = A.take(16 * 512, BF16).rearrange("p (a b) -> p a b", b=512)
        rstd = A.take(512, F32)
        rstd2 = A.take(512, F32)
        xt = [A.take(512, F32) for _ in range(2)]
        tt = [A.take(512, F32) for _ in range(2)]
        pb = PB()
        Bya, Byb, BmT, Bh2 = Buf("yaT"), Buf("ybT"), Buf("mT"), Buf("h2T")
        Bws = [Buf("wslM0"), Buf("wslM1")]
        Bga = [Buf("ga0"), Buf("ga1")]
        Bgb = [Buf("gb0"), Buf("gb1")]
        Bm1 = [Buf("m10"), Buf("m11")]
        Bm2 = [Buf("m20"), Buf("m21")]
        BoT, Bsq, Brs, Brs2 = Buf("oT"), Buf("sqM"), Buf("rstdM"), Buf("rstdM2")
        Bxt = [Buf("xtM0"), Buf("xtM1")]
        Btt = [Buf("ttM0"), Buf("ttM1")]
        Br1s = Buf("r1T_s")
        P.dma("sp", lambda e: e.dma_start(out=yaT, in_=yaT_s.rearrange("c p t -> p c t")), writes=[Bya])
        P.dma("sp", lambda e: e.dma_start(out=ybT, in_=ybT_s.rearrange("c p t -> p c t")), writes=[Byb])
        wc = 0
        bc = 0
        for nb in range(4):
            s = wc % 2
            wc += 1

            def ldw(e, s=s, nb=nb):
                r = []
                for k in range(8):
                    r.append(e.dma_start(out=wsl[s][:, k, :], in_=hy_proj_d[k * 128:(k + 1) * 128, nb * 512:(nb + 1) * 512]))
                    r.append(e.dma_start(out=wsl[s][:, 8 + k, :], in_=cf_proj_d[k * 128:(k + 1) * 128, nb * 512:(nb + 1) * 512]))
                return r
            P.dma("pool", ldw, writes=[Bws[s]], n=16)
            for q in range(4):
                dch = nb * 4 + q
                for tb in range(2):
                    ts_ = slice(tb * 512, (tb + 1) * 512)
                    j = bc % 2
                    bc += 1
                    bA, bB = 2 * j, 2 * j + 1
                    P.dma("sp", lambda e, j=j, dch=dch, ts_=ts_: e.dma_start(out=ga[j], in_=sg_s[0, dch][:, ts_]), writes=[Bga[j]])
                    P.dma("sp", lambda e, j=j, dch=dch, ts_=ts_: e.dma_start(out=gb[j], in_=sg_s[1, dch][:, ts_]), writes=[Bgb[j]])
                    for k in range(8):
                        P.op("pe", lambda e, k=k, s=s, q=q, ts_=ts_, bA=bA: e.matmul(psum[:, bA, :], lhsT=wsl[s][:, k, q * 128:(q + 1) * 128], rhs=yaT[:, k, ts_], start=(k == 0), stop=(k == 7)),
                             reads=[Bws[s], Bya], writes=[pb[bA]])
                    for k in range(8):
                        P.op("pe", lambda e, k=k, s=s, q=q, ts_=ts_, bB=bB: e.matmul(psum[:, bB, :], lhsT=wsl[s][:, 8 + k, q * 128:(q + 1) * 128], rhs=ybT[:, k, ts_], start=(k == 0), stop=(k == 7)),
                             reads=[Bws[s], Byb], writes=[pb[bB]])
                    P.op("dve", lambda e, j=j, bA=bA: e.tensor_tensor(out=m1[j], in0=psum[:, bA, :], in1=ga[j], op=ALU.mult), reads=[pb[bA], Bga[j]], writes=[Bm1[j]])
                    P.op("dve", lambda e, j=j, bB=bB: e.tensor_tensor(out=m2_[j], in0=psum[:, bB, :], in1=gb[j], op=ALU.mult), reads=[pb[bB], Bgb[j]], writes=[Bm2[j]])
                    P.op("dve", lambda e, j=j, dch=dch, ts_=ts_: e.tensor_tensor(out=mT[:, dch, ts_], in0=m1[j], in1=m2_[j], op=ALU.add), reads=[Bm1[j], Bm2[j]], writes=[BmT])
        P.barrier()
        for tb in range(2):
            ts_ = slice(tb * 512, (tb + 1) * 512)
            for nb in range(4):
                s = wc % 2
                wc += 1
                P.dma("pool", lambda e, s=s, nb=nb: e.dma_start(out=wsl[s], in_=w_out_d[:, nb * 512:(nb + 1) * 512].rearrange("(k p) n -> p k n", p=128)), writes=[Bws[s]])
                for q in range(4):
                    nch = nb * 4 + q
                    bank = 4 + bc % 2
                    bc += 1
                    for k in range(16):
                        P.op("pe", lambda e, k=k, s=s, q=q, ts_=ts_, bank=bank: e.matmul(psum[:, bank, :], lhsT=wsl[s][:, k, q * 128:(q + 1) * 128], rhs=mT[:, k, ts_], start=(k == 0), stop=(k == 15)),
                             reads=[Bws[s], BmT], writes=[pb[bank]])
                    evac(oT[:, nch, :], psum[:, bank, :], [pb[bank]], [BoT])
                P.op("act", lambda e, nb=nb: e.activation(out=sq[:, nb * 4:(nb + 1) * 4, :], in_=oT[:, nb * 4:(nb + 1) * 4, :], func=AF.Square), reads=[BoT], writes=[Bsq])
            rms_stats(sq, 16, 512, 6, pb[6], Bsq, rstd, Brs)
            for nch in range(16):
                j = nch % 2
                P.dma("sp", lambda e, j=j, nch=nch, ts_=ts_: e.dma_start(out=xt[j], in_=xT_s[nch][:, ts_]), writes=[Bxt[j]])
                P.op("dve", lambda e, j=j, nch=nch: e.scalar_tensor_tensor(out=tt[j], in0=oT[:, nch, :], scalar=pvs("g2", nch), in1=rstd, op0=ALU.mult, op1=ALU.mult),
                     reads=[BoT, Brs, Bc], writes=[Btt[j]])
                P.op("dve", lambda e, j=j, nch=nch: e.tensor_tensor(out=oT[:, nch, :], in0=tt[j], in1=xt[j], op=ALU.add), reads=[Btt[j], Bxt[j], BoT], writes=[BoT])
            P.dma("sp", lambda e, ts_=ts_: e.dma_start(out=r1T_s[:, :, ts_].rearrange("c p t -> p c t"), in_=oT), reads=[BoT], writes=[Br1s], key="oT_st")
            for g4 in range(4):
                P.op("act", lambda e, g4=g4: e.activation(out=sq[:, g4 * 4:(g4 + 1) * 4, :], in_=oT[:, g4 * 4:(g4 + 1) * 4, :], func=AF.Square), reads=[BoT], writes=[Bsq])
            rms_stats(sq, 16, 512, 7, pb[7], Bsq, rstd2, Brs2)
            for nch in range(16):
                P.op("dve", lambda e, nch=nch, ts_=ts_: e.scalar_tensor_tensor(out=h2T[:, nch, ts_], in0=oT[:, nch, :], scalar=pvs("g3", nch), in1=rstd2, op0=ALU.mult, op1=ALU.mult),
                     reads=[BoT, Brs2, Bc], writes=[Bh2])
        P.barrier()

    def phase_G():
        A = Arena()
        h2T = A.take(16 * 1024, BF16).rearrange("p (a b) -> p a b", b=1024)
        wsl = [A.take(16 * 512, BF16).rearrange("p (a b) -> p a b", b=512) for _ in range(3)]
        st = [A.take(512, F32) for _ in range(2)]
        ab = [A.take(512, BF16) for _ in range(2)]
        pb = PB()
        Bh2 = Buf("h2T")
        Bws = [Buf("wslG%d" % i) for i in range(3)]
        Bst = [Buf("st0"), Buf("st1")]
        Bab = [Buf("ab0"), Buf("ab1")]
        Bas = Buf("act_s")
        bc = 0
        for blk in range(22):
            s = blk % 3

            def ldw(e, s=s, blk=blk):
                return [e.dma_start(out=wsl[s][:, :, 0:256], in_=w_gu_d[:, blk * 256:(blk + 1) * 256].rearrange("(k p) n -> p k n", p=128)),
                        e.dma_start(out=wsl[s][:, :, 256:512], in_=w_gu_d[:, FF + blk * 256:FF + (blk + 1) * 256].rearrange("(k p) n -> p k n", p=128))]
            P.dma("pool", ldw, writes=[Bws[s]], n=2)
            for i in range(2):
                kch = blk * 2 + i
                for tb in range(2):
                    ts_ = slice(tb * 512, (tb + 1) * 512)
                    j = bc % 2
                    bc += 1
                    bG, bU = 2 * j, 2 * j + 1
                    for k in range(16):
                        P.op("pe", lambda e, k=k, s=s, i=i, ts_=ts_, bG=bG: e.matmul(psum[:, bG, :], lhsT=wsl[s][:, k, i * 128:(i + 1) * 128], rhs=h2T[:, k, ts_], start=(k == 0), stop=(k == 15)),
                             reads=[Bws[s], Bh2], writes=[pb[bG]])
                    for k in range(16):
                        P.op("pe", lambda e, k=k, s=s, i=i, ts_=ts_, bU=bU: e.matmul(psum[:, bU, :], lhsT=wsl[s][:, k, 256 + i * 128:256 + (i + 1) * 128], rhs=h2T[:, k, ts_], start=(k == 0), stop=(k == 15)),
                             reads=[Bws[s], Bh2], writes=[pb[bU]])
                    P.op("act", lambda e, j=j, bG=bG: e.activation(out=st[j], in_=psum[:, bG, :], func=AF.Silu), reads=[pb[bG]], writes=[Bst[j]])
                    P.op("dve", lambda e, j=j, bU=bU: e.tensor_tensor(out=ab[j], in0=psum[:, bU, :], in1=st[j], op=ALU.mult), reads=[pb[bU], Bst[j]], writes=[Bab[j]])
                    P.dma("sp", lambda e, j=j, kch=kch, ts_=ts_: e.dma_start(out=act_s[kch][:, ts_], in_=ab[j]), reads=[Bab[j]], writes=[Bas], key="ab_st%d" % j)
        P.barrier()

    def phase_Dn():
        A = Arena()
        oT = A.take(16 * 1024, F32).rearrange("p (a b) -> p a b", b=1024)
        wd = [A.take(4 * 512, BF16).rearrange("p (a b) -> p a b", b=512) for _ in range(3)]
        ab = [A.take(4 * 1024, BF16).rearrange("p (a b) -> p a b", b=1024) for _ in range(3)]
        sq = A.take(16 * 512, BF16).rearrange("p (a b) -> p a b", b=512)
        rstd = A.take(512, F32)
        xt = [A.take(512, F32) for _ in range(2)]
        tt = [A.take(512, F32) for _ in range(2)]
        fo = A.take(16 * 512, F32).rearrange("p (a b) -> p a b", b=512)
        ot = [A.take(2048, F32) for _ in range(2)]
        pb = PB()
        BoT, Bsq, Brs, Bfo = Buf("oTD"), Buf("sqD"), Buf("rstdD"), Buf("fo")
        Bwd = [Buf("wd%d" % i) for i in range(3)]
        Bab = [Buf("abD%d" % i) for i in range(3)]
        Bxt = [Buf("xtD0"), Buf("xtD1")]
        Btt = [Buf("ttD0"), Buf("ttD1")]
        Bot = [Buf("ot0"), Buf("ot1")]
        Byy = Buf("yout")
        c3 = 0
        for ps_ in range(4):
            for kg in range(11):
                s = c3 % 3
                c3 += 1
                def ldwd(e, s=s, kg=kg, ps_=ps_):
                    return [e.dma_start(out=wd[s][:, k4, :], in_=w_down_d[(kg * 4 + k4) * 128:(kg * 4 + k4 + 1) * 128, ps_ * 512:(ps_ + 1) * 512]) for k4 in range(4)]
                P.dma("pool", ldwd, writes=[Bwd[s]], n=4)
                P.dma("sp", lambda e, s=s, kg=kg: e.dma_start(out=ab[s], in_=act_s[kg * 4:(kg + 1) * 4].rearrange("k p t -> p k t")), writes=[Bab[s]])
                for k4 in range(4):
                    for q in range(4):
                        for tb in range(2):
                            bank = q * 2 + tb
                            P.op("pe", lambda e, s=s, k4=k4, q=q, tb=tb, bank=bank, kg=kg: e.matmul(psum[:, bank, :], lhsT=wd[s][:, k4, q * 128:(q + 1) * 128], rhs=ab[s][:, k4, tb * 512:(tb + 1) * 512], start=(kg == 0 and k4 == 0), stop=(kg == 10 and k4 == 3)),
                                 reads=[Bwd[s], Bab[s]], writes=[pb[bank]])
            for q in range(4):
                for tb in range(2):
                    bank = q * 2 + tb
                    evac(oT[:, ps_ * 4 + q, tb * 512:(tb + 1) * 512], psum[:, bank, :], [pb[bank]], [BoT])
        ocnt = 0
        for tb in range(2):
            ts_ = slice(tb * 512, (tb + 1) * 512)
            for g4 in range(4):
                P.op("act", lambda e, g4=g4, ts_=ts_: e.activation(out=sq[:, g4 * 4:(g4 + 1) * 4, :], in_=oT[:, g4 * 4:(g4 + 1) * 4, ts_], func=AF.Square), reads=[BoT], writes=[Bsq])
            rms_stats(sq, 16, 512, tb, pb[tb], Bsq, rstd, Brs)
            for nch in range(16):
                j = nch % 2
                P.dma("sp", lambda e, j=j, nch=nch, ts_=ts_: e.dma_start(out=xt[j], in_=r1T_s[nch][:, ts_]), writes=[Bxt[j]])
                P.op("dve", lambda e, j=j, nch=nch, ts_=ts_: e.scalar_tensor_tensor(out=tt[j], in0=oT[:, nch, ts_], scalar=pvs("g4", nch), in1=rstd, op0=ALU.mult, op1=ALU.mult),
                     reads=[BoT, Brs, Bc], writes=[Btt[j]])
                P.op("dve", lambda e, j=j, nch=nch: e.tensor_tensor(out=fo[:, nch, :], in0=tt[j], in1=xt[j], op=ALU.add), reads=[Btt[j], Bxt[j]], writes=[Bfo])
            for t4 in range(4):
                o2 = ocnt % 2
                ocnt += 1
                for g4 in range(4):
                    bank = 2 + (t4 * 4 + g4) % 4
                    for q in range(4):
                        nch = g4 * 4 + q
                        P.op("pe", lambda e, bank=bank, q=q, nch=nch, t4=t4: e.transpose(out=psum[:, bank, q * 128:(q + 1) * 128], in_=fo[:, nch, t4 * 128:(t4 + 1) * 128], identity=identf[:]),
                             reads=[Bfo, Bc], writes=[pb[bank]])
                    evac(ot[o2][:, g4 * 512:(g4 + 1) * 512], psum[:, bank, :], [pb[bank]], [Bot[o2]])
                row = tb * 512 + t4 * 128
                P.dma("sp", lambda e, o2=o2, row=row: e.dma_start(out=y_d[row:row + 128, :], in_=ot[o2]), reads=[Bot[o2]], writes=[Byy], key="ot_st%d" % o2)
        P.op("sp", None, reads=[Byy])
        P.barrier()

    if "N1" in stages:
        phase_N1()
    if "P" in stages:
        phase_P()
    if "C" in stages:
        phase_C(0)
        phase_C(1)
    if "M" in stages:
        phase_M()
    if "G" in stages:
        phase_G()
    if "Dn" in stages:
        phase_Dn()

    if "Dn" not in stages and "F" in stages:
        By = Buf("y")
        Bfin = Buf("fin")
        fin = nc.alloc_sbuf_tensor("fin", [128, 16], F32)
        P.op("dve", lambda e: e.tensor_copy(out=fin[:], in_=rl1s[:]), reads=[Brl1], writes=[Bfin])
        P.dma("sp", lambda e: e.dma_start(out=y_d[0:128, 0:16], in_=fin[:]), reads=[Bfin], writes=[By], key="ystore")
        P.op("sp", None, reads=[By])
    if dbg is not None:
        srcs = {"KT": lambda: KT_s[0, 1], "KT2": lambda: KT_s[1, 0],
                "v": lambda: v_s.rearrange("c p t -> (c p) t"), "x1": lambda: x1_s.rearrange("c p t -> (c p) t"),
                "x2": lambda: x2_s.rearrange("c p t -> (c p) t"), "z": lambda: z_s.rearrange("c p t -> (c p) t"),
                "xT": lambda: xT_s.rearrange("c p t -> (c p) t"), "r1": lambda: r1T_s.rearrange("c p t -> (c p) t"),
                "sg": lambda: sg_s[0].rearrange("c p t -> (c p) t")}
        Bdbg = Buf("dbg")
        P.dma("sp", lambda e: e.dma_start(out=dbg_d, in_=srcs[dbg_name]()), writes=[Bdbg], key="dbgst")
        P.op("sp", None, reads=[Bdbg])
    P.emit()
    return nc


def core_inputs(inp, b, rev):
    C = consts()
    d = {}
    xb = inp["x"][b]
    d["x"] = np.ascontiguousarray(xb[::-1] if rev else xb)
    d["pvec"] = make_pvec(inp, rev)
    d["w_in"] = inp["w_in"][0]
    d["hy_proj"] = inp["hy_proj"][0]
    d["cf_proj"] = inp["cf_proj"][0]
    d["w_out"] = inp["w_out"][0]
    d["w_gu"] = inp["ffn_w_gu"][0]
    d["w_down"] = inp["ffn_w_down"][0]
    d["fw1"] = inp["hy_filt_w1"][0]
    fm = np.zeros((64, 68), np.float32)
    fm[:, 0:64] = inp["hy_filt_w2"][0]
    fm[:, 64] = inp["hy_filt_b1"][0]
    fm[:, 65] = inp["hy_filt_fr1"][0]
    fm[:, 66] = inp["hy_filt_b2"][0]
    fm[:, 67] = inp["hy_filt_fr2"][0]
    d["fmlp"] = fm
    w3 = inp["hy_filt_w3"][0]
    if rev:
        w3 = w3.reshape(64, 2, 2, 1024)[:, :, ::-1].reshape(64, 4096)
    d["fw3"] = np.ascontiguousarray(w3)
    for k in ["zT", "tabF_c", "tabF_s", "tabD_c", "tabD_s", "tabR", "ident_bf", "ident_f", "ones_bf", "tau_row", "adel_row"]:
        d[k] = C[k]
    return d


_NC = {}


def kernel(**inputs):
    inp = {k: np.asarray(v, dtype=np.float32) for k, v in inputs.items()}
    if "nc" not in _NC:
        _NC["nc"] = build()
    nc = _NC["nc"]
    in_maps = []
    for c in range(8):
        b, j = c // 2, c % 2
        in_maps.append(core_inputs(inp, b, j == 1))
    res = run_bass_kernel_spmd(nc, in_maps, core_ids=list(range(8)))
    out = np.zeros((4, 2048, 2048), np.float32)
    for c in range(8):
        b, j = c // 2, c % 2
        y = np.asarray(res.results[c]["y"], dtype=np.float32)
        if j == 0:
            out[b, 0:1024] = y
        else:
            out[b, 1024:2048] = y[::-1]
    return out
```
